# Optimizing a Trainium2 kernel written in Bass

```python
import math
import jax
import jax.numpy as jnp
from jax import lax
import numpy as np

D_MODEL = 1024
BATCH = 8
SEQ = 8192
DEPTH = 1
DEC_BATCH = 16
DEC_SEQ = 16
PAST_LEN = 2048

CHUNK = 64
Q_BLOCK = 128
N_HEADS_A = 4
HEAD_DIM_A = 64
WIDTH_A = N_HEADS_A * 2 * HEAD_DIM_A
N_HEADS_B = 8
HEAD_DIM_B = 64
WIDTH_B = N_HEADS_B * HEAD_DIM_B
IN_WIDTH = 4 * WIDTH_A + 4 * WIDTH_B + 2 * D_MODEL
ROPE_THETA = 10000.0
NORM_EPS = 1e-6

kernel_name = 'streaming_diff_stickbreak_hybrid'


def rms_norm(x, g):
    xf = x.astype(jnp.float32)
    y = xf * lax.rsqrt(jnp.mean(xf * xf, axis=-1, keepdims=True) + NORM_EPS)
    return (y * g.astype(jnp.float32)).astype(x.dtype)


def rope(x, pos):
    half = x.shape[-1] // 2
    inv = ROPE_THETA ** (-jnp.arange(half, dtype=jnp.float32) / half)
    ang = pos.astype(jnp.float32)[:, None] * inv[None, :]
    cos = jnp.cos(ang)[None, :, None, None, :]
    sin = jnp.sin(ang)[None, :, None, None, :]
    xf = x.astype(jnp.float32)
    x1, x2 = xf[..., :half], xf[..., half:]
    return jnp.concatenate([x1 * cos - x2 * sin, x2 * cos + x1 * sin], axis=-1).astype(x.dtype)


def adaln_modulate(x, c, norm_g, w_ada, b_ada):
    mod = c @ w_ada + b_ada
    shift, scale, gate = jnp.split(mod, 3, axis=-1)
    h = rms_norm(x, norm_g) * (1.0 + scale[:, None, :]) + shift[:, None, :]
    return h, gate[:, None, :]


def project_inputs(h, w_in, pos):
    b, s, _ = h.shape
    sizes = [WIDTH_A] * 4 + [WIDTH_B] * 4 + [D_MODEL] * 2
    idx = [int(i) for i in np.cumsum(sizes)[:-1]]
    qa, ka, va, za, qb, kb, vb, zb, ga, gb = jnp.split(h @ w_in, idx, axis=-1)
    qa = rope(qa.reshape(b, s, N_HEADS_A, 2, HEAD_DIM_A), pos)
    ka = rope(ka.reshape(b, s, N_HEADS_A, 2, HEAD_DIM_A), pos)
    va = va.reshape(b, s, N_HEADS_A, 2 * HEAD_DIM_A)
    qb = qb.reshape(b, s, N_HEADS_B, HEAD_DIM_B)
    kb = kb.reshape(b, s, N_HEADS_B, HEAD_DIM_B)
    vb = vb.reshape(b, s, N_HEADS_B, HEAD_DIM_B)
    return qa, ka, va, za, qb, kb, vb, zb, ga, gb


def diff_lambda(lq1, lk1, lq2, lk2, lam_init):
    f32 = jnp.float32
    return (jnp.exp(jnp.sum(lq1.astype(f32) * lk1.astype(f32)))
            - jnp.exp(jnp.sum(lq2.astype(f32) * lk2.astype(f32))) + lam_init)


def diff_attention(q, k, v, qpos, kpos, lam):
    s = jnp.einsum('bqhcd,bkhcd->bhcqk', q, k,
                   preferred_element_type=jnp.float32) * (HEAD_DIM_A ** -0.5)
    mask = (kpos // CHUNK)[None, :] <= (qpos // CHUNK)[:, None]
    p = jax.nn.softmax(jnp.where(mask, s, -jnp.inf), axis=-1)
    w = p[:, :, 0] - lam * p[:, :, 1]
    o = jnp.einsum('bhqk,bkhe->bqhe', w, v, preferred_element_type=jnp.float32)
    return o.astype(v.dtype)


def stick_breaking_attention(q, k, v, qpos, kpos):
    z = jnp.einsum('bqhd,bkhd->bhqk', q, k,
                   preferred_element_type=jnp.float32) * (HEAD_DIM_B ** -0.5)
    mask = kpos[None, :] < qpos[:, None]
    log_1m = jnp.where(mask, jax.nn.log_sigmoid(-z), 0.0)
    log_a = jax.nn.log_sigmoid(z) + lax.cumsum(log_1m, axis=3, reverse=True) - log_1m
    a = jnp.where(mask, jnp.exp(log_a), 0.0)
    o = jnp.einsum('bhqk,bkhd->bqhd', a, v, preferred_element_type=jnp.float32)
    return o.astype(v.dtype)


def merge_branches(oa, ob, za, zb, ga, gb, subln_g, lam_init, w_branch_a, w_branch_b, w_out):
    b, s = oa.shape[:2]
    oa = rms_norm(oa, subln_g) * (1.0 - lam_init)
    ya = (oa.reshape(b, s, WIDTH_A) * jax.nn.silu(za)) @ w_branch_a
    yb = (ob.reshape(b, s, WIDTH_B) * jax.nn.silu(zb)) @ w_branch_b
    return (jax.nn.sigmoid(ga) * ya + jax.nn.sigmoid(gb) * yb) @ w_out


def setup_inputs(seed: int = 0) -> dict:
    key = jax.random.key(seed)
    ks = jax.random.split(key, 24)
    f32 = jnp.float32
    nrm = lambda k, shape, s=1.0: (jax.random.normal(k, shape, f32) * s)
    return {
        'x_prompt': nrm(ks[0], (BATCH, SEQ, D_MODEL)),
        'x_sample': nrm(ks[1], (DEC_BATCH, DEC_SEQ, D_MODEL)),
        'c_prompt': nrm(ks[2], (BATCH, D_MODEL)),
        'c_sample': nrm(ks[3], (DEC_BATCH, D_MODEL)),
        'cache_a_k': nrm(ks[4], (DEPTH, DEC_BATCH, PAST_LEN, N_HEADS_A, 2, HEAD_DIM_A)),
        'cache_a_v': nrm(ks[5], (DEPTH, DEC_BATCH, PAST_LEN, N_HEADS_A, 2 * HEAD_DIM_A)),
        'cache_b_k': nrm(ks[6], (DEPTH, DEC_BATCH, PAST_LEN, N_HEADS_B, HEAD_DIM_B)),
        'cache_b_v': nrm(ks[7], (DEPTH, DEC_BATCH, PAST_LEN, N_HEADS_B, HEAD_DIM_B)),
        'norm_g': 1.0 + nrm(ks[8], (DEPTH, D_MODEL), 0.1),
        'w_ada': nrm(ks[9], (DEPTH, D_MODEL, 3 * D_MODEL), 0.5 * D_MODEL ** -0.5),
        'b_ada': nrm(ks[10], (DEPTH, 3 * D_MODEL), 0.02),
        'w_in': nrm(ks[11], (DEPTH, D_MODEL, IN_WIDTH), D_MODEL ** -0.5),
        'lambda_q1': nrm(ks[12], (DEPTH, HEAD_DIM_A), 0.1),
        'lambda_k1': nrm(ks[13], (DEPTH, HEAD_DIM_A), 0.1),
        'lambda_q2': nrm(ks[14], (DEPTH, HEAD_DIM_A), 0.1),
        'lambda_k2': nrm(ks[15], (DEPTH, HEAD_DIM_A), 0.1),
        'subln_g': 1.0 + nrm(ks[16], (DEPTH, 2 * HEAD_DIM_A), 0.1),
        'w_branch_a': nrm(ks[17], (DEPTH, WIDTH_A, D_MODEL), WIDTH_A ** -0.5),
        'w_branch_b': nrm(ks[18], (DEPTH, WIDTH_B, D_MODEL), WIDTH_B ** -0.5),
        'w_out': nrm(ks[19], (DEPTH, D_MODEL, D_MODEL), D_MODEL ** -0.5),
        'final_g': 1.0 + nrm(ks[20], (D_MODEL,), 0.1),
    }


def reference(x_prompt, x_sample, c_prompt, c_sample, cache_a_k, cache_a_v, cache_b_k, cache_b_v,
              norm_g, w_ada, b_ada, w_in, lambda_q1, lambda_k1, lambda_q2, lambda_k2, subln_g,
              w_branch_a, w_branch_b, w_out, final_g):
    bp, sp, _ = x_prompt.shape
    ns = x_sample.shape[1]
    past = cache_a_k.shape[2]
    pos_p = jnp.arange(sp, dtype=jnp.int32)
    pos_s = past + jnp.arange(ns, dtype=jnp.int32)
    kpos_s = jnp.arange(past + ns, dtype=jnp.int32)
    n_blocks = sp // Q_BLOCK
    xp, xs = x_prompt, x_sample
    pak, pav, pbk, pbv = [], [], [], []
    sak, sav, sbk, sbv = [], [], [], []
    for l in range(DEPTH):
        lam_init = 0.8 - 0.6 * math.exp(-0.3 * l)
        lam = diff_lambda(lambda_q1[l], lambda_k1[l], lambda_q2[l], lambda_k2[l], lam_init)

        h, gate = adaln_modulate(xp, c_prompt, norm_g[l], w_ada[l], b_ada[l])
        qa, ka, va, za, qb, kb, vb, zb, ga, gb = project_inputs(h, w_in[l], pos_p)

        def block(i):
            start = i * Q_BLOCK
            qpos = start + jnp.arange(Q_BLOCK, dtype=jnp.int32)
            oa_blk = diff_attention(lax.dynamic_slice_in_dim(qa, start, Q_BLOCK, axis=1),
                                    ka, va, qpos, pos_p, lam)
            ob_blk = stick_breaking_attention(lax.dynamic_slice_in_dim(qb, start, Q_BLOCK, axis=1),
                                              kb, vb, qpos, pos_p)
            return oa_blk, ob_blk

        oa, ob = lax.map(block, jnp.arange(n_blocks, dtype=jnp.int32))
        oa = jnp.moveaxis(oa, 0, 1).reshape(bp, sp, N_HEADS_A, 2 * HEAD_DIM_A)
        ob = jnp.moveaxis(ob, 0, 1).reshape(bp, sp, N_HEADS_B, HEAD_DIM_B)
        xp = xp + gate * merge_branches(oa, ob, za, zb, ga, gb, subln_g[l], lam_init,
                                        w_branch_a[l], w_branch_b[l], w_out[l])
        pak.append(ka)
        pav.append(va)
        pbk.append(kb)
        pbv.append(vb)

        h, gate = adaln_modulate(xs, c_sample, norm_g[l], w_ada[l], b_ada[l])
        qa, ka, va, za, qb, kb, vb, zb, ga, gb = project_inputs(h, w_in[l], pos_s)
        oa = diff_attention(qa, jnp.concatenate([cache_a_k[l], ka], axis=1),
                            jnp.concatenate([cache_a_v[l], va], axis=1), pos_s, kpos_s, lam)
        ob = stick_breaking_attention(qb, jnp.concatenate([cache_b_k[l], kb], axis=1),
                                      jnp.concatenate([cache_b_v[l], vb], axis=1), pos_s, kpos_s)
        xs = xs + gate * merge_branches(oa, ob, za, zb, ga, gb, subln_g[l], lam_init,
                                        w_branch_a[l], w_branch_b[l], w_out[l])
        sak.append(ka)
        sav.append(va)
        sbk.append(kb)
        sbv.append(vb)

    y_prompt = rms_norm(xp, final_g)
    y_sample = rms_norm(xs, final_g)
    return (y_prompt, y_sample,
            jnp.stack(pak), jnp.stack(pav), jnp.stack(pbk), jnp.stack(pbv),
            jnp.stack(sak), jnp.stack(sav), jnp.stack(sbk), jnp.stack(sbv))
```

```python
import contextlib
import numpy as np
import ml_dtypes
import concourse.bass as bass
import concourse.mybir as mybir
from concourse.bass_utils import run_bass_kernel_spmd

F32 = mybir.dt.float32
BF16 = mybir.dt.bfloat16
AF = mybir.ActivationFunctionType
ALU = mybir.AluOpType

ENGS = ("pe", "act", "dve", "pool", "sp")
ND_SEMS = 24


class B:
    __slots__ = ("name", "last_w", "readers")

    def __init__(self, name=""):
        self.name = name
        self.last_w = None
        self.readers = []


class Prog:
    def __init__(self, nc, sems, state):
        self.nc = nc
        self.sems = sems
        self.st = state
        self.ops = {e: [] for e in ENGS}
        self.touched = set()

    def add(self, eng, fn, reads=(), writes=(), dma=False):
        ops = self.ops[eng]
        idx = len(ops)
        deps = {}
        self.touched.update(reads)
        self.touched.update(writes)
        for b in reads:
            if b.last_w is not None:
                deps[b.last_w] = deps.get(b.last_w, 0) | 1
        for b in writes:
            if b.last_w is not None:
                deps[b.last_w] = deps.get(b.last_w, 0) | 2
            for r in b.readers:
                deps[r] = deps.get(r, 0) | 4
        if dma:
            did = self.st["dma_id"]
            self.st["dma_id"] += 1
            ev = ("dma", did)
        else:
            did = None
            ev = (eng, idx)
        deps.pop(ev, None)
        for b in reads:
            b.readers.append(ev)
        for b in writes:
            b.last_w = ev
            b.readers = []
        ops.append({"fn": fn, "deps": deps, "dma": did, "sig": False})
        return ev

    def finish(self):
        nc = self.nc
        st = self.st
        ops = self.ops
        for e in ENGS:
            for op in ops[e]:
                for (pe_, pidx), kind in op["deps"].items():
                    if pe_ == "dma":
                        continue
                    if pe_ == e and e == "pe":
                        continue
                    ops[pe_][pidx]["sig"] = True
        for e in ENGS:
            if e != "sp" and ops[e]:
                ops[e][-1]["sig"] = True
        cnt = {}
        for e in ENGS:
            c = st["sig"][e]
            for i, op in enumerate(ops[e]):
                if op["sig"] and op["dma"] is None:
                    c += 1
                cnt[(e, i)] = c
            st["sig_end"] = st.get("sig_end", {})
            st["sig_end"][e] = c
        final_cnt = dict(st["sig_end"])
        dma_first = st["dma_id"] - sum(1 for e in ENGS for op in ops[e] if op["dma"] is not None)
        dma_last = st["dma_id"]

        def dma_target(did):
            return did % ND_SEMS, 16 * (did // ND_SEMS + 1)

        sems = self.sems
        waited = st["waited"]

        def emit_engine(e, engobj):
            w = waited[e]

            def wait(key, semh, val):
                if w.get(key, 0) >= val:
                    return
                engobj.wait_ge(semh, val)
                w[key] = val

            for i, op in enumerate(ops[e]):
                need = {}
                for (pe_, pidx), kind in op["deps"].items():
                    if pe_ == "dma":
                        si, val = dma_target(pidx)
                        key = ("dma", si)
                        need[key] = max(need.get(key, 0), val)
                    else:
                        if pe_ == e and e == "pe":
                            continue
                        need[pe_] = max(need.get(pe_, 0), cnt[(pe_, pidx)])
                if op["dma"] is not None and op["dma"] >= ND_SEMS:
                    si, val = dma_target(op["dma"] - ND_SEMS)
                    key = ("dma", si)
                    need[key] = max(need.get(key, 0), val)
                for key, val in need.items():
                    if isinstance(key, tuple):
                        wait(key, sems["dma"][key[1]], val)
                    else:
                        wait(key, sems[key], val)
                ins = op["fn"](engobj)
                if op["dma"] is not None:
                    si, _ = dma_target(op["dma"])
                    ins.then_inc(sems["dma"][si], 16)
                elif op["sig"]:
                    ins.then_inc(sems[e], 1)
            for pe_ in ENGS:
                if pe_ == "sp" or pe_ == e:
                    continue
                if final_cnt[pe_] > 0:
                    wait(pe_, sems[pe_], final_cnt[pe_])
            for did in range(max(dma_first, dma_last - ND_SEMS), dma_last):
                si, val = dma_target(did)
                wait(("dma", si), sems["dma"][si], val)

        with nc.Block() as block:
            @block.tensor
            def _(eng):
                emit_engine("pe", eng)

            @block.scalar
            def _(eng):
                emit_engine("act", eng)

            @block.vector
            def _(eng):
                emit_engine("dve", eng)

            @block.gpsimd
            def _(eng):
                emit_engine("pool", eng)

            @block.sync
            def _(eng):
                emit_engine("sp", eng)

        for e in ENGS:
            st["sig"][e] = final_cnt[e]
        for b in self.touched:
            b.last_w = None
            b.readers = []


def new_state():
    return {"dma_id": 0, "sig": {e: 0 for e in ENGS}, "waited": {e: {} for e in ENGS}}


D = 1024
EPS = 1e-6
LAM_INIT = 0.2
PAST = 2048
NS = 16


class KB:
    def __init__(self, NT, with_sample=True):
        self.NT = NT
        self.S = NT * 128
        self.with_sample = with_sample
        self.nc = bass.Bass("TRN2", target_bir_lowering=False)
        self.st = new_state()
        self.uid = 0

    def din(self, name, shape, dt=F32):
        return self.nc.dram_tensor(name, list(shape), dt, kind="ExternalInput").ap()

    def dout(self, name, shape, dt=F32):
        return self.nc.dram_tensor(name, list(shape), dt, kind="ExternalOutput").ap()

    def dscr(self, name, shape, dt):
        return self.nc.dram_tensor(name, list(shape), dt).ap()

    def sb(self, es, name, shape, dt):
        self.uid += 1
        return es.enter_context(self.nc.sbuf_tensor("%s_%d" % (name, self.uid), list(shape), dt))

    def ps(self, es, name, shape, dt):
        self.uid += 1
        return es.enter_context(self.nc.psum_tensor("%s_%d" % (name, self.uid), list(shape), dt))

    def mm(self, out, lhsT, rhs, start, stop, r, w, skip=False):
        self.pg.add("pe", lambda e: e.matmul(out, lhsT=lhsT, rhs=rhs, start=start, stop=stop,
                                             skip_group_check=skip), r, w)

    def tr(self, out, in_, P, r, w):
        ident = self.ident
        self.pg.add("pe", lambda e: e.transpose(out=out, in_=in_, identity=ident[0:P, 0:P]), r, w)

    def tr32(self, out, in_, P, r, w):
        ident = self.ident32
        self.pg.add("pe", lambda e: e.transpose(out=out, in_=in_, identity=ident[0:P, 0:P]), r, w)

    def act(self, out, in_, func, r, w, scale=1.0, bias=0.0, accum=None):
        if accum is None:
            self.pg.add("act", lambda e: e.activation(out=out, in_=in_, func=func, bias=bias, scale=scale), r, w)
        else:
            self.pg.add("act", lambda e: e.activation(out=out, in_=in_, func=func, bias=bias, scale=scale,
                                                      accum_out=accum), r, w)

    def tt(self, eng, out, a, b, op, r, w):
        self.pg.add(eng, lambda e: e.tensor_tensor(out=out, in0=a, in1=b, op=op), r, w)

    def ts(self, eng, out, a, s1, s2, op0, op1, r, w):
        if s2 is None:
            self.pg.add(eng, lambda e: e.tensor_scalar(out=out, in0=a, scalar1=s1, scalar2=None, op0=op0), r, w)
        else:
            self.pg.add(eng, lambda e: e.tensor_scalar(out=out, in0=a, scalar1=s1, scalar2=s2, op0=op0, op1=op1), r, w)

    def stt(self, eng, out, a, s, b, op0, op1, r, w, accum=None):
        if accum is None:
            self.pg.add(eng, lambda e: e.scalar_tensor_tensor(out=out, in0=a, scalar=s, in1=b, op0=op0, op1=op1), r, w)
        else:
            self.pg.add(eng, lambda e: e.scalar_tensor_tensor(out=out, in0=a, scalar=s, in1=b, op0=op0, op1=op1,
                                                              accum_out=accum), r, w)

    def cp(self, eng, out, in_, r, w):
        if eng == "act":
            self.pg.add("act", lambda e: e.activation(out=out, in_=in_, func=AF.Copy), r, w)
        else:
            self.pg.add(eng, lambda e: e.tensor_copy(out=out, in_=in_), r, w)

    def rcp(self, out, in_, r, w):
        self.pg.add("dve", lambda e: e.reciprocal(out=out, in_=in_), r, w)

    def mset(self, eng, ap, val, w):
        self.pg.add(eng, lambda e: e.memset(ap, val), (), w)

    def dma(self, out, in_, r, w, eng="sp"):
        self.pg.add(eng, lambda e: e.dma_start(out=out, in_=in_), r, w, dma=True)

    def new_prog(self):
        self.pg = Prog(self.nc, self.sems, self.st)

    def rstd(self, ss, tmpa, tmpb, out, inv_n, bss, btmp, bout):
        self.ts("dve", tmpa, ss, inv_n, EPS, ALU.mult, ALU.add, [bss], [btmp])
        self.act(tmpb, tmpa, AF.Ln, [btmp], [btmp])
        self.act(out, tmpb, AF.Exp, [btmp], [bout], scale=-0.5)


class Tl:
    __slots__ = ("t", "b")

    def __init__(self, t):
        self.t = t
        self.b = B()


def _build(self):
    nc = self.nc
    S, NT = self.S, self.NT
    NKB = NT
    NQ = NT // 4
    din, dout, dscr = self.din, self.dout, self.dscr
    xp = din("xp", [S, D]); xs = din("xs", [32, D])
    cmat_p = din("cmat_p", [128, 8, 128]); cmat_s = din("cmat_s", [128, 8, 32])
    w_ada = din("w_ada", [D, 3 * D]); b_ada = din("b_ada", [3 * D]); norm_g = din("norm_g", [D])
    w_in = din("w_in", [D, 6144]); lams = din("lams", [256]); subln_g = din("subln_g", [128])
    wba_d = din("wba", [512, D]); wbb_d = din("wbb", [512, D]); wo_d = din("w_out", [D, D]); final_g = din("final_g", [D])
    cak = din("cak", [2, PAST, 512]); cav = din("cav", [2, PAST, 512])
    cbk = din("cbk", [2, PAST, 512]); cbv = din("cbv", [2, PAST, 512])
    cos_p = din("cos_p", [S, 256]); sin_p = din("sin_p", [S, 256])
    cos_s = din("cos_s", [32, 256]); sin_s = din("sin_s", [32, 256])
    ident_d = din("ident", [128, 128], BF16); triP_d = din("triP", [128, 128], BF16)
    id32_d = din("ident32", [128, 128]); mask_d = din("mask_lt", [128, 128])
    y_p = dout("y_p", [S, D]); y_s = dout("y_s", [32, D])
    pak = dout("pak", [S, 512]); pav = dout("pav", [S, 512]); pbk = dout("pbk", [S, 512]); pbv = dout("pbv", [S, 512])
    sak = dout("sak", [32, 512]); sav = dout("sav", [32, 512]); sbk = dout("sbk", [32, 512]); sbv = dout("sbv", [32, 512])
    qkT_d = dscr("qkT_d", [16, 128, S], BF16); v_d = dscr("v_d", [S, 1024], BF16)
    oa_d = dscr("oa_d", [S, 512], F32); obT_d = dscr("obT_d", [4, 128, S], F32)
    qkT_sd = dscr("qkT_sd", [16, 128, 32], BF16); v_sd = dscr("v_sd", [32, 1024], BF16)
    oa_sd = dscr("oa_sd", [32, 512], F32); obT_sd = dscr("obT_sd", [4, 128, 32], F32)

    mm, tr, act, tt, ts, stt, cp, rcp, mset, dma = (self.mm, self.tr, self.act, self.tt, self.ts, self.stt,
                                                    self.cp, self.rcp, self.mset, self.dma)

    with contextlib.ExitStack() as top:
        self.sems = {e: top.enter_context(nc.semaphore("s_" + e)) for e in ENGS if e != "sp"}
        self.sems["dma"] = [top.enter_context(nc.semaphore("s_dma%d" % i)) for i in range(ND_SEMS)]

        def T(es, name, shape, dt):
            return Tl(self.sb(es, name, shape, dt))

        def PT(es, name, shape, dt):
            return Tl(self.ps(es, name, shape, dt))

        Abc = [T(top, "Abc%d" % i, [128, D], F32) for i in range(2)]
        shbc = [T(top, "shbc%d" % i, [128, D], F32) for i in range(2)]
        gtbc = [T(top, "gtbc%d" % i, [128, D], F32) for i in range(2)]
        fgbc = T(top, "fgbc", [128, D], F32)
        gsub = T(top, "gsub", [128, 128], F32)
        nlam = T(top, "nlam", [128, 1], F32)
        identT = T(top, "ident", [128, 128], BF16); self.ident = identT.t
        id32T = T(top, "ident32", [128, 128], F32); self.ident32 = id32T.t
        triP = T(top, "triP", [128, 128], BF16); onesT = T(top, "onesT", [128, 128], mybir.dt.float32r); ones32 = T(top, "ones32", [128, 128], F32)
        zf = T(top, "zf", [128, 2, 512], F32); self.zf = zf
        maskT = T(top, "mask", [128, 128], F32)
        zer = T(top, "zer", [128, 512], BF16)
        bconst = B()

        with contextlib.ExitStack() as es01:
            wq = T(es01, "wq", [128, 8, 3072], BF16)
            with contextlib.ExitStack() as es0:
                self.new_prog()
                wst = [T(es0, "wst%d" % i, [128, 4, 512], F32) for i in range(2)]
                bada = T(es0, "bada", [128, 3 * D], F32)
                ngbc = T(es0, "ngbc", [128, D], F32)
                modt = [T(es0, "mod%d" % i, [128, 3 * D], F32) for i in range(2)]
                cm = [T(es0, "cm0", [128, 8, 128], F32), T(es0, "cm1", [128, 8, 32], F32)]
                lamv = T(es0, "lamv", [128, 256], F32)
                sm0 = T(es0, "sm0", [128, 8], F32)
                j64 = T(es0, "j64", [128, 64], F32)
                graw = T(es0, "graw", [128, 128], F32)
                MOD = [PT(es0, "MOD%d" % i, [128, 512], F32) for i in range(2)]

                dma(identT.t[:], ident_d, [], [bconst]); dma(triP.t[:], triP_d, [], [bconst]); dma(id32T.t[:], id32_d, [], [bconst])
                mset("pool", ones32.t[:], 1.0, [ones32.b]); cp("dve", onesT.t[:], ones32.t[:], [ones32.b], [bconst])
                mset("pool", zf.t[:], 0.0, [bconst]); dma(maskT.t[:], mask_d, [], [bconst])
                mset("pool", zer.t[:], 0.0, [zer.b])
                dma(bada.t[:], b_ada.partition_broadcast(128), [], [bada.b])
                dma(ngbc.t[:], norm_g.partition_broadcast(128), [], [ngbc.b])
                dma(fgbc.t[:], final_g.partition_broadcast(128), [], [fgbc.b])
                dma(graw.t[:], subln_g.partition_broadcast(128), [], [graw.b])
                dma(lamv.t[:], lams.partition_broadcast(128), [], [lamv.b])
                dma(cm[0].t[:], cmat_p, [], [cm[0].b]); dma(cm[1].t[:], cmat_s, [], [cm[1].b])
                stt("dve", j64.t[:], lamv.t[:, 0:64], 1.0, lamv.t[:, 64:128], ALU.mult, ALU.mult, [lamv.b], [j64.b, sm0.b], accum=sm0.t[:, 0:1])
                stt("dve", j64.t[:], lamv.t[:, 128:192], 1.0, lamv.t[:, 192:256], ALU.mult, ALU.mult, [lamv.b, sm0.b], [j64.b, sm0.b], accum=sm0.t[:, 1:2])
                act(sm0.t[:, 2:4], sm0.t[:, 0:2], AF.Exp, [sm0.b], [sm0.b])
                tt("dve", sm0.t[:, 4:5], sm0.t[:, 2:3], sm0.t[:, 3:4], ALU.subtract, [sm0.b], [sm0.b])
                ts("dve", nlam.t[:], sm0.t[:, 4:5], LAM_INIT, -1.0, ALU.add, ALU.mult, [sm0.b], [nlam.b])
                ts("dve", gsub.t[:], graw.t[:], 1.0 - LAM_INIT, None, ALU.mult, None, [graw.b], [gsub.b])
                wa_v = w_ada.rearrange("(j p) n -> p j n", p=128)
                k = 0
                for g in range(6):
                    for half in range(2):
                        w_ = wst[k % 2]; k += 1
                        dma(w_.t[:], wa_v[:, 4 * half:4 * half + 4, g * 512:(g + 1) * 512], [], [w_.b])
                        for jj in range(4):
                            j = 4 * half + jj
                            mm(MOD[0].t[:, :], cm[0].t[:, j, :], w_.t[:, jj, :], j == 0, j == 7, [cm[0].b, w_.b], [MOD[0].b])
                            mm(MOD[1].t[0:32, :], cm[1].t[:, j, :], w_.t[:, jj, :], j == 0, j == 7, [cm[1].b, w_.b], [MOD[1].b])
                    cs = slice(g * 512, (g + 1) * 512)
                    tt("dve", modt[0].t[:, cs], MOD[0].t[:, :], bada.t[:, cs], ALU.add, [MOD[0].b, bada.b], [modt[0].b])
                    tt("dve", modt[1].t[0:32, cs], MOD[1].t[0:32, :], bada.t[0:32, cs], ALU.add, [MOD[1].b, bada.b], [modt[1].b])
                for i, P in ((0, 128), (1, 32)):
                    stt("dve", Abc[i].t[0:P, :], modt[i].t[0:P, D:2 * D], 1.0, ngbc.t[0:P, :], ALU.add, ALU.mult, [modt[i].b, ngbc.b], [Abc[i].b])
                    cp("pool", shbc[i].t[0:P, :], modt[i].t[0:P, 0:D], [modt[i].b], [shbc[i].b])
                    cp("pool", gtbc[i].t[0:P, :], modt[i].t[0:P, 2 * D:3 * D], [modt[i].b], [gtbc[i].b])
                wi_v = w_in.rearrange("(j p) n -> p j n", p=128)
                qkv_cols = [0, 512, 1024, 2048, 2560, 3072]
                for g in range(6):
                    for half in range(2):
                        w_ = wst[k % 2]; k += 1
                        dma(w_.t[:], wi_v[:, 4 * half:4 * half + 4, qkv_cols[g]:qkv_cols[g] + 512], [], [w_.b])
                        cp("dve" if k % 2 else "pool", wq.t[:, 4 * half:4 * half + 4, g * 512:(g + 1) * 512], w_.t[:], [w_.b], [wq.b])
                self.pg.finish()

            with contextlib.ExitStack() as es1:
                self.new_prog()
                L = self.alloc_norm(es1, T, PT)
                stage = [T(es1, "stage%d" % i, [128, 2048], F32) for i in range(2)]
                qkb = [T(es1, "qkb%d" % i, [128, 2048], BF16) for i in range(2)]
                vbf = [T(es1, "vbf%d" % i, [128, 1024], BF16) for i in range(2)]
                qkTs = [T(es1, "qkTs%d" % i, [128, 16, 128], BF16) for i in range(2)]
                cst = [T(es1, "cst%d" % i, [128, 256], F32) for i in range(2)]
                snt = [T(es1, "snt%d" % i, [128, 256], F32) for i in range(2)]
                rp = [T(es1, "rp%d" % i, [128, 512], F32) for i in range(2)]
                rt = [T(es1, "rt%d" % i, [128, 256], F32) for i in range(4)]
                PG = [PT(es1, "PG%d" % i, [128, 512], F32) for i in range(4)]
                TQ = PT(es1, "TQ", [128, 16, 128], BF16)

                tiles = []
                for t in range(NT):
                    r0 = t * 128
                    tiles.append(dict(P=128, x=xp[r0:r0 + 128, :], mod=0, cos=cos_p[r0:r0 + 128, :], sin=sin_p[r0:r0 + 128, :],
                                      outs=[pak[r0:r0 + 128, :], pav[r0:r0 + 128, :], pbk[r0:r0 + 128, :], pbv[r0:r0 + 128, :]],
                                      qkT=qkT_d[:, :, r0:r0 + 128], v=v_d[r0:r0 + 128, :]))
                if self.with_sample:
                    tiles.append(dict(P=32, x=xs, mod=1, cos=cos_s, sin=sin_s, outs=[sak, sav, sbk, sbv],
                                      qkT=qkT_sd, v=v_sd))
                n = len(tiles)
                pgi = [0]

                def load(i):
                    tl = tiles[i]; P = tl["P"]; s = i % 2
                    dma(L["xt"][s].t[0:P, :], tl["x"], [], [L["xt"][s].b])
                    dma(cst[s].t[0:P, :], tl["cos"], [], [cst[s].b])
                    dma(snt[s].t[0:P, :], tl["sin"], [], [snt[s].b])

                def pe_main(i):
                    tl = tiles[i]; P = tl["P"]; s = i % 2
                    self.pe_hT(L, s, P)
                    hT = L["hT"][s]
                    grp = []
                    for g in range(6):
                        pgt = PG[pgi[0] % 4]; pgi[0] += 1
                        for j in range(8):
                            mm(pgt.t[0:P, :], hT.t[:, j, 0:P], wq.t[:, j, g * 512:(g + 1) * 512], j == 0, j == 7, [hT.b, wq.b], [pgt.b])
                        grp.append(pgt)
                        self.p1_evac(g, pgt, P, s, stage[s], qkb[s], vbf[s], rp, rt, cst[s], snt[s])
                    return grp

                def tq(i):
                    tl = tiles[i]; P = tl["P"]; s = i % 2
                    for u in range(16):
                        tr(TQ.t[:, u, 0:P], qkb[s].t[0:P, u * 128:(u + 1) * 128], P, [qkb[s].b, bconst], [TQ.b])
                    cp("dve", qkTs[s].t[:, :, 0:P], TQ.t[:, :, 0:P], [TQ.b], [qkTs[s].b])
                    dma(tl["qkT"].rearrange("u p t -> p u t"), qkTs[s].t[:, :, 0:P], [qkTs[s].b], [], eng="pool")
                    for q in range(4):
                        dma(tl["outs"][q], stage[s].t[0:P, q * 512:(q + 1) * 512], [stage[s].b], [], eng="pool")
                    dma(tl["v"], vbf[s].t[0:P, :], [vbf[s].b], [], eng="pool")

                load(0)
                if n > 1:
                    load(1)
                self.norm(L, 0, tiles[0]["P"], Abc[tiles[0]["mod"]], shbc[tiles[0]["mod"]])
                for i in range(n):
                    if i + 1 < n:
                        self.norm(L, (i + 1) % 2, tiles[i + 1]["P"], Abc[tiles[i + 1]["mod"]], shbc[tiles[i + 1]["mod"]])
                    pe_main(i)
                    if i >= 1:
                        tq(i - 1)
                    if i + 2 < n:
                        load(i + 2)
                tq(n - 1)
                self.pg.finish()

        with contextlib.ExitStack() as es2:
            self.new_prog()
            A = self.alloc_attn(es2, T, PT, NKB)
            v_v = v_d.rearrange("(kb p) n -> p kb n", p=128)
            for u in range(8):
                s = u % 2
                isA = u < 4
                KT, V = A["KT"][s], A["V"][s]
                dma(KT.t[:, 0:S], qkT_d[(4 + u) if isA else (12 + u - 4)], [], [KT.b])
                col0 = u * 128 if isA else 512 + (u - 4) * 128
                for k0 in range(0, NKB, 16):
                    k1 = min(NKB, k0 + 16)
                    dma(V.t[:, k0:k1, 0:128], v_v[:, k0:k1, col0:col0 + 128], [], [V.b])
                if u < 2:
                    mset("pool", V.t[:, :, 128:129], 1.0, [V.b])
                for Tq in range(NQ):
                    qs = A["QT"][A["qi"] % 2]; A["qi"] += 1
                    dma(qs.t[:, :], qkT_d[u if isA else (8 + u - 4)][:, Tq * 512:(Tq + 1) * 512], [], [qs.b])
                    blocks = []
                    for kb in range(4 * Tq + 4):
                        j = kb - 4 * Tq
                        blocks.append((kb, 128, 128 * j if j > 0 else 0, j >= 0))
                    if isA:
                        chunks = [(m, 128 * m, 128, 4 * Tq + m) for m in range(4)]
                        ost = A["ost"][A["oi"] % 2]; A["oi"] += 1
                        self.a_tile(A, lambda c, kb, nk, KT=KT: KT.t[64 * c:64 * c + 64, kb * 128:(kb + 1) * 128], KT.b,
                                    lambda kb, nk, V=V: V.t[0:nk, kb, 0:129], V.b,
                                    blocks, 512, qs, chunks, ost, gsub, nlam, zer, bconst)
                        dst = oa_d.rearrange("(t m p) n -> t p m n", m=4, p=128)[Tq][:, :, u * 128:(u + 1) * 128]
                        dma(dst, ost.t[:, :, :], [ost.b], [], eng="pool")
                    else:
                        obs = A["obs"][A["oi"] % 2]; A["oi"] += 1
                        self.b_tile(A, lambda h2, kb, nk, KT=KT: KT.t[64 * h2:64 * h2 + 64, kb * 128:(kb + 1) * 128], KT.b,
                                    lambda h2, kb, nk, V=V: V.t[0:nk, kb, 64 * h2:64 * h2 + 64], V.b,
                                    blocks[::-1], 512, qs, obs, triP, onesT, maskT, bconst)
                        dma(obT_d[u - 4][:, Tq * 512:(Tq + 1) * 512], obs.t[:, :], [obs.b], [], eng="pool")
            self.pg.finish()

        if self.with_sample:
            with contextlib.ExitStack() as es2s:
                self.new_prog()
                self.sample_attn(es2s, T, PT, cak, cav, cbk, cbv, qkT_sd, v_sd, oa_sd, obT_sd,
                                 gsub, nlam, zer, triP, onesT, maskT, bconst)
                self.pg.finish()

        with contextlib.ExitStack() as es3:
            self.new_prog()
            self.phase3(es3, T, PT, w_in, wba_d, wbb_d, wo_d, xp, xs, oa_d, obT_d, oa_sd, obT_sd, y_p, y_s,
                        Abc, shbc, gtbc, fgbc, bconst)
            self.pg.finish()
    return nc


KB.build = _build


class Tv:
    __slots__ = ("t", "b")

    def __init__(self, ap, b):
        self.t = ap
        self.b = b


def _alloc_norm(self, es, T, PT):
    L = dict(
        xt=[T(es, "xt%d" % i, [128, D], F32) for i in range(2)],
        tmp=[T(es, "tmp%d" % i, [128, D], F32) for i in range(2)],
        hb=[T(es, "hb%d" % i, [128, D], BF16) for i in range(2)],
        hT=[T(es, "hT%d" % i, [128, 8, 128], BF16) for i in range(2)],
        sqj=T(es, "sqj", [128, D], BF16),
        smn=[T(es, "smn%d" % i, [128, 8], F32) for i in range(4)],
        TP=PT(es, "TP", [128, 8, 128], BF16),
        ni=0,
    )
    return L


def _norm(self, L, s, P, Abc, shbc):
    xt = L["xt"][s]; tmp = L["tmp"][s]; hb = L["hb"][s]
    sm = L["smn"][L["ni"] % 4]; L["ni"] += 1
    sqj = L["sqj"]
    self.act(sqj.t[0:P, :], xt.t[0:P, :], AF.Square, [xt.b], [sqj.b, sm.b], accum=sm.t[0:P, 0:1])
    self.rstd(sm.t[0:P, 0:1], sm.t[0:P, 1:2], sm.t[0:P, 2:3], sm.t[0:P, 3:4], 1.0 / D, sm.b, sm.b, sm.b)
    self.stt("dve", tmp.t[0:P, :], xt.t[0:P, :], sm.t[0:P, 3:4], Abc.t[0:P, :], ALU.mult, ALU.mult,
             [xt.b, sm.b, Abc.b], [tmp.b])
    self.tt("dve", hb.t[0:P, :], tmp.t[0:P, :], shbc.t[0:P, :], ALU.add, [tmp.b, shbc.b], [hb.b])


def _pe_hT(self, L, s, P):
    hb = L["hb"][s]; hT = L["hT"][s]; TP = L["TP"]
    for j in range(8):
        self.tr(TP.t[:, j, 0:P], hb.t[0:P, j * 128:(j + 1) * 128], P, [hb.b], [TP.b])
    self.cp("act", hT.t[:, :, 0:P], TP.t[:, :, 0:P], [TP.b], [hT.b])


def _rope(self, src, P, dst_ap, bdst, cst, snt, rt):
    pat = "p (g two f) -> p g two f"
    sv = src.t[0:P, :].rearrange(pat, two=2, f=32)
    dv = dst_ap.rearrange(pat, two=2, f=32)
    x1, x2 = sv[:, :, 0, :], sv[:, :, 1, :]
    cv = cst.t[0:P, :].rearrange("p (g f) -> p g f", f=32)
    sn = snt.t[0:P, :].rearrange("p (g f) -> p g f", f=32)
    t = [r_.t[0:P, :].rearrange("p (g f) -> p g f", f=32) for r_ in rt]
    tt = self.tt
    tt("dve", t[0], x1, cv, ALU.mult, [src.b, cst.b], [rt[0].b])
    tt("pool", t[1], x2, sn, ALU.mult, [src.b, snt.b], [rt[1].b])
    tt("dve", dv[:, :, 0, :], t[0], t[1], ALU.subtract, [rt[0].b, rt[1].b], [bdst])
    tt("pool", t[2], x2, cv, ALU.mult, [src.b, cst.b], [rt[2].b])
    tt("dve", t[3], x1, sn, ALU.mult, [src.b, snt.b], [rt[3].b])
    tt("pool", dv[:, :, 1, :], t[2], t[3], ALU.add, [rt[2].b, rt[3].b], [bdst])


def _p1_evac(self, g, pgt, P, s, stage, qkb, vbf, rp, rt, cst, snt):
    cp = self.cp
    if g == 0:
        cp("act", rp[0].t[0:P, :], pgt.t[0:P, :], [pgt.b], [rp[0].b])
        self.rope(rp[0], P, qkb.t[0:P, 0:512], qkb.b, cst, snt, rt)
    elif g == 1:
        cp("act", rp[1].t[0:P, :], pgt.t[0:P, :], [pgt.b], [rp[1].b])
        self.rope(rp[1], P, stage.t[0:P, 0:512], stage.b, cst, snt, rt)
        cp("dve", qkb.t[0:P, 512:1024], stage.t[0:P, 0:512], [stage.b], [qkb.b])
    elif g == 2:
        cp("act", stage.t[0:P, 512:1024], pgt.t[0:P, :], [pgt.b], [stage.b])
        cp("dve", vbf.t[0:P, 0:512], stage.t[0:P, 512:1024], [stage.b], [vbf.b])
    elif g == 3:
        cp("act", qkb.t[0:P, 1024:1536], pgt.t[0:P, :], [pgt.b], [qkb.b])
    elif g == 4:
        cp("act", stage.t[0:P, 1024:1536], pgt.t[0:P, :], [pgt.b], [stage.b])
        cp("dve", qkb.t[0:P, 1536:2048], stage.t[0:P, 1024:1536], [stage.b], [qkb.b])
    else:
        cp("act", stage.t[0:P, 1536:2048], pgt.t[0:P, :], [pgt.b], [stage.b])
        cp("dve", vbf.t[0:P, 512:1024], stage.t[0:P, 1536:2048], [stage.b], [vbf.b])


def _alloc_attn(self, es, T, PT, NKB, G=8):
    A = {}
    if NKB:
        A["KT"] = [T(es, "KT%d" % i, [128, NKB * 128], BF16) for i in range(2)]
        A["V"] = [T(es, "V%d" % i, [128, NKB, 130], BF16) for i in range(2)]
    A["zf"] = self.zf
    A["G"] = G
    A["QT"] = [T(es, "QT%d" % i, [128, 512], BF16) for i in range(2)]
    A["S"] = [PT(es, "S%d" % i, [128, 2, 512], F32) for i in range(2)]
    A["Sb"] = [[B(), B()], [B(), B()]]
    A["X"] = PT(es, "X", [128, 3, 512], F32)
    A["bX"] = [B(), B(), B()]
    A["E"] = [T(es, "E%d" % i, [128, 2, 512], BF16) for i in range(3)]
    A["spg"] = [T(es, "spg%d" % i, [128, 2, 512], BF16) for i in range(A["G"])]
    A["LsF"] = [T(es, "LsF%d" % i, [128, 2, 512], mybir.dt.float32r) for i in range(A["G"] + 1)]
    A["nq"] = [T(es, "nq%d" % i, [128, 512], BF16) for i in range(2)]
    A["a"] = [T(es, "a%d" % i, [128, 2, 512], BF16) for i in range(A["G"] + 3)]

    A["ost"] = [T(es, "ost%d" % i, [128, 4, 128], F32) for i in range(2)]
    A["obs"] = [T(es, "obs%d" % i, [128, 512], F32) for i in range(2)]
    A["oo"] = [T(es, "oo%d" % i, [128, 4, 128], F32) for i in range(2)]
    A["t0"] = [T(es, "t0%d" % i, [128, 128], F32) for i in range(2)]
    A["sma"] = [T(es, "sma%d" % i, [128, 16], F32) for i in range(4)]
    A["smb"] = [T(es, "smb%d" % i, [128, 12], F32) for i in range(4)]
    A["jk"] = T(es, "jk", [128, 128], F32)
    for k in ("si", "ei", "wi", "qi", "oi", "ti", "fi", "nqi", "zi"):
        A[k] = 0
    return A


def _a_tile(self, A, kt, bkt, vv, bv, blocks, N, qs, chunks, ost, gsub, nlam, zer, bconst):
    mm, act, tt, ts, stt, rcp, mset = self.mm, self.act, self.tt, self.ts, self.stt, self.rcp, self.mset
    nb = len(blocks)
    X, bX = A["X"], A["bX"]
    sbase, ebase = A["si"], A["ei"]
    A["si"] += nb; A["ei"] += nb
    ti = A["ti"]; A["ti"] += 1
    sm = A["sma"][ti % 4]; sm2 = A["smb"][ti % 4]; oo = A["oo"][ti % 2]; jk = A["jk"]
    qn0 = chunks[0][2]
    nm = len(chunks)

    def acc(m, c):
        a = m * 2 + c
        return X.t[:, a // 3, (a % 3) * 130:(a % 3) * 130 + 129], bX[a // 3]

    def qk(i):
        kbi, nk, c0, diag = blocks[i]
        Sl = A["S"][(sbase + i) % 2]; Sb = A["Sb"][(sbase + i) % 2]
        for c in range(2):
            mm(Sl.t[0:nk, c, c0:N], kt(c, kbi, nk), qs.t[64 * c:64 * c + 64, c0:N], True, True, [bkt, qs.b], [Sb[c]])

    def ex(i):
        kbi, nk, c0, diag = blocks[i]
        Sl = A["S"][(sbase + i) % 2]; El = A["E"][(ebase + i) % 3]; Sb = A["Sb"][(sbase + i) % 2]
        act(El.t[0:nk, :, c0:N], Sl.t[0:nk, :, c0:N], AF.Exp, Sb, [El.b], scale=0.125)
        if diag:
            mset("pool", El.t[64:128, :, c0:c0 + 64], 0.0, [El.b])

    def pv(i):
        kbi, nk, c0, diag = blocks[i]
        El = A["E"][(ebase + i) % 3]
        for c in range(2):
            for (m, q0, qn, last) in chunks:
                if q0 < c0:
                    continue
                o_ap, ob = acc(m, c)
                mm(o_ap[0:qn, :], El.t[0:nk, c, q0:q0 + qn], vv(kbi, nk), False, i == last, [El.b, bv], [ob], skip=True)

    def fin_chunk(m, q0, qn):
        (a0, b0), (a1, b1) = acc(m, 0), acc(m, 1)
        t0 = A["t0"][A["fi"] % 2]; A["fi"] += 1
        rcp(sm.t[0:qn, m:m + 1], a0[0:qn, 128:129], [b0], [sm.b])
        rcp(sm.t[0:qn, 4 + m:5 + m], a1[0:qn, 128:129], [b1], [sm.b])
        tt("dve", sm.t[0:qn, 8 + m:9 + m], sm.t[0:qn, 4 + m:5 + m], nlam.t[0:qn, :], ALU.mult, [sm.b], [sm.b])
        ts("dve", t0.t[0:qn, :], a0[0:qn, 0:128], sm.t[0:qn, m:m + 1], None, ALU.mult, None, [b0, sm.b], [t0.b])
        stt("dve", oo.t[0:qn, m, :], a1[0:qn, 0:128], sm.t[0:qn, 8 + m:9 + m], t0.t[0:qn, :], ALU.mult, ALU.add,
            [b1, sm.b, t0.b], [oo.b])
        stt("dve", jk.t[0:qn, :], oo.t[0:qn, m, :], 1.0, oo.t[0:qn, m, :], ALU.mult, ALU.mult, [oo.b], [jk.b, sm.b],
            accum=sm.t[0:qn, 12 + m:13 + m])

    qk(0)
    for b in range(3):
        mm(X.t[:, b, :], zer.t[:, 0:128], zer.t[:, :], True, False, [], [bX[b]], skip=True)
    for i in range(nb):
        if i + 1 < nb:
            qk(i + 1)
        ex(i)
        pv(i)
        for (m, q0, qn, last) in chunks:
            if last == i:
                fin_chunk(m, q0, qn)
    ts("dve", sm2.t[0:qn0, 0:nm], sm.t[0:qn0, 12:12 + nm], 1.0 / 128, EPS, ALU.mult, ALU.add, [sm.b], [sm2.b])
    act(sm2.t[0:qn0, 4:4 + nm], sm2.t[0:qn0, 0:nm], AF.Ln, [sm2.b], [sm2.b])
    act(sm2.t[0:qn0, 8:8 + nm], sm2.t[0:qn0, 4:4 + nm], AF.Exp, [sm2.b], [sm2.b], scale=-0.5)
    for (m, q0, qn, last) in chunks:
        stt("dve", ost.t[0:qn, m, :], oo.t[0:qn, m, :], sm2.t[0:qn, 8 + m:9 + m], gsub.t[0:qn, :], ALU.mult, ALU.mult,
            [oo.b, sm2.b], [ost.b])


def _b_tile(self, A, kt, bkt, vv, bv, blocks, N, qs, obs, triP, ones, maskT, bconst):
    mm, act, tt, cp = self.mm, self.act, self.tt, self.cp
    nb = len(blocks)
    G = A["G"]
    X, bX = A["X"], A["bX"]
    S0, S1 = A["S"]; Sb = A["Sb"]
    zslots = [(X.t[:, 0, :], bX[0]), (X.t[:, 1, :], bX[1]), (S0.t[:, 0, :], Sb[0][0]), (S0.t[:, 1, :], Sb[0][1]),
              (S1.t[:, 0, :], Sb[1][0]), (S1.t[:, 1, :], Sb[1][1])]
    NZ = len(zslots)
    nq = A["nq"][A["nqi"] % 2]; A["nqi"] += 1
    LsF = A["LsF"]; R = len(LsF); zf = A["zf"]
    NA = len(A["a"])
    abase = A["wi"]; A["wi"] += nb
    cbase = A["si"]; A["si"] += nb
    zbase = A["zi"]; A["zi"] += 2 * nb
    self.ts("pool", nq.t[:, 0:N], qs.t[:, 0:N], -0.125, None, ALU.mult, None, [qs.b], [nq.b])

    def zs(i, h2):
        return zslots[(zbase + 2 * i + h2) % NZ]

    ctiles = [(S0.t, Sb[0]), (S1.t, Sb[1]), (X.t[:, 0:2, :], [bX[0], bX[1]])]

    def cb(i):
        return ctiles[(cbase + i) % 3]

    def qk(i, h2):
        kbi, nk, c0, diag = blocks[i]
        zt, zb_ = zs(i, h2)
        mm(zt[0:nk, c0:N], kt(h2, kbi, nk), qs.t[64 * h2:64 * h2 + 64, c0:N], True, True, [bkt, qs.b], [zb_])

    def spl(i):
        kbi, nk, c0, diag = blocks[i]
        sp = A["spg"][i % G]
        for h2 in range(2):
            zt, zb_ = zs(i, h2)
            act(sp.t[0:nk, h2, c0:N], zt[0:nk, c0:N], AF.Softplus, [zb_], [sp.b], scale=0.125)
        if diag:
            mw = min(nk, N - c0)
            for h2 in range(2):
                tt("pool", sp.t[0:nk, h2, c0:c0 + mw], sp.t[0:nk, h2, c0:c0 + mw], maskT.t[0:nk, 0:mw], ALU.mult, [sp.b], [sp.b])
        if i + 1 < nb:
            cur = LsF[i % R]; nxt = LsF[(i + 1) % R]
            full = (nk == 128 and c0 == 0)
            if i == 0:
                if not full:
                    cp("dve", nxt.t[:, :, 0:N], zf.t[:, :, 0:N], [], [nxt.b])
                cp("dve", nxt.t[0:nk, :, c0:N], sp.t[0:nk, :, c0:N], [sp.b], [nxt.b])
            else:
                assert nk == 128
                if c0 > 0:
                    cp("dve", nxt.t[:, :, 0:c0], zf.t[:, :, 0:c0], [], [nxt.b])
                tt("dve", nxt.t[:, :, c0:N], cur.t[:, :, c0:N], sp.t[:, :, c0:N], ALU.add, [cur.b, sp.b], [nxt.b])

    def cmm(i):
        kbi, nk, c0, diag = blocks[i]
        (Ct, Cb) = cb(i); sp = A["spg"][i % G]; cur = LsF[i % R]
        for h2 in range(2):
            mm(Ct[0:nk, h2, c0:N], triP.t[0:nk, 0:nk], sp.t[0:nk, h2, c0:N], True, False, [sp.b], Cb)
            if i > 0:
                mm(Ct[:, h2, c0:N], ones.t[:, :], cur.t[:, h2, c0:N], False, False, [cur.b], Cb)
        for h2 in range(2):
            mm(Ct[0:nk, h2, c0:N], kt(h2, kbi, nk), nq.t[64 * h2:64 * h2 + 64, c0:N], False, True, [bkt, nq.b], Cb)

    def ex(i):
        kbi, nk, c0, diag = blocks[i]
        (Ct, Cb) = cb(i); a = A["a"][(abase + i) % NA]
        act(a.t[0:nk, :, c0:N], Ct[0:nk, :, c0:N], AF.Exp, Cb, [a.b], scale=-1.0)
        if diag:
            mw = min(nk, N - c0)
            for h2 in range(2):
                tt("pool", a.t[0:nk, h2, c0:c0 + mw], a.t[0:nk, h2, c0:c0 + mw], maskT.t[0:nk, 0:mw], ALU.mult, [a.b], [a.b])

    def pv(i):
        kbi, nk, c0, diag = blocks[i]
        a = A["a"][(abase + i) % NA]
        for h2 in range(2):
            mm(X.t[64 * h2:64 * h2 + 64, 2, c0:N], vv(h2, kbi, nk), a.t[0:nk, h2, c0:N], i == 0, i == nb - 1,
               [a.b, bv], [bX[2]], skip=True)

    groups = [(g0, min(nb, g0 + G)) for g0 in range(0, nb, G)]
    for gi, (g0, g1) in enumerate(groups):
        prev = groups[gi - 1] if gi > 0 else None
        pend = list(range(prev[0], prev[1])) if prev else []
        zq = [(i, h2) for i in range(g0, g1) for h2 in range(2)]
        for k in range(min(NZ, len(zq))):
            qk(*zq[k])
        zn = NZ
        for i in range(g0, g1):
            spl(i)
            for _ in range(2):
                if zn < len(zq):
                    qk(*zq[zn]); zn += 1
            if pend:
                pv(pend.pop(0))
        while pend:
            pv(pend.pop(0))
        ahead = min(3, g1 - g0)
        for i in range(g0, g0 + ahead):
            cmm(i)
        for i in range(g0, g1):
            ex(i)
            if i + ahead < g1:
                cmm(i + ahead)
    for i in range(groups[-1][0], groups[-1][1]):
        pv(i)
    cp("dve", obs.t[:, 0:N], X.t[:, 2, 0:N], [bX[2]], [obs.b])


KB.alloc_norm = _alloc_norm
KB.norm = _norm
KB.pe_hT = _pe_hT
KB.rope = _rope
KB.p1_evac = _p1_evac
KB.alloc_attn = _alloc_attn
KB.a_tile = _a_tile
KB.b_tile = _b_tile


def _sample_attn(self, es, T, PT, cak, cav, cbk, cbv, qkT_sd, v_sd, oa_sd, obT_sd,
                 gsub, nlam, zer, triP, ones, maskT, bconst):
    mm, tr, act, tt, cp, mset, dma = self.mm, self.tr, self.act, self.tt, self.cp, self.mset, self.dma
    A = self.alloc_attn(es, T, PT, 0, G=4)
    NK = PAST + NS
    KTa = T(es, "KTa", [128, 4, NK], BF16); KTb = T(es, "KTb", [128, 4, NK], BF16)
    VAs = T(es, "VAs", [128, 17, 4, 130], BF16); VBs = T(es, "VBs", [128, 17, 512], BF16)
    ct = [T(es, "ct%d" % i, [128, 512], F32) for i in range(4)]
    qsa = T(es, "qsa", [128, 4, NS], BF16); qsb = T(es, "qsb", [128, 4, NS], BF16)
    Yv = A["X"].t[:, 2, :].rearrange("p (h k) -> p h k", k=128)
    bY = A["bX"][2]
    mset("pool", VAs.t[:, :, :, 128:129], 1.0, [VAs.b])
    qv = qkT_sd.rearrange("u p t -> p u t")
    ci = 0
    for s in range(2):
        t0, t1 = s * NS, (s + 1) * NS
        for i in range(16):
            r0 = i * 128
            for which, src in enumerate((cak, cbk, cav, cbv)):
                c_ = ct[ci % 4]; ci += 1
                dma(c_.t[:, :], src[s, r0:r0 + 128, :], [], [c_.b])
                if which < 2:
                    for h in range(4):
                        self.tr32(Yv[:, h, :], c_.t[:, h * 128:(h + 1) * 128], 128, [c_.b], [bY])
                    dstT = KTa if which == 0 else KTb
                    cp("act", dstT.t[:, :, r0:r0 + 128], Yv, [bY], [dstT.b])
                elif which == 2:
                    cp("dve", VAs.t[:, i, :, 0:128], c_.t[:, :].rearrange("p (h e) -> p h e", e=128), [c_.b], [VAs.b])
                else:
                    cp("dve", VBs.t[:, i, :], c_.t[:, :], [c_.b], [VBs.b])
        dma(KTa.t[:, :, PAST:NK], qv[:, 4:8, t0:t1], [], [KTa.b])
        dma(KTb.t[:, :, PAST:NK], qv[:, 12:16, t0:t1], [], [KTb.b])
        dma(VAs.t[0:NS, 16, :, 0:128], v_sd[t0:t1, 0:512].rearrange("t (h e) -> t h e", e=128), [], [VAs.b])
        dma(VBs.t[0:NS, 16, :], v_sd[t0:t1, 512:1024], [], [VBs.b])
        dma(qsa.t[:, :, :], qv[:, 0:4, t0:t1], [], [qsa.b])
        dma(qsb.t[:, :, :], qv[:, 8:12, t0:t1], [], [qsb.b])
        blocks = [(kb, 128, 0, False) for kb in range(16)] + [(16, NS, 0, False)]
        for h in range(4):
            ost = A["ost"][A["oi"] % 2]; A["oi"] += 1
            self.a_tile(A, lambda c, kb, nk, h=h: KTa.t[64 * c:64 * c + 64, h, kb * 128:kb * 128 + nk], KTa.b,
                        lambda kb, nk, h=h: VAs.t[0:nk, kb, h, 0:129], VAs.b,
                        blocks, NS, Tv(qsa.t[:, h, :], qsa.b), [(0, 0, NS, 16)], ost, gsub, nlam, zer, bconst)
            dma(oa_sd[t0:t1, h * 128:(h + 1) * 128], ost.t[0:NS, 0, :], [ost.b], [], eng="pool")
        blocks_b = [(16, NS, 0, True)] + [(kb, 128, 0, False) for kb in range(15, -1, -1)]
        for p in range(4):
            obs = A["obs"][A["oi"] % 2]; A["oi"] += 1
            self.b_tile(A, lambda h2, kb, nk, p=p: KTb.t[64 * h2:64 * h2 + 64, p, kb * 128:kb * 128 + nk], KTb.b,
                        lambda h2, kb, nk, p=p: VBs.t[0:nk, kb, p * 128 + 64 * h2:p * 128 + 64 * h2 + 64], VBs.b,
                        blocks_b, NS, Tv(qsb.t[:, p, :], qsb.b), obs, triP, ones, maskT, bconst)
            dma(obT_sd[p][:, t0:t1], obs.t[:, 0:NS], [obs.b], [], eng="pool")


def _phase3(self, es, T, PT, w_in, wba_d, wbb_d, wo_d, xp, xs, oa_d, obT_d, oa_sd, obT_sd, y_p, y_s,
            Abc, shbc, gtbc, fgbc, bconst):
    mm, tr, act, tt, stt, cp, dma = self.mm, self.tr, self.act, self.tt, self.stt, self.cp, self.dma
    NT = self.NT
    L = self.alloc_norm(es, T, PT)
    wg = T(es, "wg", [128, 8, 3072], BF16)
    wba = T(es, "wba", [128, 4, D], BF16); wbb = T(es, "wbb", [128, 4, D], BF16)
    wo = T(es, "wo", [128, 8, D], BF16)
    wst = [T(es, "wst3%d" % i, [128, 4, 512], F32) for i in range(2)]
    oat = [T(es, "oat%d" % i, [128, 512], F32) for i in range(2)]
    obt = [T(es, "obt%d" % i, [128, 4, 128], F32) for i in range(2)]
    sza = T(es, "sza", [128, 512], F32); u1 = T(es, "u1", [128, 512], F32); ua = T(es, "ua", [128, 512], BF16)
    szb = T(es, "szb", [128, 4, 128], F32); u2 = T(es, "u2", [128, 4, 128], F32); ubT = T(es, "ubT", [128, 4, 128], BF16)
    sga = T(es, "sga", [128, D], F32); sgb = T(es, "sgb", [128, D], F32)
    uaT = T(es, "uaT", [128, 4, 128], BF16)
    m1 = T(es, "m1", [128, D], F32); m2 = T(es, "m2", [128, D], F32); mb = T(es, "mb", [128, D], BF16)
    mT = T(es, "mT", [128, 8, 128], BF16)
    ys = [T(es, "ys%d" % i, [128, D], F32) for i in range(2)]
    sm3 = [T(es, "sm3%d" % i, [128, 8], F32) for i in range(4)]
    ZA = PT(es, "ZA", [128, 512], F32); ZB = PT(es, "ZB", [128, 4, 128], F32)
    WA = PT(es, "WA", [128, 2, 512], F32); WB = PT(es, "WB", [128, 2, 512], F32)
    TP = L["TP"]
    flat = "p a b -> p (a b)"

    wi_v = w_in.rearrange("(j p) n -> p j n", p=128)
    k = 0
    srcs = [1536, 3584, 4096, 4608, 5120, 5632]
    for g in range(6):
        for half in range(2):
            w_ = wst[k % 2]; k += 1
            dma(w_.t[:], wi_v[:, 4 * half:4 * half + 4, srcs[g]:srcs[g] + 512], [], [w_.b])
            cp("dve" if k % 2 else "pool", wg.t[:, 4 * half:4 * half + 4, g * 512:(g + 1) * 512], w_.t[:], [w_.b], [wg.b])
    for wd, wt in ((wba_d, wba), (wbb_d, wbb)):
        wv = wd.rearrange("(c p) n -> p c n", p=128)
        for nh in range(2):
            w_ = wst[k % 2]; k += 1
            dma(w_.t[:], wv[:, :, nh * 512:(nh + 1) * 512], [], [w_.b])
            cp("dve" if k % 2 else "pool", wt.t[:, :, nh * 512:(nh + 1) * 512], w_.t[:], [w_.b], [wt.b])
    wv = wo_d.rearrange("(j p) n -> p j n", p=128)
    for half in range(2):
        for nh in range(2):
            w_ = wst[k % 2]; k += 1
            dma(w_.t[:], wv[:, 4 * half:4 * half + 4, nh * 512:(nh + 1) * 512], [], [w_.b])
            cp("dve" if k % 2 else "pool", wo.t[:, 4 * half:4 * half + 4, nh * 512:(nh + 1) * 512], w_.t[:], [w_.b], [wo.b])

    tiles = []
    obT_v = obT_d.rearrange("u p t -> p u t")
    for t in range(NT):
        r0 = t * 128
        tiles.append(dict(P=128, x=xp[r0:r0 + 128, :], mod=0, oa=oa_d[r0:r0 + 128, :], ob=obT_v[:, :, r0:r0 + 128],
                          y=y_p[r0:r0 + 128, :]))
    if self.with_sample:
        tiles.append(dict(P=32, x=xs, mod=1, oa=oa_sd, ob=obT_sd.rearrange("u p t -> p u t"), y=y_s))
    n = len(tiles)

    def load(i):
        tl = tiles[i]; P = tl["P"]; s = i % 2
        dma(L["xt"][s].t[0:P, :], tl["x"], [], [L["xt"][s].b])
        dma(oat[s].t[0:P, :], tl["oa"], [], [oat[s].b])
        dma(obt[s].t[:, :, 0:P], tl["ob"], [], [obt[s].b])

    def head(i):
        tl = tiles[i]; P = tl["P"]; s = i % 2
        hT = L["hT"][s]
        self.pe_hT(L, s, P)
        for j in range(8):
            mm(ZA.t[0:P, :], hT.t[:, j, 0:P], wg.t[:, j, 0:512], j == 0, j == 7, [hT.b, wg.b], [ZA.b])
        for fc in range(4):
            for j in range(8):
                mm(ZB.t[:, fc, 0:P], wg.t[:, j, 512 + fc * 128:512 + (fc + 1) * 128], hT.t[:, j, 0:P], j == 0, j == 7,
                   [hT.b, wg.b], [ZB.b])
        for nh in range(2):
            for j in range(8):
                mm(WB.t[0:P, nh, :], hT.t[:, j, 0:P], wg.t[:, j, 2048 + nh * 512:2048 + (nh + 1) * 512], j == 0, j == 7,
                   [hT.b, wg.b], [WB.b])

    def head_b(i):
        tl = tiles[i]; P = tl["P"]; s = i % 2
        hT = L["hT"][s]
        for nh in range(2):
            for j in range(8):
                mm(WA.t[0:P, nh, :], hT.t[:, j, 0:P], wg.t[:, j, 1024 + nh * 512:1024 + (nh + 1) * 512], j == 0, j == 7,
                   [hT.b, wg.b], [WA.b])

    def mid(i):
        tl = tiles[i]; P = tl["P"]; s = i % 2
        act(sza.t[0:P, :], ZA.t[0:P, :], AF.Sigmoid, [ZA.b], [sza.b])
        tt("dve", u1.t[0:P, :], ZA.t[0:P, :], sza.t[0:P, :], ALU.mult, [ZA.b, sza.b], [u1.b])
        tt("dve", ua.t[0:P, :], u1.t[0:P, :], oat[s].t[0:P, :], ALU.mult, [u1.b, oat[s].b], [ua.b])
        act(szb.t[:, :, 0:P], ZB.t[:, :, 0:P], AF.Sigmoid, [ZB.b], [szb.b])
        tt("dve", u2.t[:, :, 0:P], ZB.t[:, :, 0:P], szb.t[:, :, 0:P], ALU.mult, [ZB.b, szb.b], [u2.b])
        tt("dve", ubT.t[:, :, 0:P], u2.t[:, :, 0:P], obt[s].t[:, :, 0:P], ALU.mult, [u2.b, obt[s].b], [ubT.b])
        act(sgb.t[0:P, :], WB.t[0:P, :, :].rearrange(flat), AF.Sigmoid, [WB.b], [sgb.b])
        act(sga.t[0:P, :], WA.t[0:P, :, :].rearrange(flat), AF.Sigmoid, [WA.b], [sga.b])
        for c in range(4):
            tr(TP.t[:, c, 0:P], ua.t[0:P, c * 128:(c + 1) * 128], P, [ua.b], [TP.b])
        cp("act", uaT.t[:, :, 0:P], TP.t[:, 0:4, 0:P], [TP.b], [uaT.b])
        for nh in range(2):
            for c in range(4):
                mm(WB.t[0:P, nh, :], ubT.t[:, c, 0:P], wbb.t[:, c, nh * 512:(nh + 1) * 512], c == 0, c == 3, [ubT.b, wbb.b], [WB.b])
        for nh in range(2):
            for c in range(4):
                mm(WA.t[0:P, nh, :], uaT.t[:, c, 0:P], wba.t[:, c, nh * 512:(nh + 1) * 512], c == 0, c == 3, [uaT.b, wba.b], [WA.b])
        tt("dve", m2.t[0:P, :], WB.t[0:P, :, :].rearrange(flat), sgb.t[0:P, :], ALU.mult, [WB.b, sgb.b], [m2.b])
        tt("dve", m1.t[0:P, :], WA.t[0:P, :, :].rearrange(flat), sga.t[0:P, :], ALU.mult, [WA.b, sga.b], [m1.b])
        tt("dve", mb.t[0:P, :], m1.t[0:P, :], m2.t[0:P, :], ALU.add, [m1.b, m2.b], [mb.b])
        for j in range(8):
            tr(TP.t[:, j, 0:P], mb.t[0:P, j * 128:(j + 1) * 128], P, [mb.b], [TP.b])
        cp("act", mT.t[:, :, 0:P], TP.t[:, :, 0:P], [TP.b], [mT.b])
        for nh in range(2):
            for j in range(8):
                mm(WA.t[0:P, nh, :], mT.t[:, j, 0:P], wo.t[:, j, nh * 512:(nh + 1) * 512], j == 0, j == 7, [mT.b, wo.b], [WA.b])

    def tail_a(i):
        tl = tiles[i]; P = tl["P"]; md = tl["mod"]
        tt("dve", m1.t[0:P, :], WA.t[0:P, :, :].rearrange(flat), gtbc[md].t[0:P, :], ALU.mult, [WA.b, gtbc[md].b], [m1.b])

    def tail(i):
        tl = tiles[i]; P = tl["P"]; s = i % 2; md = tl["mod"]
        xt = L["xt"][s]
        tt("dve", m2.t[0:P, :], m1.t[0:P, :], xt.t[0:P, :], ALU.add, [m1.b, xt.b], [m2.b])
        sm = sm3[i % 4]; sqj = L["sqj"]
        act(sqj.t[0:P, :], m2.t[0:P, :], AF.Square, [m2.b], [sqj.b, sm.b], accum=sm.t[0:P, 0:1])
        self.rstd(sm.t[0:P, 0:1], sm.t[0:P, 1:2], sm.t[0:P, 2:3], sm.t[0:P, 3:4], 1.0 / D, sm.b, sm.b, sm.b)
        stt("dve", ys[s].t[0:P, :], m2.t[0:P, :], sm.t[0:P, 3:4], fgbc.t[0:P, :], ALU.mult, ALU.mult, [m2.b, sm.b, fgbc.b], [ys[s].b])
        dma(tl["y"], ys[s].t[0:P, :], [ys[s].b], [], eng="pool")

    load(0)
    if n > 1:
        load(1)
    self.norm(L, 0, tiles[0]["P"], Abc[tiles[0]["mod"]], shbc[tiles[0]["mod"]])
    head(0); head_b(0)
    for i in range(n):
        if i + 1 < n:
            self.norm(L, (i + 1) % 2, tiles[i + 1]["P"], Abc[tiles[i + 1]["mod"]], shbc[tiles[i + 1]["mod"]])
        mid(i)
        if i + 1 < n:
            head(i + 1)
        tail_a(i)
        if i + 1 < n:
            head_b(i + 1)
        tail(i)
        if i + 2 < n:
            load(i + 2)


KB.sample_attn = _sample_attn
KB.phase3 = _phase3


def _rope_tables(pos):
    half = 32
    inv = (np.float32(10000.0) ** (-np.arange(half, dtype=np.float32) / np.float32(half))).astype(np.float32)
    ang = pos.astype(np.float32)[:, None] * inv[None, :]
    cos = np.cos(ang).astype(np.float32); sin = np.sin(ang).astype(np.float32)
    return np.tile(cos, (1, 8)), np.tile(sin, (1, 8))


_CACHE = {}


def run(inputs, NT, n_cores, with_sample=True, trace=False):
    key = (NT, with_sample)
    if key not in _CACHE:
        kb = KB(NT, with_sample)
        kb.build()
        _CACHE[key] = kb
    kb = _CACHE[key]
    S = NT * 128
    bf = ml_dtypes.bfloat16
    f32 = np.float32
    g = lambda k: np.ascontiguousarray(np.asarray(inputs[k], dtype=f32))
    xP, xS, cP, cS = g("x_prompt"), g("x_sample"), g("c_prompt"), g("c_sample")
    cos_p, sin_p = _rope_tables(np.arange(S))
    pos_s = PAST + np.tile(np.arange(NS), 2)
    cos_s, sin_s = _rope_tables(pos_s)
    idx = np.arange(128)
    consts = dict(
        ident=np.eye(128, dtype=f32).astype(bf),
        triP=(idx[:, None] >= idx[None, :]).astype(f32).astype(bf),
        ident32=np.eye(128, dtype=f32),
        mask_lt=(idx[:, None] < idx[None, :]).astype(f32),
        cos_p=cos_p, sin_p=sin_p, cos_s=cos_s, sin_s=sin_s,
        w_ada=g("w_ada")[0], b_ada=g("b_ada")[0], norm_g=g("norm_g")[0], w_in=g("w_in")[0],
        lams=np.concatenate([g("lambda_q1")[0], g("lambda_k1")[0], g("lambda_q2")[0], g("lambda_k2")[0]]),
        subln_g=g("subln_g")[0], wba=g("w_branch_a")[0], wbb=g("w_branch_b")[0], w_out=g("w_out")[0],
        final_g=g("final_g"),
    )
    cak, cav, cbk, cbv = (g(k)[0].reshape(-1, PAST, 512) for k in ("cache_a_k", "cache_a_v", "cache_b_k", "cache_b_v"))
    in_maps = []
    for i in range(n_cores):
        m = dict(consts)
        m["xp"] = xP[i]
        m["xs"] = np.ascontiguousarray(xS[2 * i:2 * i + 2].reshape(32, D))
        cm = cP[i].reshape(8, 128).T
        m["cmat_p"] = np.ascontiguousarray(np.broadcast_to(cm[:, :, None], (128, 8, 128)))
        cs = cS[2 * i:2 * i + 2].reshape(2, 8, 128).transpose(2, 1, 0)
        m["cmat_s"] = np.ascontiguousarray(np.repeat(cs, NS, axis=2))
        for nm_, arr in (("cak", cak), ("cav", cav), ("cbk", cbk), ("cbv", cbv)):
            m[nm_] = np.ascontiguousarray(arr[2 * i:2 * i + 2])
        in_maps.append(m)
    res = run_bass_kernel_spmd(kb.nc, in_maps, core_ids=list(range(n_cores)), trace=trace)
    return res


def kernel(**inputs):
    NT = 64
    res = run(inputs, NT, 8)
    r = res.results
    S = NT * 128
    cat = lambda k: np.stack([r[i][k] for i in range(8)], 0)
    cats = lambda k: np.concatenate([r[i][k].reshape(2, NS, -1) for i in range(8)], 0)
    y_prompt = cat("y_p")
    y_sample = cats("y_s")
    pak = cat("pak").reshape(1, 8, S, 4, 2, 64)
    pav = cat("pav").reshape(1, 8, S, 4, 128)
    pbk = cat("pbk").reshape(1, 8, S, 8, 64)
    pbv = cat("pbv").reshape(1, 8, S, 8, 64)
    sak = cats("sak").reshape(1, 16, NS, 4, 2, 64)
    sav = cats("sav").reshape(1, 16, NS, 4, 128)
    sbk = cats("sbk").reshape(1, 16, NS, 8, 64)
    sbv = cats("sbv").reshape(1, 16, NS, 8, 64)
    return (y_prompt, y_sample, pak, pav, pbk, pbv, sak, sav, sbk, sbv)
```

```python
import contextlib
import numpy as np
import ml_dtypes
import concourse.bass as bass
import concourse.mybir as mybir
from concourse.bass_utils import run_bass_kernel_spmd

F32 = mybir.dt.float32
BF16 = mybir.dt.bfloat16
AF = mybir.ActivationFunctionType
ALU = mybir.AluOpType

ENGS = ("pe", "act", "dve", "pool", "sp")
ND_SEMS = 24


class B:
    __slots__ = ("name", "last_w", "readers")

    def __init__(self, name=""):
        self.name = name
        self.last_w = None
        self.readers = []


class Prog:
    def __init__(self, nc, sems, state):
        self.nc = nc
        self.sems = sems
        self.st = state
        self.ops = {e: [] for e in ENGS}
        self.touched = set()

    def add(self, eng, fn, reads=(), writes=(), dma=False):
        ops = self.ops[eng]
        idx = len(ops)
        deps = {}
        self.touched.update(reads)
        self.touched.update(writes)
        for b in reads:
            if b.last_w is not None:
                deps[b.last_w] = deps.get(b.last_w, 0) | 1
        for b in writes:
            if b.last_w is not None:
                deps[b.last_w] = deps.get(b.last_w, 0) | 2
            for r in b.readers:
                deps[r] = deps.get(r, 0) | 4
        if dma:
            did = self.st["dma_id"]
            self.st["dma_id"] += 1
            ev = ("dma", did)
        else:
            did = None
            ev = (eng, idx)
        deps.pop(ev, None)
        for b in reads:
            b.readers.append(ev)
        for b in writes:
            b.last_w = ev
            b.readers = []
        ops.append({"fn": fn, "deps": deps, "dma": did, "sig": False})
        return ev

    def finish(self):
        nc = self.nc
        st = self.st
        ops = self.ops
        for e in ENGS:
            for op in ops[e]:
                for (pe_, pidx), kind in op["deps"].items():
                    if pe_ == "dma":
                        continue
                    if pe_ == e and e == "pe":
                        continue
                    ops[pe_][pidx]["sig"] = True
        for e in ENGS:
            if e != "sp" and ops[e]:
                ops[e][-1]["sig"] = True
        cnt = {}
        for e in ENGS:
            c = st["sig"][e]
            for i, op in enumerate(ops[e]):
                if op["sig"] and op["dma"] is None:
                    c += 1
                cnt[(e, i)] = c
            st["sig_end"] = st.get("sig_end", {})
            st["sig_end"][e] = c
        final_cnt = dict(st["sig_end"])
        dma_first = st["dma_id"] - sum(1 for e in ENGS for op in ops[e] if op["dma"] is not None)
        dma_last = st["dma_id"]

        def dma_target(did):
            return did % ND_SEMS, 16 * (did // ND_SEMS + 1)

        sems = self.sems
        waited = st["waited"]

        def emit_engine(e, engobj):
            w = waited[e]

            def wait(key, semh, val):
                if w.get(key, 0) >= val:
                    return
                engobj.wait_ge(semh, val)
                w[key] = val

            for i, op in enumerate(ops[e]):
                need = {}
                for (pe_, pidx), kind in op["deps"].items():
                    if pe_ == "dma":
                        si, val = dma_target(pidx)
                        key = ("dma", si)
                        need[key] = max(need.get(key, 0), val)
                    else:
                        if pe_ == e and e == "pe":
                            continue
                        need[pe_] = max(need.get(pe_, 0), cnt[(pe_, pidx)])
                if op["dma"] is not None and op["dma"] >= ND_SEMS:
                    si, val = dma_target(op["dma"] - ND_SEMS)
                    key = ("dma", si)
                    need[key] = max(need.get(key, 0), val)
                for key, val in need.items():
                    if isinstance(key, tuple):
                        wait(key, sems["dma"][key[1]], val)
                    else:
                        wait(key, sems[key], val)
                ins = op["fn"](engobj)
                if op["dma"] is not None:
                    si, _ = dma_target(op["dma"])
                    ins.then_inc(sems["dma"][si], 16)
                elif op["sig"]:
                    ins.then_inc(sems[e], 1)
            for pe_ in ENGS:
                if pe_ == "sp" or pe_ == e:
                    continue
                if final_cnt[pe_] > 0:
                    wait(pe_, sems[pe_], final_cnt[pe_])
            for did in range(max(dma_first, dma_last - ND_SEMS), dma_last):
                si, val = dma_target(did)
                wait(("dma", si), sems["dma"][si], val)

        with nc.Block() as block:
            @block.tensor
            def _(eng):
                emit_engine("pe", eng)

            @block.scalar
            def _(eng):
                emit_engine("act", eng)

            @block.vector
            def _(eng):
                emit_engine("dve", eng)

            @block.gpsimd
            def _(eng):
                emit_engine("pool", eng)

            @block.sync
            def _(eng):
                emit_engine("sp", eng)

        for e in ENGS:
            st["sig"][e] = final_cnt[e]
        for b in self.touched:
            b.last_w = None
            b.readers = []


def new_state():
    return {"dma_id": 0, "sig": {e: 0 for e in ENGS}, "waited": {e: {} for e in ENGS}}


D = 1024
EPS = 1e-6
LAM_INIT = 0.2
PAST = 2048
NS = 16


class KB:
    def __init__(self, NT, with_sample=True):
        self.NT = NT
        self.S = NT * 128
        self.with_sample = with_sample
        self.nc = bass.Bass("TRN2", target_bir_lowering=False)
        self.st = new_state()
        self.uid = 0

    def din(self, name, shape, dt=F32):
        return self.nc.dram_tensor(name, list(shape), dt, kind="ExternalInput").ap()

    def dout(self, name, shape, dt=F32):
        return self.nc.dram_tensor(name, list(shape), dt, kind="ExternalOutput").ap()

    def dscr(self, name, shape, dt):
        return self.nc.dram_tensor(name, list(shape), dt).ap()

    def sb(self, es, name, shape, dt):
        self.uid += 1
        return es.enter_context(self.nc.sbuf_tensor("%s_%d" % (name, self.uid), list(shape), dt))

    def ps(self, es, name, shape, dt):
        self.uid += 1
        return es.enter_context(self.nc.psum_tensor("%s_%d" % (name, self.uid), list(shape), dt))

    def mm(self, out, lhsT, rhs, start, stop, r, w, skip=False):
        self.pg.add("pe", lambda e: e.matmul(out, lhsT=lhsT, rhs=rhs, start=start, stop=stop,
                                             skip_group_check=skip), r, w)

    def tr(self, out, in_, P, r, w):
        ident = self.ident
        self.pg.add("pe", lambda e: e.transpose(out=out, in_=in_, identity=ident[0:P, 0:P]), r, w)

    def tr32(self, out, in_, P, r, w):
        ident = self.ident32
        self.pg.add("pe", lambda e: e.transpose(out=out, in_=in_, identity=ident[0:P, 0:P]), r, w)

    def act(self, out, in_, func, r, w, scale=1.0, bias=0.0, accum=None):
        if accum is None:
            self.pg.add("act", lambda e: e.activation(out=out, in_=in_, func=func, bias=bias, scale=scale), r, w)
        else:
            self.pg.add("act", lambda e: e.activation(out=out, in_=in_, func=func, bias=bias, scale=scale,
                                                      accum_out=accum), r, w)

    def tt(self, eng, out, a, b, op, r, w):
        self.pg.add(eng, lambda e: e.tensor_tensor(out=out, in0=a, in1=b, op=op), r, w)

    def ts(self, eng, out, a, s1, s2, op0, op1, r, w):
        if s2 is None:
            self.pg.add(eng, lambda e: e.tensor_scalar(out=out, in0=a, scalar1=s1, scalar2=None, op0=op0), r, w)
        else:
            self.pg.add(eng, lambda e: e.tensor_scalar(out=out, in0=a, scalar1=s1, scalar2=s2, op0=op0, op1=op1), r, w)

    def stt(self, eng, out, a, s, b, op0, op1, r, w, accum=None):
        if accum is None:
            self.pg.add(eng, lambda e: e.scalar_tensor_tensor(out=out, in0=a, scalar=s, in1=b, op0=op0, op1=op1), r, w)
        else:
            self.pg.add(eng, lambda e: e.scalar_tensor_tensor(out=out, in0=a, scalar=s, in1=b, op0=op0, op1=op1,
                                                              accum_out=accum), r, w)

    def cp(self, eng, out, in_, r, w):
        if eng == "act":
            self.pg.add("act", lambda e: e.activation(out=out, in_=in_, func=AF.Copy), r, w)
        else:
            self.pg.add(eng, lambda e: e.tensor_copy(out=out, in_=in_), r, w)

    def rcp(self, out, in_, r, w):
        self.pg.add("dve", lambda e: e.reciprocal(out=out, in_=in_), r, w)

    def mset(self, eng, ap, val, w):
        self.pg.add(eng, lambda e: e.memset(ap, val), (), w)

    def dma(self, out, in_, r, w, eng="sp"):
        self.pg.add(eng, lambda e: e.dma_start(out=out, in_=in_), r, w, dma=True)

    def new_prog(self):
        self.pg = Prog(self.nc, self.sems, self.st)

    def rstd(self, ss, tmpa, tmpb, out, inv_n, bss, btmp, bout):
        self.ts("dve", tmpa, ss, inv_n, EPS, ALU.mult, ALU.add, [bss], [btmp])
        self.act(tmpb, tmpa, AF.Ln, [btmp], [btmp])
        self.act(out, tmpb, AF.Exp, [btmp], [bout], scale=-0.5)


class Tl:
    __slots__ = ("t", "b")

    def __init__(self, t):
        self.t = t
        self.b = B()


def _build(self):
    nc = self.nc
    S, NT = self.S, self.NT
    NKB = NT
    NQ = NT // 4
    din, dout, dscr = self.din, self.dout, self.dscr
    xp = din("xp", [S, D]); xs = din("xs", [32, D])
    cmat_p = din("cmat_p", [128, 8, 128]); cmat_s = din("cmat_s", [128, 8, 32])
    w_ada = din("w_ada", [D, 3 * D]); b_ada = din("b_ada", [3 * D]); norm_g = din("norm_g", [D])
    w_in = din("w_in", [D, 6144]); lams = din("lams", [256]); subln_g = din("subln_g", [128])
    wba_d = din("wba", [512, D]); wbb_d = din("wbb", [512, D]); wo_d = din("w_out", [D, D]); final_g = din("final_g", [D])
    cak = din("cak", [2, PAST, 512]); cav = din("cav", [2, PAST, 512])
    cbk = din("cbk", [2, PAST, 512]); cbv = din("cbv", [2, PAST, 512])
    cos_p = din("cos_p", [S, 256]); sin_p = din("sin_p", [S, 256])
    cos_s = din("cos_s", [32, 256]); sin_s = din("sin_s", [32, 256])
    ident_d = din("ident", [128, 128], BF16); triP_d = din("triP", [128, 128], BF16)
    id32_d = din("ident32", [128, 128]); mask_d = din("mask_lt", [128, 128])
    y_p = dout("y_p", [S, D]); y_s = dout("y_s", [32, D])
    pak = dout("pak", [S, 512]); pav = dout("pav", [S, 512]); pbk = dout("pbk", [S, 512]); pbv = dout("pbv", [S, 512])
    sak = dout("sak", [32, 512]); sav = dout("sav", [32, 512]); sbk = dout("sbk", [32, 512]); sbv = dout("sbv", [32, 512])
    qkT_d = dscr("qkT_d", [16, 128, S], BF16); v_d = dscr("v_d", [S, 1024], BF16)
    oa_d = dscr("oa_d", [S, 512], F32); obT_d = dscr("obT_d", [4, 128, S], F32)
    qkT_sd = dscr("qkT_sd", [16, 128, 32], BF16); v_sd = dscr("v_sd", [32, 1024], BF16)
    oa_sd = dscr("oa_sd", [32, 512], F32); obT_sd = dscr("obT_sd", [4, 128, 32], F32)

    mm, tr, act, tt, ts, stt, cp, rcp, mset, dma = (self.mm, self.tr, self.act, self.tt, self.ts, self.stt,
                                                    self.cp, self.rcp, self.mset, self.dma)

    with contextlib.ExitStack() as top:
        self.sems = {e: top.enter_context(nc.semaphore("s_" + e)) for e in ENGS if e != "sp"}
        self.sems["dma"] = [top.enter_context(nc.semaphore("s_dma%d" % i)) for i in range(ND_SEMS)]

        def T(es, name, shape, dt):
            return Tl(self.sb(es, name, shape, dt))

        def PT(es, name, shape, dt):
            return Tl(self.ps(es, name, shape, dt))

        Abc = [T(top, "Abc%d" % i, [128, D], F32) for i in range(2)]
        shbc = [T(top, "shbc%d" % i, [128, D], F32) for i in range(2)]
        gtbc = [T(top, "gtbc%d" % i, [128, D], F32) for i in range(2)]
        fgbc = T(top, "fgbc", [128, D], F32)
        gsub = T(top, "gsub", [128, 128], F32)
        nlam = T(top, "nlam", [128, 1], F32)
        identT = T(top, "ident", [128, 128], BF16); self.ident = identT.t
        id32T = T(top, "ident32", [128, 128], F32); self.ident32 = id32T.t
        triP = T(top, "triP", [128, 128], BF16); onesT = T(top, "onesT", [128, 128], mybir.dt.float32r); ones32 = T(top, "ones32", [128, 128], F32)
        zf = T(top, "zf", [128, 2, 512], F32); self.zf = zf
        maskT = T(top, "mask", [128, 128], F32)
        zer = T(top, "zer", [128, 512], BF16)
        bconst = B()

        with contextlib.ExitStack() as es01:
            wq = T(es01, "wq", [128, 8, 3072], BF16)
            with contextlib.ExitStack() as es0:
                self.new_prog()
                wst = [T(es0, "wst%d" % i, [128, 4, 512], F32) for i in range(2)]
                bada = T(es0, "bada", [128, 3 * D], F32)
                ngbc = T(es0, "ngbc", [128, D], F32)
                modt = [T(es0, "mod%d" % i, [128, 3 * D], F32) for i in range(2)]
                cm = [T(es0, "cm0", [128, 8, 128], F32), T(es0, "cm1", [128, 8, 32], F32)]
                lamv = T(es0, "lamv", [128, 256], F32)
                sm0 = T(es0, "sm0", [128, 8], F32)
                j64 = T(es0, "j64", [128, 64], F32)
                graw = T(es0, "graw", [128, 128], F32)
                MOD = [PT(es0, "MOD%d" % i, [128, 512], F32) for i in range(2)]

                dma(identT.t[:], ident_d, [], [bconst]); dma(triP.t[:], triP_d, [], [bconst]); dma(id32T.t[:], id32_d, [], [bconst])
                mset("pool", ones32.t[:], 1.0, [ones32.b]); cp("dve", onesT.t[:], ones32.t[:], [ones32.b], [bconst])
                mset("pool", zf.t[:], 0.0, [bconst]); dma(maskT.t[:], mask_d, [], [bconst])
                mset("pool", zer.t[:], 0.0, [zer.b])
                dma(bada.t[:], b_ada.partition_broadcast(128), [], [bada.b])
                dma(ngbc.t[:], norm_g.partition_broadcast(128), [], [ngbc.b])
                dma(fgbc.t[:], final_g.partition_broadcast(128), [], [fgbc.b])
                dma(graw.t[:], subln_g.partition_broadcast(128), [], [graw.b])
                dma(lamv.t[:], lams.partition_broadcast(128), [], [lamv.b])
                dma(cm[0].t[:], cmat_p, [], [cm[0].b]); dma(cm[1].t[:], cmat_s, [], [cm[1].b])
                stt("dve", j64.t[:], lamv.t[:, 0:64], 1.0, lamv.t[:, 64:128], ALU.mult, ALU.mult, [lamv.b], [j64.b, sm0.b], accum=sm0.t[:, 0:1])
                stt("dve", j64.t[:], lamv.t[:, 128:192], 1.0, lamv.t[:, 192:256], ALU.mult, ALU.mult, [lamv.b, sm0.b], [j64.b, sm0.b], accum=sm0.t[:, 1:2])
                act(sm0.t[:, 2:4], sm0.t[:, 0:2], AF.Exp, [sm0.b], [sm0.b])
                tt("dve", sm0.t[:, 4:5], sm0.t[:, 2:3], sm0.t[:, 3:4], ALU.subtract, [sm0.b], [sm0.b])
                ts("dve", nlam.t[:], sm0.t[:, 4:5], LAM_INIT, -1.0, ALU.add, ALU.mult, [sm0.b], [nlam.b])
                ts("dve", gsub.t[:], graw.t[:], 1.0 - LAM_INIT, None, ALU.mult, None, [graw.b], [gsub.b])
                wa_v = w_ada.rearrange("(j p) n -> p j n", p=128)
                k = 0
                for g in range(6):
                    for half in range(2):
                        w_ = wst[k % 2]; k += 1
                        dma(w_.t[:], wa_v[:, 4 * half:4 * half + 4, g * 512:(g + 1) * 512], [], [w_.b])
                        for jj in range(4):
                            j = 4 * half + jj
                            mm(MOD[0].t[:, :], cm[0].t[:, j, :], w_.t[:, jj, :], j == 0, j == 7, [cm[0].b, w_.b], [MOD[0].b])
                            mm(MOD[1].t[0:32, :], cm[1].t[:, j, :], w_.t[:, jj, :], j == 0, j == 7, [cm[1].b, w_.b], [MOD[1].b])
                    cs = slice(g * 512, (g + 1) * 512)
                    tt("dve", modt[0].t[:, cs], MOD[0].t[:, :], bada.t[:, cs], ALU.add, [MOD[0].b, bada.b], [modt[0].b])
                    tt("dve", modt[1].t[0:32, cs], MOD[1].t[0:32, :], bada.t[0:32, cs], ALU.add, [MOD[1].b, bada.b], [modt[1].b])
                for i, P in ((0, 128), (1, 32)):
                    stt("dve", Abc[i].t[0:P, :], modt[i].t[0:P, D:2 * D], 1.0, ngbc.t[0:P, :], ALU.add, ALU.mult, [modt[i].b, ngbc.b], [Abc[i].b])
                    cp("pool", shbc[i].t[0:P, :], modt[i].t[0:P, 0:D], [modt[i].b], [shbc[i].b])
                    cp("pool", gtbc[i].t[0:P, :], modt[i].t[0:P, 2 * D:3 * D], [modt[i].b], [gtbc[i].b])
                wi_v = w_in.rearrange("(j p) n -> p j n", p=128)
                qkv_cols = [0, 512, 1024, 2048, 2560, 3072]
                for g in range(6):
                    for half in range(2):
                        w_ = wst[k % 2]; k += 1
                        dma(w_.t[:], wi_v[:, 4 * half:4 * half + 4, qkv_cols[g]:qkv_cols[g] + 512], [], [w_.b])
                        cp("dve" if k % 2 else "pool", wq.t[:, 4 * half:4 * half + 4, g * 512:(g + 1) * 512], w_.t[:], [w_.b], [wq.b])
                self.pg.finish()

            with contextlib.ExitStack() as es1:
                self.new_prog()
                L = self.alloc_norm(es1, T, PT)
                stage = [T(es1, "stage%d" % i, [128, 2048], F32) for i in range(2)]
                qkb = [T(es1, "qkb%d" % i, [128, 2048], BF16) for i in range(2)]
                vbf = [T(es1, "vbf%d" % i, [128, 1024], BF16) for i in range(2)]
                qkTs = [T(es1, "qkTs%d" % i, [128, 16, 128], BF16) for i in range(2)]
                cst = [T(es1, "cst%d" % i, [128, 256], F32) for i in range(2)]
                snt = [T(es1, "snt%d" % i, [128, 256], F32) for i in range(2)]
                rp = [T(es1, "rp%d" % i, [128, 512], F32) for i in range(2)]
                rt = [T(es1, "rt%d" % i, [128, 256], F32) for i in range(4)]
                PG = [PT(es1, "PG%d" % i, [128, 512], F32) for i in range(4)]
                TQ = PT(es1, "TQ", [128, 16, 128], BF16)

                tiles = []
                for t in range(NT):
                    r0 = t * 128
                    tiles.append(dict(P=128, x=xp[r0:r0 + 128, :], mod=0, cos=cos_p[r0:r0 + 128, :], sin=sin_p[r0:r0 + 128, :],
                                      outs=[pak[r0:r0 + 128, :], pav[r0:r0 + 128, :], pbk[r0:r0 + 128, :], pbv[r0:r0 + 128, :]],
                                      qkT=qkT_d[:, :, r0:r0 + 128], v=v_d[r0:r0 + 128, :]))
                if self.with_sample:
                    tiles.append(dict(P=32, x=xs, mod=1, cos=cos_s, sin=sin_s, outs=[sak, sav, sbk, sbv],
                                      qkT=qkT_sd, v=v_sd))
                n = len(tiles)
                pgi = [0]

                def load(i):
                    tl = tiles[i]; P = tl["P"]; s = i % 2
                    dma(L["xt"][s].t[0:P, :], tl["x"], [], [L["xt"][s].b])
                    dma(cst[s].t[0:P, :], tl["cos"], [], [cst[s].b])
                    dma(snt[s].t[0:P, :], tl["sin"], [], [snt[s].b])

                def pe_main(i):
                    tl = tiles[i]; P = tl["P"]; s = i % 2
                    self.pe_hT(L, s, P)
                    hT = L["hT"][s]
                    grp = []
                    for g in range(6):
                        pgt = PG[pgi[0] % 4]; pgi[0] += 1
                        for j in range(8):
                            mm(pgt.t[0:P, :], hT.t[:, j, 0:P], wq.t[:, j, g * 512:(g + 1) * 512], j == 0, j == 7, [hT.b, wq.b], [pgt.b])
                        grp.append(pgt)
                        self.p1_evac(g, pgt, P, s, stage[s], qkb[s], vbf[s], rp, rt, cst[s], snt[s])
                    return grp

                def tq(i):
                    tl = tiles[i]; P = tl["P"]; s = i % 2
                    for u in range(16):
                        tr(TQ.t[:, u, 0:P], qkb[s].t[0:P, u * 128:(u + 1) * 128], P, [qkb[s].b, bconst], [TQ.b])
                    cp("dve", qkTs[s].t[:, :, 0:P], TQ.t[:, :, 0:P], [TQ.b], [qkTs[s].b])
                    dma(tl["qkT"].rearrange("u p t -> p u t"), qkTs[s].t[:, :, 0:P], [qkTs[s].b], [], eng="pool")
                    for q in range(4):
                        dma(tl["outs"][q], stage[s].t[0:P, q * 512:(q + 1) * 512], [stage[s].b], [], eng="pool")
                    dma(tl["v"], vbf[s].t[0:P, :], [vbf[s].b], [], eng="pool")

                load(0)
                if n > 1:
                    load(1)
                self.norm(L, 0, tiles[0]["P"], Abc[tiles[0]["mod"]], shbc[tiles[0]["mod"]])
                for i in range(n):
                    if i + 1 < n:
                        self.norm(L, (i + 1) % 2, tiles[i + 1]["P"], Abc[tiles[i + 1]["mod"]], shbc[tiles[i + 1]["mod"]])
                    pe_main(i)
                    if i >= 1:
                        tq(i - 1)
                    if i + 2 < n:
                        load(i + 2)
                tq(n - 1)
                self.pg.finish()

        with contextlib.ExitStack() as es2:
            self.new_prog()
            A = self.alloc_attn(es2, T, PT, NKB)
            v_v = v_d.rearrange("(kb p) n -> p kb n", p=128)
            pendA = None; pendB = None
            for u in range(8):
                s = u % 2
                isA = u < 4
                KT, V = A["KT"][s], A["V"][s]
                dma(KT.t[:, 0:S], qkT_d[(4 + u) if isA else (12 + u - 4)], [], [KT.b])
                col0 = u * 128 if isA else 512 + (u - 4) * 128
                for k0 in range(0, NKB, 16):
                    k1 = min(NKB, k0 + 16)
                    dma(V.t[:, k0:k1, 0:128], v_v[:, k0:k1, col0:col0 + 128], [], [V.b])
                if u < 2:
                    mset("pool", V.t[:, :, 128:129], 1.0, [V.b])
                for Tq in range(NQ):
                    qs = A["QT"][A["qi"] % 2]; A["qi"] += 1
                    dma(qs.t[:, :], qkT_d[u if isA else (8 + u - 4)][:, Tq * 512:(Tq + 1) * 512], [], [qs.b])
                    blocks = []
                    for kb in range(4 * Tq + 4):
                        j = kb - 4 * Tq
                        blocks.append((kb, 128, 128 * j if j > 0 else 0, j >= 0))
                    if isA:
                        chunks = [(m, 128 * m, 128, 4 * Tq + m) for m in range(4)]
                        ost = A["ost"][A["oi"] % 2]; A["oi"] += 1
                        dst = oa_d.rearrange("(t m p) n -> t p m n", m=4, p=128)[Tq][:, :, u * 128:(u + 1) * 128]
                        pendA = self.a_tile(A, lambda c, kb, nk, KT=KT: KT.t[64 * c:64 * c + 64, kb * 128:(kb + 1) * 128], KT.b,
                                            lambda kb, nk, V=V: V.t[0:nk, kb, 0:129], V.b,
                                            blocks, 512, qs, chunks, ost, gsub, nlam, zer, bconst, prev=pendA,
                                            store=lambda dst=dst, ost=ost: dma(dst, ost.t[:, :, :], [ost.b], [], eng="pool"))
                    else:
                        if pendA is not None:
                            pendA(); pendA = None
                        obs = A["obs"][A["oi"] % 2]; A["oi"] += 1
                        dstb = obT_d[u - 4][:, Tq * 512:(Tq + 1) * 512]
                        pendB = self.b_tile(A, lambda h2, kb, nk, KT=KT: KT.t[64 * h2:64 * h2 + 64, kb * 128:(kb + 1) * 128], KT.b,
                                            lambda h2, kb, nk, V=V: V.t[0:nk, kb, 64 * h2:64 * h2 + 64], V.b,
                                            blocks[::-1], 512, qs, obs, triP, onesT, maskT, bconst, prev=pendB,
                                            store=lambda dstb=dstb, obs=obs: dma(dstb, obs.t[:, :], [obs.b], [], eng="pool"))
            if pendB:
                for f in pendB:
                    f()
            self.pg.finish()

        if self.with_sample:
            with contextlib.ExitStack() as es2s:
                self.new_prog()
                self.sample_attn(es2s, T, PT, cak, cav, cbk, cbv, qkT_sd, v_sd, oa_sd, obT_sd,
                                 gsub, nlam, zer, triP, onesT, maskT, bconst)
                self.pg.finish()

        with contextlib.ExitStack() as es3:
            self.new_prog()
            self.phase3(es3, T, PT, w_in, wba_d, wbb_d, wo_d, xp, xs, oa_d, obT_d, oa_sd, obT_sd, y_p, y_s,
                        Abc, shbc, gtbc, fgbc, bconst)
            self.pg.finish()
    return nc


KB.build = _build


class Tv:
    __slots__ = ("t", "b")

    def __init__(self, ap, b):
        self.t = ap
        self.b = b


def _alloc_norm(self, es, T, PT):
    L = dict(
        xt=[T(es, "xt%d" % i, [128, D], F32) for i in range(2)],
        tmp=[T(es, "tmp%d" % i, [128, D], F32) for i in range(2)],
        hb=[T(es, "hb%d" % i, [128, D], BF16) for i in range(2)],
        hT=[T(es, "hT%d" % i, [128, 8, 128], BF16) for i in range(2)],
        sqj=T(es, "sqj", [128, D], BF16),
        smn=[T(es, "smn%d" % i, [128, 8], F32) for i in range(4)],
        TP=PT(es, "TP", [128, 8, 128], BF16),
        ni=0,
    )
    return L


def _norm(self, L, s, P, Abc, shbc):
    xt = L["xt"][s]; tmp = L["tmp"][s]; hb = L["hb"][s]
    sm = L["smn"][L["ni"] % 4]; L["ni"] += 1
    sqj = L["sqj"]
    self.act(sqj.t[0:P, :], xt.t[0:P, :], AF.Square, [xt.b], [sqj.b, sm.b], accum=sm.t[0:P, 0:1])
    self.rstd(sm.t[0:P, 0:1], sm.t[0:P, 1:2], sm.t[0:P, 2:3], sm.t[0:P, 3:4], 1.0 / D, sm.b, sm.b, sm.b)
    self.stt("dve", tmp.t[0:P, :], xt.t[0:P, :], sm.t[0:P, 3:4], Abc.t[0:P, :], ALU.mult, ALU.mult,
             [xt.b, sm.b, Abc.b], [tmp.b])
    self.tt("dve", hb.t[0:P, :], tmp.t[0:P, :], shbc.t[0:P, :], ALU.add, [tmp.b, shbc.b], [hb.b])


def _pe_hT(self, L, s, P):
    hb = L["hb"][s]; hT = L["hT"][s]; TP = L["TP"]
    for j in range(8):
        self.tr(TP.t[:, j, 0:P], hb.t[0:P, j * 128:(j + 1) * 128], P, [hb.b], [TP.b])
    self.cp("act", hT.t[:, :, 0:P], TP.t[:, :, 0:P], [TP.b], [hT.b])


def _rope(self, src, P, dst_ap, bdst, cst, snt, rt):
    pat = "p (g two f) -> p g two f"
    sv = src.t[0:P, :].rearrange(pat, two=2, f=32)
    dv = dst_ap.rearrange(pat, two=2, f=32)
    x1, x2 = sv[:, :, 0, :], sv[:, :, 1, :]
    cv = cst.t[0:P, :].rearrange("p (g f) -> p g f", f=32)
    sn = snt.t[0:P, :].rearrange("p (g f) -> p g f", f=32)
    t = [r_.t[0:P, :].rearrange("p (g f) -> p g f", f=32) for r_ in rt]
    tt = self.tt
    tt("dve", t[0], x1, cv, ALU.mult, [src.b, cst.b], [rt[0].b])
    tt("pool", t[1], x2, sn, ALU.mult, [src.b, snt.b], [rt[1].b])
    tt("dve", dv[:, :, 0, :], t[0], t[1], ALU.subtract, [rt[0].b, rt[1].b], [bdst])
    tt("pool", t[2], x2, cv, ALU.mult, [src.b, cst.b], [rt[2].b])
    tt("dve", t[3], x1, sn, ALU.mult, [src.b, snt.b], [rt[3].b])
    tt("pool", dv[:, :, 1, :], t[2], t[3], ALU.add, [rt[2].b, rt[3].b], [bdst])


def _p1_evac(self, g, pgt, P, s, stage, qkb, vbf, rp, rt, cst, snt):
    cp = self.cp
    if g == 0:
        cp("act", rp[0].t[0:P, :], pgt.t[0:P, :], [pgt.b], [rp[0].b])
        self.rope(rp[0], P, qkb.t[0:P, 0:512], qkb.b, cst, snt, rt)
    elif g == 1:
        cp("act", rp[1].t[0:P, :], pgt.t[0:P, :], [pgt.b], [rp[1].b])
        self.rope(rp[1], P, stage.t[0:P, 0:512], stage.b, cst, snt, rt)
        cp("dve", qkb.t[0:P, 512:1024], stage.t[0:P, 0:512], [stage.b], [qkb.b])
    elif g == 2:
        cp("act", stage.t[0:P, 512:1024], pgt.t[0:P, :], [pgt.b], [stage.b])
        cp("dve", vbf.t[0:P, 0:512], stage.t[0:P, 512:1024], [stage.b], [vbf.b])
    elif g == 3:
        cp("act", qkb.t[0:P, 1024:1536], pgt.t[0:P, :], [pgt.b], [qkb.b])
    elif g == 4:
        cp("act", stage.t[0:P, 1024:1536], pgt.t[0:P, :], [pgt.b], [stage.b])
        cp("dve", qkb.t[0:P, 1536:2048], stage.t[0:P, 1024:1536], [stage.b], [qkb.b])
    else:
        cp("act", stage.t[0:P, 1536:2048], pgt.t[0:P, :], [pgt.b], [stage.b])
        cp("dve", vbf.t[0:P, 512:1024], stage.t[0:P, 1536:2048], [stage.b], [vbf.b])


def _alloc_attn(self, es, T, PT, NKB, G=8):
    A = {}
    if NKB:
        A["KT"] = [T(es, "KT%d" % i, [128, NKB * 128], BF16) for i in range(2)]
        A["V"] = [T(es, "V%d" % i, [128, NKB, 130], BF16) for i in range(2)]
    A["zf"] = self.zf
    A["G"] = G
    A["QT"] = [T(es, "QT%d" % i, [128, 512], BF16) for i in range(2)]
    A["S"] = [PT(es, "S%d" % i, [128, 2, 512], F32) for i in range(2)]
    A["Sb"] = [[B(), B()], [B(), B()]]
    A["X"] = PT(es, "X", [128, 3, 512], F32)
    A["bX"] = [B(), B(), B()]
    A["E"] = [T(es, "E%d" % i, [128, 2, 512], BF16) for i in range(3)]
    A["spg"] = [T(es, "spg%d" % i, [128, 2, 512], BF16) for i in range(A["G"])]
    A["LsF"] = [T(es, "LsF%d" % i, [128, 2, 512], mybir.dt.float32r) for i in range(A["G"] + 1)]
    A["nq"] = [T(es, "nq%d" % i, [128, 512], BF16) for i in range(2)]
    A["a"] = [T(es, "a%d" % i, [128, 2, 512], BF16) for i in range(A["G"] + 3)]

    A["ost"] = [T(es, "ost%d" % i, [128, 4, 128], F32) for i in range(2)]
    A["obs"] = [T(es, "obs%d" % i, [128, 512], F32) for i in range(2)]
    A["oo"] = [T(es, "oo%d" % i, [128, 4, 128], F32) for i in range(2)]
    A["t0"] = [T(es, "t0%d" % i, [128, 128], F32) for i in range(2)]
    A["sma"] = [T(es, "sma%d" % i, [128, 16], F32) for i in range(4)]
    A["smb"] = [T(es, "smb%d" % i, [128, 12], F32) for i in range(4)]
    A["jk"] = T(es, "jk", [128, 128], F32)
    for k in ("si", "ei", "wi", "qi", "oi", "ti", "fi", "nqi", "zi"):
        A[k] = 0
    return A


def _a_tile(self, A, kt, bkt, vv, bv, blocks, N, qs, chunks, ost, gsub, nlam, zer, bconst, prev=None, store=None):
    mm, act, tt, ts, stt, rcp, mset = self.mm, self.act, self.tt, self.ts, self.stt, self.rcp, self.mset
    nb = len(blocks)
    X, bX = A["X"], A["bX"]
    sbase, ebase = A["si"], A["ei"]
    A["si"] += nb; A["ei"] += nb
    ti = A["ti"]; A["ti"] += 1
    sm = A["sma"][ti % 4]; sm2 = A["smb"][ti % 4]; oo = A["oo"][ti % 2]; jk = A["jk"]
    qn0 = chunks[0][2]
    nm = len(chunks)

    def acc(m, c):
        a = m * 2 + c
        return X.t[:, a // 3, (a % 3) * 130:(a % 3) * 130 + 129], bX[a // 3]

    def qk(i):
        kbi, nk, c0, diag = blocks[i]
        Sl = A["S"][(sbase + i) % 2]; Sb = A["Sb"][(sbase + i) % 2]
        for c in range(2):
            mm(Sl.t[0:nk, c, c0:N], kt(c, kbi, nk), qs.t[64 * c:64 * c + 64, c0:N], True, True, [bkt, qs.b], [Sb[c]])

    def ex(i):
        kbi, nk, c0, diag = blocks[i]
        Sl = A["S"][(sbase + i) % 2]; El = A["E"][(ebase + i) % 3]; Sb = A["Sb"][(sbase + i) % 2]
        act(El.t[0:nk, :, c0:N], Sl.t[0:nk, :, c0:N], AF.Exp, Sb, [El.b], scale=0.125)
        if diag:
            mset("pool", El.t[64:128, :, c0:c0 + 64], 0.0, [El.b])

    def pv(i):
        kbi, nk, c0, diag = blocks[i]
        El = A["E"][(ebase + i) % 3]
        for c in range(2):
            for (m, q0, qn, last) in chunks:
                if q0 < c0:
                    continue
                o_ap, ob = acc(m, c)
                mm(o_ap[0:qn, :], El.t[0:nk, c, q0:q0 + qn], vv(kbi, nk), False, i == last, [El.b, bv], [ob], skip=True)

    def fin_chunk(m, q0, qn):
        (a0, b0), (a1, b1) = acc(m, 0), acc(m, 1)
        t0 = A["t0"][A["fi"] % 2]; A["fi"] += 1
        rcp(sm.t[0:qn, m:m + 1], a0[0:qn, 128:129], [b0], [sm.b])
        rcp(sm.t[0:qn, 4 + m:5 + m], a1[0:qn, 128:129], [b1], [sm.b])
        tt("dve", sm.t[0:qn, 8 + m:9 + m], sm.t[0:qn, 4 + m:5 + m], nlam.t[0:qn, :], ALU.mult, [sm.b], [sm.b])
        ts("dve", t0.t[0:qn, :], a0[0:qn, 0:128], sm.t[0:qn, m:m + 1], None, ALU.mult, None, [b0, sm.b], [t0.b])
        stt("dve", oo.t[0:qn, m, :], a1[0:qn, 0:128], sm.t[0:qn, 8 + m:9 + m], t0.t[0:qn, :], ALU.mult, ALU.add,
            [b1, sm.b, t0.b], [oo.b])
        stt("dve", jk.t[0:qn, :], oo.t[0:qn, m, :], 1.0, oo.t[0:qn, m, :], ALU.mult, ALU.mult, [oo.b], [jk.b, sm.b],
            accum=sm.t[0:qn, 12 + m:13 + m])

    qk(0)
    for b in range(3):
        mm(X.t[:, b, :], zer.t[:, 0:128], zer.t[:, :], True, False, [], [bX[b]], skip=True)
    for i in range(nb):
        if i + 1 < nb:
            qk(i + 1)
        ex(i)
        pv(i)
        if prev is not None and i == min(1, nb - 1):
            prev()
        for (m, q0, qn, last) in chunks:
            if last == i:
                fin_chunk(m, q0, qn)

    def finish():
        ts("dve", sm2.t[0:qn0, 0:nm], sm.t[0:qn0, 12:12 + nm], 1.0 / 128, EPS, ALU.mult, ALU.add, [sm.b], [sm2.b])
        act(sm2.t[0:qn0, 4:4 + nm], sm2.t[0:qn0, 0:nm], AF.Ln, [sm2.b], [sm2.b])
        act(sm2.t[0:qn0, 8:8 + nm], sm2.t[0:qn0, 4:4 + nm], AF.Exp, [sm2.b], [sm2.b], scale=-0.5)
        for (m, q0, qn, last) in chunks:
            stt("dve", ost.t[0:qn, m, :], oo.t[0:qn, m, :], sm2.t[0:qn, 8 + m:9 + m], gsub.t[0:qn, :], ALU.mult, ALU.mult,
                [oo.b, sm2.b], [ost.b])
        if store is not None:
            store()
    return finish


def _b_tile(self, A, kt, bkt, vv, bv, blocks, N, qs, obs, triP, ones, maskT, bconst, prev=None, store=None):
    mm, act, tt, cp = self.mm, self.act, self.tt, self.cp
    nb = len(blocks)
    G = A["G"]
    X, bX = A["X"], A["bX"]
    S0, S1 = A["S"]; Sb = A["Sb"]
    zslots = [(X.t[:, 0, :], bX[0]), (X.t[:, 1, :], bX[1]), (S0.t[:, 0, :], Sb[0][0]), (S0.t[:, 1, :], Sb[0][1]),
              (S1.t[:, 0, :], Sb[1][0]), (S1.t[:, 1, :], Sb[1][1])]
    NZ = len(zslots)
    nq = A["nq"][A["nqi"] % 2]; A["nqi"] += 1
    LsF = A["LsF"]; R = len(LsF); zf = A["zf"]
    NA = len(A["a"])
    abase = A["wi"]; A["wi"] += nb
    cbase = A["si"]; A["si"] += nb
    zbase = A["zi"]; A["zi"] += 2 * nb
    self.ts("pool", nq.t[:, 0:N], qs.t[:, 0:N], -0.125, None, ALU.mult, None, [qs.b], [nq.b])

    def zs(i, h2):
        return zslots[(zbase + 2 * i + h2) % NZ]

    ctiles = [(S0.t, Sb[0]), (S1.t, Sb[1]), (X.t[:, 0:2, :], [bX[0], bX[1]])]

    def cb(i):
        return ctiles[(cbase + i) % 3]

    def qk(i, h2):
        kbi, nk, c0, diag = blocks[i]
        zt, zb_ = zs(i, h2)
        mm(zt[0:nk, c0:N], kt(h2, kbi, nk), qs.t[64 * h2:64 * h2 + 64, c0:N], True, True, [bkt, qs.b], [zb_])

    def spl(i):
        kbi, nk, c0, diag = blocks[i]
        sp = A["spg"][i % G]
        for h2 in range(2):
            zt, zb_ = zs(i, h2)
            act(sp.t[0:nk, h2, c0:N], zt[0:nk, c0:N], AF.Softplus, [zb_], [sp.b], scale=0.125)
        if diag:
            mw = min(nk, N - c0)
            for h2 in range(2):
                tt("pool", sp.t[0:nk, h2, c0:c0 + mw], sp.t[0:nk, h2, c0:c0 + mw], maskT.t[0:nk, 0:mw], ALU.mult, [sp.b], [sp.b])
        if i + 1 < nb:
            cur = LsF[i % R]; nxt = LsF[(i + 1) % R]
            full = (nk == 128 and c0 == 0)
            if i == 0:
                if not full:
                    cp("dve", nxt.t[:, :, 0:N], zf.t[:, :, 0:N], [], [nxt.b])
                cp("dve", nxt.t[0:nk, :, c0:N], sp.t[0:nk, :, c0:N], [sp.b], [nxt.b])
            else:
                assert nk == 128
                if c0 > 0:
                    cp("dve", nxt.t[:, :, 0:c0], zf.t[:, :, 0:c0], [], [nxt.b])
                tt("dve", nxt.t[:, :, c0:N], cur.t[:, :, c0:N], sp.t[:, :, c0:N], ALU.add, [cur.b, sp.b], [nxt.b])

    def cmm(i):
        kbi, nk, c0, diag = blocks[i]
        (Ct, Cb) = cb(i); sp = A["spg"][i % G]; cur = LsF[i % R]
        for h2 in range(2):
            mm(Ct[0:nk, h2, c0:N], triP.t[0:nk, 0:nk], sp.t[0:nk, h2, c0:N], True, False, [sp.b], Cb)
            if i > 0:
                mm(Ct[:, h2, c0:N], ones.t[:, :], cur.t[:, h2, c0:N], False, False, [cur.b], Cb)
        for h2 in range(2):
            mm(Ct[0:nk, h2, c0:N], kt(h2, kbi, nk), nq.t[64 * h2:64 * h2 + 64, c0:N], False, True, [bkt, nq.b], Cb)

    def ex(i):
        kbi, nk, c0, diag = blocks[i]
        (Ct, Cb) = cb(i); a = A["a"][(abase + i) % NA]
        act(a.t[0:nk, :, c0:N], Ct[0:nk, :, c0:N], AF.Exp, Cb, [a.b], scale=-1.0)
        if diag:
            mw = min(nk, N - c0)
            for h2 in range(2):
                tt("pool", a.t[0:nk, h2, c0:c0 + mw], a.t[0:nk, h2, c0:c0 + mw], maskT.t[0:nk, 0:mw], ALU.mult, [a.b], [a.b])

    def pv(i):
        kbi, nk, c0, diag = blocks[i]
        a = A["a"][(abase + i) % NA]
        for h2 in range(2):
            mm(X.t[64 * h2:64 * h2 + 64, 2, c0:N], vv(h2, kbi, nk), a.t[0:nk, h2, c0:N], i == 0, i == nb - 1,
               [a.b, bv], [bX[2]], skip=True)

    groups = [(g0, min(nb, g0 + G)) for g0 in range(0, nb, G)]
    for gi, (g0, g1) in enumerate(groups):
        pg_ = groups[gi - 1] if gi > 0 else None
        if pg_:
            pend = [(lambda i=i: pv(i)) for i in range(pg_[0], pg_[1])]
        else:
            pend = list(prev) if prev else []
        zq = [(i, h2) for i in range(g0, g1) for h2 in range(2)]
        for k in range(min(NZ, len(zq))):
            qk(*zq[k])
        zn = NZ
        for i in range(g0, g1):
            spl(i)
            for _ in range(2):
                if zn < len(zq):
                    qk(*zq[zn]); zn += 1
            if pend:
                pend.pop(0)()
        while pend:
            pend.pop(0)()
        ahead = min(3, g1 - g0)
        for i in range(g0, g0 + ahead):
            cmm(i)
        for i in range(g0, g1):
            ex(i)
            if i + ahead < g1:
                cmm(i + ahead)
    tail = [(lambda i=i: pv(i)) for i in range(groups[-1][0], groups[-1][1])]

    def evac():
        cp("dve", obs.t[:, 0:N], X.t[:, 2, 0:N], [bX[2]], [obs.b])
        if store is not None:
            store()
    tail.append(evac)
    return tail


KB.alloc_norm = _alloc_norm
KB.norm = _norm
KB.pe_hT = _pe_hT
KB.rope = _rope
KB.p1_evac = _p1_evac
KB.alloc_attn = _alloc_attn
KB.a_tile = _a_tile
KB.b_tile = _b_tile


def _sample_attn(self, es, T, PT, cak, cav, cbk, cbv, qkT_sd, v_sd, oa_sd, obT_sd,
                 gsub, nlam, zer, triP, ones, maskT, bconst):
    mm, tr, act, tt, cp, mset, dma = self.mm, self.tr, self.act, self.tt, self.cp, self.mset, self.dma
    A = self.alloc_attn(es, T, PT, 0, G=4)
    NK = PAST + NS
    KTa = T(es, "KTa", [128, 4, NK], BF16); KTb = T(es, "KTb", [128, 4, NK], BF16)
    VAs = T(es, "VAs", [128, 17, 4, 130], BF16); VBs = T(es, "VBs", [128, 17, 512], BF16)
    ct = [T(es, "ct%d" % i, [128, 512], F32) for i in range(4)]
    qsa = T(es, "qsa", [128, 4, NS], BF16); qsb = T(es, "qsb", [128, 4, NS], BF16)
    Yv = A["X"].t[:, 2, :].rearrange("p (h k) -> p h k", k=128)
    bY = A["bX"][2]
    mset("pool", VAs.t[:, :, :, 128:129], 1.0, [VAs.b])
    qv = qkT_sd.rearrange("u p t -> p u t")
    ci = 0
    for s in range(2):
        t0, t1 = s * NS, (s + 1) * NS
        for i in range(16):
            r0 = i * 128
            for which, src in enumerate((cak, cbk, cav, cbv)):
                c_ = ct[ci % 4]; ci += 1
                dma(c_.t[:, :], src[s, r0:r0 + 128, :], [], [c_.b])
                if which < 2:
                    for h in range(4):
                        self.tr32(Yv[:, h, :], c_.t[:, h * 128:(h + 1) * 128], 128, [c_.b], [bY])
                    dstT = KTa if which == 0 else KTb
                    cp("act", dstT.t[:, :, r0:r0 + 128], Yv, [bY], [dstT.b])
                elif which == 2:
                    cp("dve", VAs.t[:, i, :, 0:128], c_.t[:, :].rearrange("p (h e) -> p h e", e=128), [c_.b], [VAs.b])
                else:
                    cp("dve", VBs.t[:, i, :], c_.t[:, :], [c_.b], [VBs.b])
        dma(KTa.t[:, :, PAST:NK], qv[:, 4:8, t0:t1], [], [KTa.b])
        dma(KTb.t[:, :, PAST:NK], qv[:, 12:16, t0:t1], [], [KTb.b])
        dma(VAs.t[0:NS, 16, :, 0:128], v_sd[t0:t1, 0:512].rearrange("t (h e) -> t h e", e=128), [], [VAs.b])
        dma(VBs.t[0:NS, 16, :], v_sd[t0:t1, 512:1024], [], [VBs.b])
        dma(qsa.t[:, :, :], qv[:, 0:4, t0:t1], [], [qsa.b])
        dma(qsb.t[:, :, :], qv[:, 8:12, t0:t1], [], [qsb.b])
        blocks = [(kb, 128, 0, False) for kb in range(16)] + [(16, NS, 0, False)]
        for h in range(4):
            ost = A["ost"][A["oi"] % 2]; A["oi"] += 1
            fin = self.a_tile(A, lambda c, kb, nk, h=h: KTa.t[64 * c:64 * c + 64, h, kb * 128:kb * 128 + nk], KTa.b,
                              lambda kb, nk, h=h: VAs.t[0:nk, kb, h, 0:129], VAs.b,
                              blocks, NS, Tv(qsa.t[:, h, :], qsa.b), [(0, 0, NS, 16)], ost, gsub, nlam, zer, bconst)
            fin()
            dma(oa_sd[t0:t1, h * 128:(h + 1) * 128], ost.t[0:NS, 0, :], [ost.b], [], eng="pool")
        blocks_b = [(16, NS, 0, True)] + [(kb, 128, 0, False) for kb in range(15, -1, -1)]
        for p in range(4):
            obs = A["obs"][A["oi"] % 2]; A["oi"] += 1
            tl_ = self.b_tile(A, lambda h2, kb, nk, p=p: KTb.t[64 * h2:64 * h2 + 64, p, kb * 128:kb * 128 + nk], KTb.b,
                              lambda h2, kb, nk, p=p: VBs.t[0:nk, kb, p * 128 + 64 * h2:p * 128 + 64 * h2 + 64], VBs.b,
                              blocks_b, NS, Tv(qsb.t[:, p, :], qsb.b), obs, triP, ones, maskT, bconst)
            for f in tl_:
                f()
            dma(obT_sd[p][:, t0:t1], obs.t[:, 0:NS], [obs.b], [], eng="pool")


def _phase3(self, es, T, PT, w_in, wba_d, wbb_d, wo_d, xp, xs, oa_d, obT_d, oa_sd, obT_sd, y_p, y_s,
            Abc, shbc, gtbc, fgbc, bconst):
    mm, tr, act, tt, stt, cp, dma = self.mm, self.tr, self.act, self.tt, self.stt, self.cp, self.dma
    NT = self.NT
    L = self.alloc_norm(es, T, PT)
    wg = T(es, "wg", [128, 8, 3072], BF16)
    wba = T(es, "wba", [128, 4, D], BF16); wbb = T(es, "wbb", [128, 4, D], BF16)
    wo = T(es, "wo", [128, 8, D], BF16)
    wst = [T(es, "wst3%d" % i, [128, 4, 512], F32) for i in range(2)]
    oat = [T(es, "oat%d" % i, [128, 512], F32) for i in range(2)]
    obt = [T(es, "obt%d" % i, [128, 4, 128], F32) for i in range(2)]
    sza = T(es, "sza", [128, 512], F32); u1 = T(es, "u1", [128, 512], F32); ua = T(es, "ua", [128, 512], BF16)
    szb = T(es, "szb", [128, 4, 128], F32); u2 = T(es, "u2", [128, 4, 128], F32); ubT = T(es, "ubT", [128, 4, 128], BF16)
    sga = T(es, "sga", [128, D], F32); sgb = T(es, "sgb", [128, D], F32)
    uaT = T(es, "uaT", [128, 4, 128], BF16)
    m1 = T(es, "m1", [128, D], F32); m2 = T(es, "m2", [128, D], F32); mb = T(es, "mb", [128, D], BF16)
    mT = T(es, "mT", [128, 8, 128], BF16)
    ys = [T(es, "ys%d" % i, [128, D], F32) for i in range(2)]
    sm3 = [T(es, "sm3%d" % i, [128, 8], F32) for i in range(4)]
    ZA = PT(es, "ZA", [128, 512], F32); ZB = PT(es, "ZB", [128, 4, 128], F32)
    WA = PT(es, "WA", [128, 2, 512], F32); WB = PT(es, "WB", [128, 2, 512], F32)
    TP = L["TP"]
    flat = "p a b -> p (a b)"

    wi_v = w_in.rearrange("(j p) n -> p j n", p=128)
    k = 0
    srcs = [1536, 3584, 4096, 4608, 5120, 5632]
    for g in range(6):
        for half in range(2):
            w_ = wst[k % 2]; k += 1
            dma(w_.t[:], wi_v[:, 4 * half:4 * half + 4, srcs[g]:srcs[g] + 512], [], [w_.b])
            cp("dve" if k % 2 else "pool", wg.t[:, 4 * half:4 * half + 4, g * 512:(g + 1) * 512], w_.t[:], [w_.b], [wg.b])
    for wd, wt in ((wba_d, wba), (wbb_d, wbb)):
        wv = wd.rearrange("(c p) n -> p c n", p=128)
        for nh in range(2):
            w_ = wst[k % 2]; k += 1
            dma(w_.t[:], wv[:, :, nh * 512:(nh + 1) * 512], [], [w_.b])
            cp("dve" if k % 2 else "pool", wt.t[:, :, nh * 512:(nh + 1) * 512], w_.t[:], [w_.b], [wt.b])
    wv = wo_d.rearrange("(j p) n -> p j n", p=128)
    for half in range(2):
        for nh in range(2):
            w_ = wst[k % 2]; k += 1
            dma(w_.t[:], wv[:, 4 * half:4 * half + 4, nh * 512:(nh + 1) * 512], [], [w_.b])
            cp("dve" if k % 2 else "pool", wo.t[:, 4 * half:4 * half + 4, nh * 512:(nh + 1) * 512], w_.t[:], [w_.b], [wo.b])

    tiles = []
    obT_v = obT_d.rearrange("u p t -> p u t")
    for t in range(NT):
        r0 = t * 128
        tiles.append(dict(P=128, x=xp[r0:r0 + 128, :], mod=0, oa=oa_d[r0:r0 + 128, :], ob=obT_v[:, :, r0:r0 + 128],
                          y=y_p[r0:r0 + 128, :]))
    if self.with_sample:
        tiles.append(dict(P=32, x=xs, mod=1, oa=oa_sd, ob=obT_sd.rearrange("u p t -> p u t"), y=y_s))
    n = len(tiles)

    def load(i):
        tl = tiles[i]; P = tl["P"]; s = i % 2
        dma(L["xt"][s].t[0:P, :], tl["x"], [], [L["xt"][s].b])
        dma(oat[s].t[0:P, :], tl["oa"], [], [oat[s].b])
        dma(obt[s].t[:, :, 0:P], tl["ob"], [], [obt[s].b])

    def head(i):
        tl = tiles[i]; P = tl["P"]; s = i % 2
        hT = L["hT"][s]
        self.pe_hT(L, s, P)
        for j in range(8):
            mm(ZA.t[0:P, :], hT.t[:, j, 0:P], wg.t[:, j, 0:512], j == 0, j == 7, [hT.b, wg.b], [ZA.b])
        for fc in range(4):
            for j in range(8):
                mm(ZB.t[:, fc, 0:P], wg.t[:, j, 512 + fc * 128:512 + (fc + 1) * 128], hT.t[:, j, 0:P], j == 0, j == 7,
                   [hT.b, wg.b], [ZB.b])
        for nh in range(2):
            for j in range(8):
                mm(WB.t[0:P, nh, :], hT.t[:, j, 0:P], wg.t[:, j, 2048 + nh * 512:2048 + (nh + 1) * 512], j == 0, j == 7,
                   [hT.b, wg.b], [WB.b])

    def head_b(i):
        tl = tiles[i]; P = tl["P"]; s = i % 2
        hT = L["hT"][s]
        for nh in range(2):
            for j in range(8):
                mm(WA.t[0:P, nh, :], hT.t[:, j, 0:P], wg.t[:, j, 1024 + nh * 512:1024 + (nh + 1) * 512], j == 0, j == 7,
                   [hT.b, wg.b], [WA.b])

    def mid(i):
        tl = tiles[i]; P = tl["P"]; s = i % 2
        act(sza.t[0:P, :], ZA.t[0:P, :], AF.Sigmoid, [ZA.b], [sza.b])
        tt("dve", u1.t[0:P, :], ZA.t[0:P, :], sza.t[0:P, :], ALU.mult, [ZA.b, sza.b], [u1.b])
        tt("dve", ua.t[0:P, :], u1.t[0:P, :], oat[s].t[0:P, :], ALU.mult, [u1.b, oat[s].b], [ua.b])
        act(szb.t[:, :, 0:P], ZB.t[:, :, 0:P], AF.Sigmoid, [ZB.b], [szb.b])
        tt("dve", u2.t[:, :, 0:P], ZB.t[:, :, 0:P], szb.t[:, :, 0:P], ALU.mult, [ZB.b, szb.b], [u2.b])
        tt("dve", ubT.t[:, :, 0:P], u2.t[:, :, 0:P], obt[s].t[:, :, 0:P], ALU.mult, [u2.b, obt[s].b], [ubT.b])
        act(sgb.t[0:P, :], WB.t[0:P, :, :].rearrange(flat), AF.Sigmoid, [WB.b], [sgb.b])
        act(sga.t[0:P, :], WA.t[0:P, :, :].rearrange(flat), AF.Sigmoid, [WA.b], [sga.b])
        for c in range(4):
            tr(TP.t[:, c, 0:P], ua.t[0:P, c * 128:(c + 1) * 128], P, [ua.b], [TP.b])
        cp("act", uaT.t[:, :, 0:P], TP.t[:, 0:4, 0:P], [TP.b], [uaT.b])
        for nh in range(2):
            for c in range(4):
                mm(WB.t[0:P, nh, :], ubT.t[:, c, 0:P], wbb.t[:, c, nh * 512:(nh + 1) * 512], c == 0, c == 3, [ubT.b, wbb.b], [WB.b])
        for nh in range(2):
            for c in range(4):
                mm(WA.t[0:P, nh, :], uaT.t[:, c, 0:P], wba.t[:, c, nh * 512:(nh + 1) * 512], c == 0, c == 3, [uaT.b, wba.b], [WA.b])
        tt("dve", m2.t[0:P, :], WB.t[0:P, :, :].rearrange(flat), sgb.t[0:P, :], ALU.mult, [WB.b, sgb.b], [m2.b])
        tt("dve", m1.t[0:P, :], WA.t[0:P, :, :].rearrange(flat), sga.t[0:P, :], ALU.mult, [WA.b, sga.b], [m1.b])
        tt("dve", mb.t[0:P, :], m1.t[0:P, :], m2.t[0:P, :], ALU.add, [m1.b, m2.b], [mb.b])
        for j in range(8):
            tr(TP.t[:, j, 0:P], mb.t[0:P, j * 128:(j + 1) * 128], P, [mb.b], [TP.b])
        cp("act", mT.t[:, :, 0:P], TP.t[:, :, 0:P], [TP.b], [mT.b])
        for nh in range(2):
            for j in range(8):
                mm(WA.t[0:P, nh, :], mT.t[:, j, 0:P], wo.t[:, j, nh * 512:(nh + 1) * 512], j == 0, j == 7, [mT.b, wo.b], [WA.b])

    def tail_a(i):
        tl = tiles[i]; P = tl["P"]; md = tl["mod"]
        tt("dve", m1.t[0:P, :], WA.t[0:P, :, :].rearrange(flat), gtbc[md].t[0:P, :], ALU.mult, [WA.b, gtbc[md].b], [m1.b])

    def tail(i):
        tl = tiles[i]; P = tl["P"]; s = i % 2; md = tl["mod"]
        xt = L["xt"][s]
        tt("dve", m2.t[0:P, :], m1.t[0:P, :], xt.t[0:P, :], ALU.add, [m1.b, xt.b], [m2.b])
        sm = sm3[i % 4]; sqj = L["sqj"]
        act(sqj.t[0:P, :], m2.t[0:P, :], AF.Square, [m2.b], [sqj.b, sm.b], accum=sm.t[0:P, 0:1])
        self.rstd(sm.t[0:P, 0:1], sm.t[0:P, 1:2], sm.t[0:P, 2:3], sm.t[0:P, 3:4], 1.0 / D, sm.b, sm.b, sm.b)
        stt("dve", ys[s].t[0:P, :], m2.t[0:P, :], sm.t[0:P, 3:4], fgbc.t[0:P, :], ALU.mult, ALU.mult, [m2.b, sm.b, fgbc.b], [ys[s].b])
        dma(tl["y"], ys[s].t[0:P, :], [ys[s].b], [], eng="pool")

    load(0)
    if n > 1:
        load(1)
    self.norm(L, 0, tiles[0]["P"], Abc[tiles[0]["mod"]], shbc[tiles[0]["mod"]])
    head(0); head_b(0)
    for i in range(n):
        if i + 1 < n:
            self.norm(L, (i + 1) % 2, tiles[i + 1]["P"], Abc[tiles[i + 1]["mod"]], shbc[tiles[i + 1]["mod"]])
        mid(i)
        if i + 1 < n:
            head(i + 1)
        tail_a(i)
        if i + 1 < n:
            head_b(i + 1)
        tail(i)
        if i + 2 < n:
            load(i + 2)


KB.sample_attn = _sample_attn
KB.phase3 = _phase3


def _rope_tables(pos):
    half = 32
    inv = (np.float32(10000.0) ** (-np.arange(half, dtype=np.float32) / np.float32(half))).astype(np.float32)
    ang = pos.astype(np.float32)[:, None] * inv[None, :]
    cos = np.cos(ang).astype(np.float32); sin = np.sin(ang).astype(np.float32)
    return np.tile(cos, (1, 8)), np.tile(sin, (1, 8))


_CACHE = {}


def run(inputs, NT, n_cores, with_sample=True, trace=False):
    key = (NT, with_sample)
    if key not in _CACHE:
        kb = KB(NT, with_sample)
        kb.build()
        _CACHE[key] = kb
    kb = _CACHE[key]
    S = NT * 128
    bf = ml_dtypes.bfloat16
    f32 = np.float32
    g = lambda k: np.ascontiguousarray(np.asarray(inputs[k], dtype=f32))
    xP, xS, cP, cS = g("x_prompt"), g("x_sample"), g("c_prompt"), g("c_sample")
    cos_p, sin_p = _rope_tables(np.arange(S))
    pos_s = PAST + np.tile(np.arange(NS), 2)
    cos_s, sin_s = _rope_tables(pos_s)
    idx = np.arange(128)
    consts = dict(
        ident=np.eye(128, dtype=f32).astype(bf),
        triP=(idx[:, None] >= idx[None, :]).astype(f32).astype(bf),
        ident32=np.eye(128, dtype=f32),
        mask_lt=(idx[:, None] < idx[None, :]).astype(f32),
        cos_p=cos_p, sin_p=sin_p, cos_s=cos_s, sin_s=sin_s,
        w_ada=g("w_ada")[0], b_ada=g("b_ada")[0], norm_g=g("norm_g")[0], w_in=g("w_in")[0],
        lams=np.concatenate([g("lambda_q1")[0], g("lambda_k1")[0], g("lambda_q2")[0], g("lambda_k2")[0]]),
        subln_g=g("subln_g")[0], wba=g("w_branch_a")[0], wbb=g("w_branch_b")[0], w_out=g("w_out")[0],
        final_g=g("final_g"),
    )
    cak, cav, cbk, cbv = (g(k)[0].reshape(-1, PAST, 512) for k in ("cache_a_k", "cache_a_v", "cache_b_k", "cache_b_v"))
    in_maps = []
    for i in range(n_cores):
        m = dict(consts)
        m["xp"] = xP[i]
        m["xs"] = np.ascontiguousarray(xS[2 * i:2 * i + 2].reshape(32, D))
        cm = cP[i].reshape(8, 128).T
        m["cmat_p"] = np.ascontiguousarray(np.broadcast_to(cm[:, :, None], (128, 8, 128)))
        cs = cS[2 * i:2 * i + 2].reshape(2, 8, 128).transpose(2, 1, 0)
        m["cmat_s"] = np.ascontiguousarray(np.repeat(cs, NS, axis=2))
        for nm_, arr in (("cak", cak), ("cav", cav), ("cbk", cbk), ("cbv", cbv)):
            m[nm_] = np.ascontiguousarray(arr[2 * i:2 * i + 2])
        in_maps.append(m)
    res = run_bass_kernel_spmd(kb.nc, in_maps, core_ids=list(range(n_cores)), trace=trace)
    return res


def kernel(**inputs):
    NT = 64
    res = run(inputs, NT, 8)
    r = res.results
    S = NT * 128
    cat = lambda k: np.stack([r[i][k] for i in range(8)], 0)
    cats = lambda k: np.concatenate([r[i][k].reshape(2, NS, -1) for i in range(8)], 0)
    y_prompt = cat("y_p")
    y_sample = cats("y_s")
    pak = cat("pak").reshape(1, 8, S, 4, 2, 64)
    pav = cat("pav").reshape(1, 8, S, 4, 128)
    pbk = cat("pbk").reshape(1, 8, S, 8, 64)
    pbv = cat("pbv").reshape(1, 8, S, 8, 64)
    sak = cats("sak").reshape(1, 16, NS, 4, 2, 64)
    sav = cats("sav").reshape(1, 16, NS, 4, 128)
    sbk = cats("sbk").reshape(1, 16, NS, 8, 64)
    sbv = cats("sbv").reshape(1, 16, NS, 8, 64)
    return (y_prompt, y_sample, pak, pav, pbk, pbv, sak, sav, sbk, sbv)
```

```python
import contextlib
import numpy as np
import ml_dtypes
import concourse.bass as bass
import concourse.mybir as mybir
from concourse.bass_utils import run_bass_kernel_spmd

F32 = mybir.dt.float32
BF16 = mybir.dt.bfloat16
AF = mybir.ActivationFunctionType
ALU = mybir.AluOpType

ENGS = ("pe", "act", "dve", "pool", "sp")
ND_SEMS = 24


class B:
    __slots__ = ("name", "last_w", "readers")

    def __init__(self, name=""):
        self.name = name
        self.last_w = None
        self.readers = []


class Prog:
    def __init__(self, nc, sems, state):
        self.nc = nc
        self.sems = sems
        self.st = state
        self.ops = {e: [] for e in ENGS}
        self.touched = set()

    def add(self, eng, fn, reads=(), writes=(), dma=False):
        ops = self.ops[eng]
        idx = len(ops)
        deps = {}
        self.touched.update(reads)
        self.touched.update(writes)
        for b in reads:
            if b.last_w is not None:
                deps[b.last_w] = deps.get(b.last_w, 0) | 1
        for b in writes:
            if b.last_w is not None:
                deps[b.last_w] = deps.get(b.last_w, 0) | 2
            for r in b.readers:
                deps[r] = deps.get(r, 0) | 4
        if dma:
            did = self.st["dma_id"]
            self.st["dma_id"] += 1
            ev = ("dma", did)
        else:
            did = None
            ev = (eng, idx)
        deps.pop(ev, None)
        for b in reads:
            b.readers.append(ev)
        for b in writes:
            b.last_w = ev
            b.readers = []
        ops.append({"fn": fn, "deps": deps, "dma": did, "sig": False})
        return ev

    def finish(self):
        nc = self.nc
        st = self.st
        ops = self.ops
        for e in ENGS:
            for op in ops[e]:
                for (pe_, pidx), kind in op["deps"].items():
                    if pe_ == "dma":
                        continue
                    if pe_ == e and e == "pe":
                        continue
                    ops[pe_][pidx]["sig"] = True
        for e in ENGS:
            if e != "sp" and ops[e]:
                ops[e][-1]["sig"] = True
        cnt = {}
        for e in ENGS:
            c = st["sig"][e]
            for i, op in enumerate(ops[e]):
                if op["sig"] and op["dma"] is None:
                    c += 1
                cnt[(e, i)] = c
            st["sig_end"] = st.get("sig_end", {})
            st["sig_end"][e] = c
        final_cnt = dict(st["sig_end"])
        dma_first = st["dma_id"] - sum(1 for e in ENGS for op in ops[e] if op["dma"] is not None)
        dma_last = st["dma_id"]

        def dma_target(did):
            return did % ND_SEMS, 16 * (did // ND_SEMS + 1)

        sems = self.sems
        waited = st["waited"]

        def emit_engine(e, engobj):
            w = waited[e]

            def wait(key, semh, val):
                if w.get(key, 0) >= val:
                    return
                engobj.wait_ge(semh, val)
                w[key] = val

            for i, op in enumerate(ops[e]):
                need = {}
                for (pe_, pidx), kind in op["deps"].items():
                    if pe_ == "dma":
                        si, val = dma_target(pidx)
                        key = ("dma", si)
                        need[key] = max(need.get(key, 0), val)
                    else:
                        if pe_ == e and e == "pe":
                            continue
                        need[pe_] = max(need.get(pe_, 0), cnt[(pe_, pidx)])
                if op["dma"] is not None and op["dma"] >= ND_SEMS:
                    si, val = dma_target(op["dma"] - ND_SEMS)
                    key = ("dma", si)
                    need[key] = max(need.get(key, 0), val)
                for key, val in need.items():
                    if isinstance(key, tuple):
                        wait(key, sems["dma"][key[1]], val)
                    else:
                        wait(key, sems[key], val)
                ins = op["fn"](engobj)
                if op["dma"] is not None:
                    si, _ = dma_target(op["dma"])
                    ins.then_inc(sems["dma"][si], 16)
                elif op["sig"]:
                    ins.then_inc(sems[e], 1)
            for pe_ in ENGS:
                if pe_ == "sp" or pe_ == e:
                    continue
                if final_cnt[pe_] > 0:
                    wait(pe_, sems[pe_], final_cnt[pe_])
            for did in range(max(dma_first, dma_last - ND_SEMS), dma_last):
                si, val = dma_target(did)
                wait(("dma", si), sems["dma"][si], val)

        with nc.Block() as block:
            @block.tensor
            def _(eng):
                emit_engine("pe", eng)

            @block.scalar
            def _(eng):
                emit_engine("act", eng)

            @block.vector
            def _(eng):
                emit_engine("dve", eng)

            @block.gpsimd
            def _(eng):
                emit_engine("pool", eng)

            @block.sync
            def _(eng):
                emit_engine("sp", eng)

        for e in ENGS:
            st["sig"][e] = final_cnt[e]
        for b in self.touched:
            b.last_w = None
            b.readers = []


def new_state():
    return {"dma_id": 0, "sig": {e: 0 for e in ENGS}, "waited": {e: {} for e in ENGS}}


D = 1024
EPS = 1e-6
LAM_INIT = 0.2
PAST = 2048
NS = 16


class KB:
    def __init__(self, NT, with_sample=True):
        self.NT = NT
        self.S = NT * 128
        self.with_sample = with_sample
        self.nc = bass.Bass("TRN2", target_bir_lowering=False)
        self.st = new_state()
        self.uid = 0

    def din(self, name, shape, dt=F32):
        return self.nc.dram_tensor(name, list(shape), dt, kind="ExternalInput").ap()

    def dout(self, name, shape, dt=F32):
        return self.nc.dram_tensor(name, list(shape), dt, kind="ExternalOutput").ap()

    def dscr(self, name, shape, dt):
        return self.nc.dram_tensor(name, list(shape), dt).ap()

    def sb(self, es, name, shape, dt):
        self.uid += 1
        return es.enter_context(self.nc.sbuf_tensor("%s_%d" % (name, self.uid), list(shape), dt))

    def ps(self, es, name, shape, dt):
        self.uid += 1
        return es.enter_context(self.nc.psum_tensor("%s_%d" % (name, self.uid), list(shape), dt))

    def mm(self, out, lhsT, rhs, start, stop, r, w, skip=False):
        self.pg.add("pe", lambda e: e.matmul(out, lhsT=lhsT, rhs=rhs, start=start, stop=stop,
                                             skip_group_check=skip), r, w)

    def tr(self, out, in_, P, r, w):
        ident = self.ident
        self.pg.add("pe", lambda e: e.transpose(out=out, in_=in_, identity=ident[0:P, 0:P]), r, w)

    def tr32(self, out, in_, P, r, w):
        ident = self.ident32
        self.pg.add("pe", lambda e: e.transpose(out=out, in_=in_, identity=ident[0:P, 0:P]), r, w)

    def act(self, out, in_, func, r, w, scale=1.0, bias=0.0, accum=None):
        if accum is None:
            self.pg.add("act", lambda e: e.activation(out=out, in_=in_, func=func, bias=bias, scale=scale), r, w)
        else:
            self.pg.add("act", lambda e: e.activation(out=out, in_=in_, func=func, bias=bias, scale=scale,
                                                      accum_out=accum), r, w)

    def tt(self, eng, out, a, b, op, r, w):
        self.pg.add(eng, lambda e: e.tensor_tensor(out=out, in0=a, in1=b, op=op), r, w)

    def ts(self, eng, out, a, s1, s2, op0, op1, r, w):
        if s2 is None:
            self.pg.add(eng, lambda e: e.tensor_scalar(out=out, in0=a, scalar1=s1, scalar2=None, op0=op0), r, w)
        else:
            self.pg.add(eng, lambda e: e.tensor_scalar(out=out, in0=a, scalar1=s1, scalar2=s2, op0=op0, op1=op1), r, w)

    def stt(self, eng, out, a, s, b, op0, op1, r, w, accum=None):
        if accum is None:
            self.pg.add(eng, lambda e: e.scalar_tensor_tensor(out=out, in0=a, scalar=s, in1=b, op0=op0, op1=op1), r, w)
        else:
            self.pg.add(eng, lambda e: e.scalar_tensor_tensor(out=out, in0=a, scalar=s, in1=b, op0=op0, op1=op1,
                                                              accum_out=accum), r, w)

    def cp(self, eng, out, in_, r, w):
        if eng == "act":
            self.pg.add("act", lambda e: e.activation(out=out, in_=in_, func=AF.Copy), r, w)
        else:
            self.pg.add(eng, lambda e: e.tensor_copy(out=out, in_=in_), r, w)

    def rcp(self, out, in_, r, w):
        self.pg.add("dve", lambda e: e.reciprocal(out=out, in_=in_), r, w)

    def mset(self, eng, ap, val, w):
        self.pg.add(eng, lambda e: e.memset(ap, val), (), w)

    def dma(self, out, in_, r, w, eng="sp"):
        self.pg.add(eng, lambda e: e.dma_start(out=out, in_=in_), r, w, dma=True)

    def new_prog(self):
        self.pg = Prog(self.nc, self.sems, self.st)

    def rstd(self, ss, tmpa, tmpb, out, inv_n, bss, btmp, bout):
        self.ts("dve", tmpa, ss, inv_n, EPS, ALU.mult, ALU.add, [bss], [btmp])
        self.act(tmpb, tmpa, AF.Ln, [btmp], [btmp])
        self.act(out, tmpb, AF.Exp, [btmp], [bout], scale=-0.5)


class Tl:
    __slots__ = ("t", "b")

    def __init__(self, t):
        self.t = t
        self.b = B()


def _build(self):
    nc = self.nc
    S, NT = self.S, self.NT
    NKB = NT
    NQ = NT // 4
    din, dout, dscr = self.din, self.dout, self.dscr
    xp = din("xp", [S, D]); xs = din("xs", [32, D])
    cmat_p = din("cmat_p", [128, 8, 128]); cmat_s = din("cmat_s", [128, 8, 32])
    w_ada = din("w_ada", [D, 3 * D]); b_ada = din("b_ada", [3 * D]); norm_g = din("norm_g", [D])
    w_in = din("w_in", [D, 6144]); lams = din("lams", [256]); subln_g = din("subln_g", [128])
    wba_d = din("wba", [512, D]); wbb_d = din("wbb", [512, D]); wo_d = din("w_out", [D, D]); final_g = din("final_g", [D])
    cak = din("cak", [2, PAST, 512]); cav = din("cav", [2, PAST, 512])
    cbk = din("cbk", [2, PAST, 512]); cbv = din("cbv", [2, PAST, 512])
    cos_p = din("cos_p", [S, 256]); sin_p = din("sin_p", [S, 256])
    cos_s = din("cos_s", [32, 256]); sin_s = din("sin_s", [32, 256])
    ident_d = din("ident", [128, 128], BF16); triP_d = din("triP", [128, 128], BF16)
    id32_d = din("ident32", [128, 128]); mask_d = din("mask_lt", [128, 128])
    y_p = dout("y_p", [S, D]); y_s = dout("y_s", [32, D])
    pak = dout("pak", [S, 512]); pav = dout("pav", [S, 512]); pbk = dout("pbk", [S, 512]); pbv = dout("pbv", [S, 512])
    sak = dout("sak", [32, 512]); sav = dout("sav", [32, 512]); sbk = dout("sbk", [32, 512]); sbv = dout("sbv", [32, 512])
    qkT_d = dscr("qkT_d", [16, 128, S], BF16); v_d = dscr("v_d", [S, 1024], BF16)
    oa_d = dscr("oa_d", [S, 512], F32); obT_d = dscr("obT_d", [4, 128, S], F32)
    qkT_sd = dscr("qkT_sd", [16, 128, 32], BF16); v_sd = dscr("v_sd", [32, 1024], BF16)
    oa_sd = dscr("oa_sd", [32, 512], F32); obT_sd = dscr("obT_sd", [4, 128, 32], F32)

    mm, tr, act, tt, ts, stt, cp, rcp, mset, dma = (self.mm, self.tr, self.act, self.tt, self.ts, self.stt,
                                                    self.cp, self.rcp, self.mset, self.dma)

    with contextlib.ExitStack() as top:
        self.sems = {e: top.enter_context(nc.semaphore("s_" + e)) for e in ENGS if e != "sp"}
        self.sems["dma"] = [top.enter_context(nc.semaphore("s_dma%d" % i)) for i in range(ND_SEMS)]

        def T(es, name, shape, dt):
            return Tl(self.sb(es, name, shape, dt))

        def PT(es, name, shape, dt):
            return Tl(self.ps(es, name, shape, dt))

        Abc = [T(top, "Abc%d" % i, [128, D], F32) for i in range(2)]
        shbc = [T(top, "shbc%d" % i, [128, D], F32) for i in range(2)]
        gtbc = [T(top, "gtbc%d" % i, [128, D], F32) for i in range(2)]
        fgbc = T(top, "fgbc", [128, D], F32)
        gsub = T(top, "gsub", [128, 128], F32)
        nlam = T(top, "nlam", [128, 1], F32)
        identT = T(top, "ident", [128, 128], BF16); self.ident = identT.t
        id32T = T(top, "ident32", [128, 128], F32); self.ident32 = id32T.t
        triP = T(top, "triP", [128, 128], BF16); onesT = T(top, "onesT", [128, 128], mybir.dt.float32r); ones32 = T(top, "ones32", [128, 128], F32)
        zf = T(top, "zf", [128, 2, 512], F32); self.zf = zf
        maskT = T(top, "mask", [128, 128], F32)
        zer = T(top, "zer", [128, 512], BF16)
        bconst = B()

        with contextlib.ExitStack() as es01:
            wq = T(es01, "wq", [128, 8, 3072], BF16)
            with contextlib.ExitStack() as es0:
                self.new_prog()
                wst = [T(es0, "wst%d" % i, [128, 4, 512], F32) for i in range(2)]
                bada = T(es0, "bada", [128, 3 * D], F32)
                ngbc = T(es0, "ngbc", [128, D], F32)
                modt = [T(es0, "mod%d" % i, [128, 3 * D], F32) for i in range(2)]
                cm = [T(es0, "cm0", [128, 8, 128], F32), T(es0, "cm1", [128, 8, 32], F32)]
                lamv = T(es0, "lamv", [128, 256], F32)
                sm0 = T(es0, "sm0", [128, 8], F32)
                j64 = T(es0, "j64", [128, 64], F32)
                graw = T(es0, "graw", [128, 128], F32)
                MOD = [PT(es0, "MOD%d" % i, [128, 512], F32) for i in range(2)]

                dma(identT.t[:], ident_d, [], [bconst]); dma(triP.t[:], triP_d, [], [bconst]); dma(id32T.t[:], id32_d, [], [bconst])
                mset("pool", ones32.t[:], 1.0, [ones32.b]); cp("dve", onesT.t[:], ones32.t[:], [ones32.b], [bconst])
                mset("pool", zf.t[:], 0.0, [bconst]); dma(maskT.t[:], mask_d, [], [bconst])
                mset("pool", zer.t[:], 0.0, [zer.b])
                dma(bada.t[:], b_ada.partition_broadcast(128), [], [bada.b])
                dma(ngbc.t[:], norm_g.partition_broadcast(128), [], [ngbc.b])
                dma(fgbc.t[:], final_g.partition_broadcast(128), [], [fgbc.b])
                dma(graw.t[:], subln_g.partition_broadcast(128), [], [graw.b])
                dma(lamv.t[:], lams.partition_broadcast(128), [], [lamv.b])
                dma(cm[0].t[:], cmat_p, [], [cm[0].b]); dma(cm[1].t[:], cmat_s, [], [cm[1].b])
                stt("dve", j64.t[:], lamv.t[:, 0:64], 1.0, lamv.t[:, 64:128], ALU.mult, ALU.mult, [lamv.b], [j64.b, sm0.b], accum=sm0.t[:, 0:1])
                stt("dve", j64.t[:], lamv.t[:, 128:192], 1.0, lamv.t[:, 192:256], ALU.mult, ALU.mult, [lamv.b, sm0.b], [j64.b, sm0.b], accum=sm0.t[:, 1:2])
                act(sm0.t[:, 2:4], sm0.t[:, 0:2], AF.Exp, [sm0.b], [sm0.b])
                tt("dve", sm0.t[:, 4:5], sm0.t[:, 2:3], sm0.t[:, 3:4], ALU.subtract, [sm0.b], [sm0.b])
                ts("dve", nlam.t[:], sm0.t[:, 4:5], LAM_INIT, -1.0, ALU.add, ALU.mult, [sm0.b], [nlam.b])
                ts("dve", gsub.t[:], graw.t[:], 1.0 - LAM_INIT, None, ALU.mult, None, [graw.b], [gsub.b])
                wa_v = w_ada.rearrange("(j p) n -> p j n", p=128)
                k = 0
                for g in range(6):
                    for half in range(2):
                        w_ = wst[k % 2]; k += 1
                        dma(w_.t[:], wa_v[:, 4 * half:4 * half + 4, g * 512:(g + 1) * 512], [], [w_.b])
                        for jj in range(4):
                            j = 4 * half + jj
                            mm(MOD[0].t[:, :], cm[0].t[:, j, :], w_.t[:, jj, :], j == 0, j == 7, [cm[0].b, w_.b], [MOD[0].b])
                            mm(MOD[1].t[0:32, :], cm[1].t[:, j, :], w_.t[:, jj, :], j == 0, j == 7, [cm[1].b, w_.b], [MOD[1].b])
                    cs = slice(g * 512, (g + 1) * 512)
                    tt("dve", modt[0].t[:, cs], MOD[0].t[:, :], bada.t[:, cs], ALU.add, [MOD[0].b, bada.b], [modt[0].b])
                    tt("dve", modt[1].t[0:32, cs], MOD[1].t[0:32, :], bada.t[0:32, cs], ALU.add, [MOD[1].b, bada.b], [modt[1].b])
                for i, P in ((0, 128), (1, 32)):
                    stt("dve", Abc[i].t[0:P, :], modt[i].t[0:P, D:2 * D], 1.0, ngbc.t[0:P, :], ALU.add, ALU.mult, [modt[i].b, ngbc.b], [Abc[i].b])
                    cp("pool", shbc[i].t[0:P, :], modt[i].t[0:P, 0:D], [modt[i].b], [shbc[i].b])
                    cp("pool", gtbc[i].t[0:P, :], modt[i].t[0:P, 2 * D:3 * D], [modt[i].b], [gtbc[i].b])
                wi_v = w_in.rearrange("(j p) n -> p j n", p=128)
                qkv_cols = [0, 512, 1024, 2048, 2560, 3072]
                for g in range(6):
                    for half in range(2):
                        w_ = wst[k % 2]; k += 1
                        dma(w_.t[:], wi_v[:, 4 * half:4 * half + 4, qkv_cols[g]:qkv_cols[g] + 512], [], [w_.b])
                        cp("dve" if k % 2 else "pool", wq.t[:, 4 * half:4 * half + 4, g * 512:(g + 1) * 512], w_.t[:], [w_.b], [wq.b])
                self.pg.finish()

            with contextlib.ExitStack() as es1:
                self.new_prog()
                L = self.alloc_norm(es1, T, PT)
                stage = [T(es1, "stage%d" % i, [128, 2048], F32) for i in range(2)]
                qkb = [T(es1, "qkb%d" % i, [128, 2048], BF16) for i in range(2)]
                vbf = [T(es1, "vbf%d" % i, [128, 1024], BF16) for i in range(2)]
                qkTs = [T(es1, "qkTs%d" % i, [128, 16, 128], BF16) for i in range(2)]
                cst = [T(es1, "cst%d" % i, [128, 256], F32) for i in range(2)]
                snt = [T(es1, "snt%d" % i, [128, 256], F32) for i in range(2)]
                rp = [T(es1, "rp%d" % i, [128, 512], F32) for i in range(2)]
                rt = [T(es1, "rt%d" % i, [128, 256], F32) for i in range(4)]
                PG = [PT(es1, "PG%d" % i, [128, 512], F32) for i in range(4)]
                TQ = PT(es1, "TQ", [128, 16, 128], BF16)

                tiles = []
                for t in range(NT):
                    r0 = t * 128
                    tiles.append(dict(P=128, x=xp[r0:r0 + 128, :], mod=0, cos=cos_p[r0:r0 + 128, :], sin=sin_p[r0:r0 + 128, :],
                                      outs=[pak[r0:r0 + 128, :], pav[r0:r0 + 128, :], pbk[r0:r0 + 128, :], pbv[r0:r0 + 128, :]],
                                      qkT=qkT_d[:, :, r0:r0 + 128], v=v_d[r0:r0 + 128, :]))
                if self.with_sample:
                    tiles.append(dict(P=32, x=xs, mod=1, cos=cos_s, sin=sin_s, outs=[sak, sav, sbk, sbv],
                                      qkT=qkT_sd, v=v_sd))
                n = len(tiles)
                pgi = [0]

                def load(i):
                    tl = tiles[i]; P = tl["P"]; s = i % 2
                    dma(L["xt"][s].t[0:P, :], tl["x"], [], [L["xt"][s].b])
                    dma(cst[s].t[0:P, :], tl["cos"], [], [cst[s].b])
                    dma(snt[s].t[0:P, :], tl["sin"], [], [snt[s].b])

                def pe_main(i):
                    tl = tiles[i]; P = tl["P"]; s = i % 2
                    self.pe_hT(L, s, P)
                    hT = L["hT"][s]
                    grp = []
                    for g in range(6):
                        pgt = PG[pgi[0] % 4]; pgi[0] += 1
                        for j in range(8):
                            mm(pgt.t[0:P, :], hT.t[:, j, 0:P], wq.t[:, j, g * 512:(g + 1) * 512], j == 0, j == 7, [hT.b, wq.b], [pgt.b])
                        grp.append(pgt)
                        self.p1_evac(g, pgt, P, s, stage[s], qkb[s], vbf[s], rp, rt, cst[s], snt[s])
                    return grp

                def tq(i):
                    tl = tiles[i]; P = tl["P"]; s = i % 2
                    for u in range(16):
                        tr(TQ.t[:, u, 0:P], qkb[s].t[0:P, u * 128:(u + 1) * 128], P, [qkb[s].b, bconst], [TQ.b])
                    cp("dve", qkTs[s].t[:, :, 0:P], TQ.t[:, :, 0:P], [TQ.b], [qkTs[s].b])
                    dma(tl["qkT"].rearrange("u p t -> p u t"), qkTs[s].t[:, :, 0:P], [qkTs[s].b], [], eng="pool")
                    for q in range(4):
                        dma(tl["outs"][q], stage[s].t[0:P, q * 512:(q + 1) * 512], [stage[s].b], [], eng="pool")
                    dma(tl["v"], vbf[s].t[0:P, :], [vbf[s].b], [], eng="pool")

                load(0)
                if n > 1:
                    load(1)
                self.norm(L, 0, tiles[0]["P"], Abc[tiles[0]["mod"]], shbc[tiles[0]["mod"]])
                for i in range(n):
                    if i + 1 < n:
                        self.norm(L, (i + 1) % 2, tiles[i + 1]["P"], Abc[tiles[i + 1]["mod"]], shbc[tiles[i + 1]["mod"]])
                    pe_main(i)
                    if i >= 1:
                        tq(i - 1)
                    if i + 2 < n:
                        load(i + 2)
                tq(n - 1)
                self.pg.finish()

        with contextlib.ExitStack() as es2:
            self.new_prog()
            A = self.alloc_attn(es2, T, PT, NKB)
            v_v = v_d.rearrange("(kb p) n -> p kb n", p=128)
            pendA = None; pendB = None
            for u in range(8):
                s = u % 2
                isA = u < 4
                KT, V = A["KT"][s], A["V"][s]
                dma(KT.t[:, 0:S], qkT_d[(4 + u) if isA else (12 + u - 4)], [], [KT.b])
                col0 = u * 128 if isA else 512 + (u - 4) * 128
                for k0 in range(0, NKB, 16):
                    k1 = min(NKB, k0 + 16)
                    dma(V.t[:, k0:k1, 0:128], v_v[:, k0:k1, col0:col0 + 128], [], [V.b])
                if u < 2:
                    mset("pool", V.t[:, :, 128:129], 1.0, [V.b])
                for Tq in range(NQ):
                    qs = A["QT"][A["qi"] % 2]; A["qi"] += 1
                    dma(qs.t[:, :], qkT_d[u if isA else (8 + u - 4)][:, Tq * 512:(Tq + 1) * 512], [], [qs.b])
                    blocks = []
                    for kb in range(4 * Tq + 4):
                        j = kb - 4 * Tq
                        blocks.append((kb, 128, 128 * j if j > 0 else 0, j >= 0))
                    if isA:
                        chunks = [(m, 128 * m, 128, 4 * Tq + m) for m in range(4)]
                        ost = A["ost"][A["oi"] % 2]; A["oi"] += 1
                        dst = oa_d.rearrange("(t m p) n -> t p m n", m=4, p=128)[Tq][:, :, u * 128:(u + 1) * 128]
                        pendA = self.a_tile(A, lambda c, kb, nk, KT=KT: KT.t[64 * c:64 * c + 64, kb * 128:(kb + 1) * 128], KT.b,
                                            lambda kb, nk, V=V: V.t[0:nk, kb, 0:129], V.b,
                                            blocks, 512, qs, chunks, ost, gsub, nlam, zer, bconst, prev=pendA,
                                            store=lambda dst=dst, ost=ost: dma(dst, ost.t[:, :, :], [ost.b], [], eng="pool"))
                    else:
                        if pendA is not None:
                            pendA(); pendA = None
                        obs = A["obs"][A["oi"] % 2]; A["oi"] += 1
                        dstb = obT_d[u - 4][:, Tq * 512:(Tq + 1) * 512]
                        pendB = self.b_tile(A, lambda h2, kb, nk, KT=KT: KT.t[64 * h2:64 * h2 + 64, kb * 128:(kb + 1) * 128], KT.b,
                                            lambda h2, kb, nk, V=V: V.t[0:nk, kb, 64 * h2:64 * h2 + 64], V.b,
                                            blocks[::-1], 512, qs, obs, triP, onesT, maskT, bconst, prev=pendB,
                                            store=lambda dstb=dstb, obs=obs: dma(dstb, obs.t[:, :], [obs.b], [], eng="pool"))
            if pendB:
                for f in pendB:
                    f()
            self.pg.finish()

        if self.with_sample:
            with contextlib.ExitStack() as es2s:
                self.new_prog()
                self.sample_attn(es2s, T, PT, cak, cav, cbk, cbv, qkT_sd, v_sd, oa_sd, obT_sd,
                                 gsub, nlam, zer, triP, onesT, maskT, bconst)
                self.pg.finish()

        with contextlib.ExitStack() as es3:
            self.new_prog()
            self.phase3(es3, T, PT, w_in, wba_d, wbb_d, wo_d, xp, xs, oa_d, obT_d, oa_sd, obT_sd, y_p, y_s,
                        Abc, shbc, gtbc, fgbc, bconst)
            self.pg.finish()
    return nc


KB.build = _build


class Tv:
    __slots__ = ("t", "b")

    def __init__(self, ap, b):
        self.t = ap
        self.b = b


def _alloc_norm(self, es, T, PT):
    L = dict(
        xt=[T(es, "xt%d" % i, [128, D], F32) for i in range(2)],
        tmp=[T(es, "tmp%d" % i, [128, D], F32) for i in range(2)],
        hb=[T(es, "hb%d" % i, [128, D], BF16) for i in range(2)],
        hT=[T(es, "hT%d" % i, [128, 8, 128], BF16) for i in range(2)],
        sqj=T(es, "sqj", [128, D], BF16),
        smn=[T(es, "smn%d" % i, [128, 8], F32) for i in range(4)],
        TP=PT(es, "TP", [128, 8, 128], BF16),
        ni=0,
    )
    return L


def _norm(self, L, s, P, Abc, shbc):
    xt = L["xt"][s]; tmp = L["tmp"][s]; hb = L["hb"][s]
    sm = L["smn"][L["ni"] % 4]; L["ni"] += 1
    sqj = L["sqj"]
    self.act(sqj.t[0:P, :], xt.t[0:P, :], AF.Square, [xt.b], [sqj.b, sm.b], accum=sm.t[0:P, 0:1])
    self.rstd(sm.t[0:P, 0:1], sm.t[0:P, 1:2], sm.t[0:P, 2:3], sm.t[0:P, 3:4], 1.0 / D, sm.b, sm.b, sm.b)
    self.stt("dve", tmp.t[0:P, :], xt.t[0:P, :], sm.t[0:P, 3:4], Abc.t[0:P, :], ALU.mult, ALU.mult,
             [xt.b, sm.b, Abc.b], [tmp.b])
    self.tt("dve", hb.t[0:P, :], tmp.t[0:P, :], shbc.t[0:P, :], ALU.add, [tmp.b, shbc.b], [hb.b])


def _pe_hT(self, L, s, P):
    hb = L["hb"][s]; hT = L["hT"][s]; TP = L["TP"]
    for j in range(8):
        self.tr(TP.t[:, j, 0:P], hb.t[0:P, j * 128:(j + 1) * 128], P, [hb.b], [TP.b])
    self.cp("act", hT.t[:, :, 0:P], TP.t[:, :, 0:P], [TP.b], [hT.b])


def _rope(self, src, P, dst_ap, bdst, cst, snt, rt):
    pat = "p (g two f) -> p g two f"
    sv = src.t[0:P, :].rearrange(pat, two=2, f=32)
    dv = dst_ap.rearrange(pat, two=2, f=32)
    x1, x2 = sv[:, :, 0, :], sv[:, :, 1, :]
    cv = cst.t[0:P, :].rearrange("p (g f) -> p g f", f=32)
    sn = snt.t[0:P, :].rearrange("p (g f) -> p g f", f=32)
    t = [r_.t[0:P, :].rearrange("p (g f) -> p g f", f=32) for r_ in rt]
    tt = self.tt
    tt("dve", t[0], x1, cv, ALU.mult, [src.b, cst.b], [rt[0].b])
    tt("pool", t[1], x2, sn, ALU.mult, [src.b, snt.b], [rt[1].b])
    tt("dve", dv[:, :, 0, :], t[0], t[1], ALU.subtract, [rt[0].b, rt[1].b], [bdst])
    tt("pool", t[2], x2, cv, ALU.mult, [src.b, cst.b], [rt[2].b])
    tt("dve", t[3], x1, sn, ALU.mult, [src.b, snt.b], [rt[3].b])
    tt("pool", dv[:, :, 1, :], t[2], t[3], ALU.add, [rt[2].b, rt[3].b], [bdst])


def _p1_evac(self, g, pgt, P, s, stage, qkb, vbf, rp, rt, cst, snt):
    cp = self.cp
    if g == 0:
        cp("act", rp[0].t[0:P, :], pgt.t[0:P, :], [pgt.b], [rp[0].b])
        self.rope(rp[0], P, qkb.t[0:P, 0:512], qkb.b, cst, snt, rt)
    elif g == 1:
        cp("act", rp[1].t[0:P, :], pgt.t[0:P, :], [pgt.b], [rp[1].b])
        self.rope(rp[1], P, stage.t[0:P, 0:512], stage.b, cst, snt, rt)
        cp("dve", qkb.t[0:P, 512:1024], stage.t[0:P, 0:512], [stage.b], [qkb.b])
    elif g == 2:
        cp("act", stage.t[0:P, 512:1024], pgt.t[0:P, :], [pgt.b], [stage.b])
        cp("dve", vbf.t[0:P, 0:512], stage.t[0:P, 512:1024], [stage.b], [vbf.b])
    elif g == 3:
        cp("act", qkb.t[0:P, 1024:1536], pgt.t[0:P, :], [pgt.b], [qkb.b])
    elif g == 4:
        cp("act", stage.t[0:P, 1024:1536], pgt.t[0:P, :], [pgt.b], [stage.b])
        cp("dve", qkb.t[0:P, 1536:2048], stage.t[0:P, 1024:1536], [stage.b], [qkb.b])
    else:
        cp("act", stage.t[0:P, 1536:2048], pgt.t[0:P, :], [pgt.b], [stage.b])
        cp("dve", vbf.t[0:P, 512:1024], stage.t[0:P, 1536:2048], [stage.b], [vbf.b])


def _alloc_attn(self, es, T, PT, NKB, G=8):
    A = {}
    if NKB:
        A["KT"] = [T(es, "KT%d" % i, [128, NKB * 128], BF16) for i in range(2)]
        A["V"] = [T(es, "V%d" % i, [128, NKB, 130], BF16) for i in range(2)]
    A["zf"] = self.zf
    A["G"] = G
    A["QT"] = [T(es, "QT%d" % i, [128, 512], BF16) for i in range(2)]
    A["S"] = [PT(es, "S%d" % i, [128, 2, 512], F32) for i in range(2)]
    A["Sb"] = [[B(), B()], [B(), B()]]
    A["X"] = PT(es, "X", [128, 3, 512], F32)
    A["bX"] = [B(), B(), B()]
    A["E"] = [T(es, "E%d" % i, [128, 2, 512], BF16) for i in range(3)]
    A["spg"] = [T(es, "spg%d" % i, [128, 2, 512], BF16) for i in range(A["G"])]
    A["LsF"] = [T(es, "LsF%d" % i, [128, 2, 512], mybir.dt.float32r) for i in range(A["G"] + 1)]
    A["nq"] = [T(es, "nq%d" % i, [128, 512], BF16) for i in range(2)]
    A["a"] = [T(es, "a%d" % i, [128, 2, 512], BF16) for i in range(A["G"] + 3)]

    A["ost"] = [T(es, "ost%d" % i, [128, 4, 128], F32) for i in range(2)]
    A["obs"] = [T(es, "obs%d" % i, [128, 512], F32) for i in range(2)]
    A["oo"] = [T(es, "oo%d" % i, [128, 4, 128], F32) for i in range(2)]
    A["t0"] = [T(es, "t0%d" % i, [128, 128], F32) for i in range(2)]
    A["sma"] = [T(es, "sma%d" % i, [128, 16], F32) for i in range(4)]
    A["smb"] = [T(es, "smb%d" % i, [128, 12], F32) for i in range(4)]
    A["jk"] = T(es, "jk", [128, 128], F32)
    for k in ("si", "ei", "wi", "qi", "oi", "ti", "fi", "nqi", "zi"):
        A[k] = 0
    return A


def _a_tile(self, A, kt, bkt, vv, bv, blocks, N, qs, chunks, ost, gsub, nlam, zer, bconst, prev=None, store=None):
    mm, act, tt, ts, stt, rcp, mset = self.mm, self.act, self.tt, self.ts, self.stt, self.rcp, self.mset
    nb = len(blocks)
    X, bX = A["X"], A["bX"]
    sbase, ebase = A["si"], A["ei"]
    A["si"] += nb; A["ei"] += nb
    ti = A["ti"]; A["ti"] += 1
    sm = A["sma"][ti % 4]; sm2 = A["smb"][ti % 4]; oo = A["oo"][ti % 2]; jk = A["jk"]
    qn0 = chunks[0][2]
    nm = len(chunks)

    def acc(m, c):
        a = m * 2 + c
        return X.t[:, a // 3, (a % 3) * 130:(a % 3) * 130 + 129], bX[a // 3]

    def qk(i):
        kbi, nk, c0, diag = blocks[i]
        Sl = A["S"][(sbase + i) % 2]; Sb = A["Sb"][(sbase + i) % 2]
        for c in range(2):
            mm(Sl.t[0:nk, c, c0:N], kt(c, kbi, nk), qs.t[64 * c:64 * c + 64, c0:N], True, True, [bkt, qs.b], [Sb[c]])

    def ex(i):
        kbi, nk, c0, diag = blocks[i]
        Sl = A["S"][(sbase + i) % 2]; El = A["E"][(ebase + i) % 3]; Sb = A["Sb"][(sbase + i) % 2]
        act(El.t[0:nk, :, c0:N], Sl.t[0:nk, :, c0:N], AF.Exp, Sb, [El.b], scale=0.125)
        if diag:
            mset("pool", El.t[64:128, :, c0:c0 + 64], 0.0, [El.b])

    def pv(i):
        kbi, nk, c0, diag = blocks[i]
        El = A["E"][(ebase + i) % 3]
        for c in range(2):
            for (m, q0, qn, last) in chunks:
                if q0 < c0:
                    continue
                o_ap, ob = acc(m, c)
                mm(o_ap[0:qn, :], El.t[0:nk, c, q0:q0 + qn], vv(kbi, nk), False, i == last, [El.b, bv], [ob], skip=True)

    def fin_chunk(m, q0, qn):
        (a0, b0), (a1, b1) = acc(m, 0), acc(m, 1)
        t0 = A["t0"][A["fi"] % 2]; A["fi"] += 1
        rcp(sm.t[0:qn, m:m + 1], a0[0:qn, 128:129], [b0], [sm.b])
        rcp(sm.t[0:qn, 4 + m:5 + m], a1[0:qn, 128:129], [b1], [sm.b])
        tt("dve", sm.t[0:qn, 8 + m:9 + m], sm.t[0:qn, 4 + m:5 + m], nlam.t[0:qn, :], ALU.mult, [sm.b], [sm.b])
        ts("dve", t0.t[0:qn, :], a0[0:qn, 0:128], sm.t[0:qn, m:m + 1], None, ALU.mult, None, [b0, sm.b], [t0.b])
        stt("dve", oo.t[0:qn, m, :], a1[0:qn, 0:128], sm.t[0:qn, 8 + m:9 + m], t0.t[0:qn, :], ALU.mult, ALU.add,
            [b1, sm.b, t0.b], [oo.b])
        stt("dve", jk.t[0:qn, :], oo.t[0:qn, m, :], 1.0, oo.t[0:qn, m, :], ALU.mult, ALU.mult, [oo.b], [jk.b, sm.b],
            accum=sm.t[0:qn, 12 + m:13 + m])

    qk(0)
    for b in range(3):
        mm(X.t[:, b, :], zer.t[:, 0:128], zer.t[:, :], True, False, [], [bX[b]], skip=True)
    if nb > 1:
        qk(1)
    for i in range(nb):
        ex(i)
        if i + 2 < nb:
            qk(i + 2)
        pv(i)
        if prev is not None and i == min(1, nb - 1):
            prev()
        for (m, q0, qn, last) in chunks:
            if last == i:
                fin_chunk(m, q0, qn)

    def finish():
        ts("dve", sm2.t[0:qn0, 0:nm], sm.t[0:qn0, 12:12 + nm], 1.0 / 128, EPS, ALU.mult, ALU.add, [sm.b], [sm2.b])
        act(sm2.t[0:qn0, 4:4 + nm], sm2.t[0:qn0, 0:nm], AF.Ln, [sm2.b], [sm2.b])
        act(sm2.t[0:qn0, 8:8 + nm], sm2.t[0:qn0, 4:4 + nm], AF.Exp, [sm2.b], [sm2.b], scale=-0.5)
        for (m, q0, qn, last) in chunks:
            stt("dve", ost.t[0:qn, m, :], oo.t[0:qn, m, :], sm2.t[0:qn, 8 + m:9 + m], gsub.t[0:qn, :], ALU.mult, ALU.mult,
                [oo.b, sm2.b], [ost.b])
        if store is not None:
            store()
    return finish


def _b_tile(self, A, kt, bkt, vv, bv, blocks, N, qs, obs, triP, ones, maskT, bconst, prev=None, store=None):
    mm, act, tt, cp = self.mm, self.act, self.tt, self.cp
    nb = len(blocks)
    G = A["G"]
    X, bX = A["X"], A["bX"]
    S0, S1 = A["S"]; Sb = A["Sb"]
    zslots = [(X.t[:, 0, :], bX[0]), (X.t[:, 1, :], bX[1]), (S0.t[:, 0, :], Sb[0][0]), (S0.t[:, 1, :], Sb[0][1]),
              (S1.t[:, 0, :], Sb[1][0]), (S1.t[:, 1, :], Sb[1][1])]
    NZ = len(zslots)
    nq = A["nq"][A["nqi"] % 2]; A["nqi"] += 1
    LsF = A["LsF"]; R = len(LsF); zf = A["zf"]
    NA = len(A["a"])
    abase = A["wi"]; A["wi"] += nb
    cbase = A["si"]; A["si"] += nb
    zbase = A["zi"]; A["zi"] += 2 * nb
    self.ts("pool", nq.t[:, 0:N], qs.t[:, 0:N], -0.125, None, ALU.mult, None, [qs.b], [nq.b])

    def zs(i, h2):
        return zslots[(zbase + 2 * i + h2) % NZ]

    ctiles = [(S0.t, Sb[0]), (S1.t, Sb[1]), (X.t[:, 0:2, :], [bX[0], bX[1]])]

    def cb(i):
        return ctiles[(cbase + i) % 3]

    def qk(i, h2):
        kbi, nk, c0, diag = blocks[i]
        zt, zb_ = zs(i, h2)
        mm(zt[0:nk, c0:N], kt(h2, kbi, nk), qs.t[64 * h2:64 * h2 + 64, c0:N], True, True, [bkt, qs.b], [zb_])

    def spl(i):
        kbi, nk, c0, diag = blocks[i]
        sp = A["spg"][i % G]
        for h2 in range(2):
            zt, zb_ = zs(i, h2)
            act(sp.t[0:nk, h2, c0:N], zt[0:nk, c0:N], AF.Softplus, [zb_], [sp.b], scale=0.125)
        if diag:
            mw = min(nk, N - c0)
            for h2 in range(2):
                tt("pool", sp.t[0:nk, h2, c0:c0 + mw], sp.t[0:nk, h2, c0:c0 + mw], maskT.t[0:nk, 0:mw], ALU.mult, [sp.b], [sp.b])
        if i + 1 < nb:
            cur = LsF[i % R]; nxt = LsF[(i + 1) % R]
            full = (nk == 128 and c0 == 0)
            if i == 0:
                if not full:
                    cp("dve", nxt.t[:, :, 0:N], zf.t[:, :, 0:N], [], [nxt.b])
                cp("dve", nxt.t[0:nk, :, c0:N], sp.t[0:nk, :, c0:N], [sp.b], [nxt.b])
            else:
                assert nk == 128
                if c0 > 0:
                    cp("dve", nxt.t[:, :, 0:c0], zf.t[:, :, 0:c0], [], [nxt.b])
                tt("dve", nxt.t[:, :, c0:N], cur.t[:, :, c0:N], sp.t[:, :, c0:N], ALU.add, [cur.b, sp.b], [nxt.b])

    def cmm(i):
        kbi, nk, c0, diag = blocks[i]
        (Ct, Cb) = cb(i); sp = A["spg"][i % G]; cur = LsF[i % R]
        for h2 in range(2):
            mm(Ct[0:nk, h2, c0:N], triP.t[0:nk, 0:nk], sp.t[0:nk, h2, c0:N], True, False, [sp.b], Cb)
            if i > 0:
                mm(Ct[:, h2, c0:N], ones.t[:, :], cur.t[:, h2, c0:N], False, False, [cur.b], Cb)
        for h2 in range(2):
            mm(Ct[0:nk, h2, c0:N], kt(h2, kbi, nk), nq.t[64 * h2:64 * h2 + 64, c0:N], False, True, [bkt, nq.b], Cb)

    def ex(i):
        kbi, nk, c0, diag = blocks[i]
        (Ct, Cb) = cb(i); a = A["a"][(abase + i) % NA]
        act(a.t[0:nk, :, c0:N], Ct[0:nk, :, c0:N], AF.Exp, Cb, [a.b], scale=-1.0)
        if diag:
            mw = min(nk, N - c0)
            for h2 in range(2):
                tt("pool", a.t[0:nk, h2, c0:c0 + mw], a.t[0:nk, h2, c0:c0 + mw], maskT.t[0:nk, 0:mw], ALU.mult, [a.b], [a.b])

    def pv(i):
        kbi, nk, c0, diag = blocks[i]
        a = A["a"][(abase + i) % NA]
        for h2 in range(2):
            mm(X.t[64 * h2:64 * h2 + 64, 2, c0:N], vv(h2, kbi, nk), a.t[0:nk, h2, c0:N], i == 0, i == nb - 1,
               [a.b, bv], [bX[2]], skip=True)

    groups = [(g0, min(nb, g0 + G)) for g0 in range(0, nb, G)]
    for gi, (g0, g1) in enumerate(groups):
        pg_ = groups[gi - 1] if gi > 0 else None
        if pg_:
            pend = [(lambda i=i: pv(i)) for i in range(pg_[0], pg_[1])]
        else:
            pend = list(prev) if prev else []
        zq = [(i, h2) for i in range(g0, g1) for h2 in range(2)]
        for k in range(min(NZ, len(zq))):
            qk(*zq[k])
        zn = NZ
        for i in range(g0, g1):
            spl(i)
            for _ in range(2):
                if zn < len(zq):
                    qk(*zq[zn]); zn += 1
            if pend:
                pend.pop(0)()
        while pend:
            pend.pop(0)()
        ahead = min(3, g1 - g0)
        for i in range(g0, g0 + ahead):
            cmm(i)
        for i in range(g0, g1):
            ex(i)
            if i + ahead < g1:
                cmm(i + ahead)
    tail = [(lambda i=i: pv(i)) for i in range(groups[-1][0], groups[-1][1])]

    def evac():
        cp("dve", obs.t[:, 0:N], X.t[:, 2, 0:N], [bX[2]], [obs.b])
        if store is not None:
            store()
    tail.append(evac)
    return tail


KB.alloc_norm = _alloc_norm
KB.norm = _norm
KB.pe_hT = _pe_hT
KB.rope = _rope
KB.p1_evac = _p1_evac
KB.alloc_attn = _alloc_attn
KB.a_tile = _a_tile
KB.b_tile = _b_tile


def _sample_attn(self, es, T, PT, cak, cav, cbk, cbv, qkT_sd, v_sd, oa_sd, obT_sd,
                 gsub, nlam, zer, triP, ones, maskT, bconst):
    mm, tr, act, tt, cp, mset, dma = self.mm, self.tr, self.act, self.tt, self.cp, self.mset, self.dma
    A = self.alloc_attn(es, T, PT, 0, G=4)
    NK = PAST + NS
    KTa = T(es, "KTa", [128, 4, NK], BF16); KTb = T(es, "KTb", [128, 4, NK], BF16)
    VAs = T(es, "VAs", [128, 17, 4, 130], BF16); VBs = T(es, "VBs", [128, 17, 512], BF16)
    ct = [T(es, "ct%d" % i, [128, 512], F32) for i in range(4)]
    qsa = T(es, "qsa", [128, 4, NS], BF16); qsb = T(es, "qsb", [128, 4, NS], BF16)
    Yv = A["X"].t[:, 2, :].rearrange("p (h k) -> p h k", k=128)
    bY = A["bX"][2]
    mset("pool", VAs.t[:, :, :, 128:129], 1.0, [VAs.b])
    qv = qkT_sd.rearrange("u p t -> p u t")
    ci = 0
    for s in range(2):
        t0, t1 = s * NS, (s + 1) * NS
        for i in range(16):
            r0 = i * 128
            for which, src in enumerate((cak, cbk, cav, cbv)):
                c_ = ct[ci % 4]; ci += 1
                dma(c_.t[:, :], src[s, r0:r0 + 128, :], [], [c_.b])
                if which < 2:
                    for h in range(4):
                        self.tr32(Yv[:, h, :], c_.t[:, h * 128:(h + 1) * 128], 128, [c_.b], [bY])
                    dstT = KTa if which == 0 else KTb
                    cp("act", dstT.t[:, :, r0:r0 + 128], Yv, [bY], [dstT.b])
                elif which == 2:
                    cp("dve", VAs.t[:, i, :, 0:128], c_.t[:, :].rearrange("p (h e) -> p h e", e=128), [c_.b], [VAs.b])
                else:
                    cp("dve", VBs.t[:, i, :], c_.t[:, :], [c_.b], [VBs.b])
        dma(KTa.t[:, :, PAST:NK], qv[:, 4:8, t0:t1], [], [KTa.b])
        dma(KTb.t[:, :, PAST:NK], qv[:, 12:16, t0:t1], [], [KTb.b])
        dma(VAs.t[0:NS, 16, :, 0:128], v_sd[t0:t1, 0:512].rearrange("t (h e) -> t h e", e=128), [], [VAs.b])
        dma(VBs.t[0:NS, 16, :], v_sd[t0:t1, 512:1024], [], [VBs.b])
        dma(qsa.t[:, :, :], qv[:, 0:4, t0:t1], [], [qsa.b])
        dma(qsb.t[:, :, :], qv[:, 8:12, t0:t1], [], [qsb.b])
        blocks = [(kb, 128, 0, False) for kb in range(16)] + [(16, NS, 0, False)]
        for h in range(4):
            ost = A["ost"][A["oi"] % 2]; A["oi"] += 1
            fin = self.a_tile(A, lambda c, kb, nk, h=h: KTa.t[64 * c:64 * c + 64, h, kb * 128:kb * 128 + nk], KTa.b,
                              lambda kb, nk, h=h: VAs.t[0:nk, kb, h, 0:129], VAs.b,
                              blocks, NS, Tv(qsa.t[:, h, :], qsa.b), [(0, 0, NS, 16)], ost, gsub, nlam, zer, bconst)
            fin()
            dma(oa_sd[t0:t1, h * 128:(h + 1) * 128], ost.t[0:NS, 0, :], [ost.b], [], eng="pool")
        blocks_b = [(16, NS, 0, True)] + [(kb, 128, 0, False) for kb in range(15, -1, -1)]
        for p in range(4):
            obs = A["obs"][A["oi"] % 2]; A["oi"] += 1
            tl_ = self.b_tile(A, lambda h2, kb, nk, p=p: KTb.t[64 * h2:64 * h2 + 64, p, kb * 128:kb * 128 + nk], KTb.b,
                              lambda h2, kb, nk, p=p: VBs.t[0:nk, kb, p * 128 + 64 * h2:p * 128 + 64 * h2 + 64], VBs.b,
                              blocks_b, NS, Tv(qsb.t[:, p, :], qsb.b), obs, triP, ones, maskT, bconst)
            for f in tl_:
                f()
            dma(obT_sd[p][:, t0:t1], obs.t[:, 0:NS], [obs.b], [], eng="pool")


def _phase3(self, es, T, PT, w_in, wba_d, wbb_d, wo_d, xp, xs, oa_d, obT_d, oa_sd, obT_sd, y_p, y_s,
            Abc, shbc, gtbc, fgbc, bconst):
    mm, tr, act, tt, stt, cp, dma = self.mm, self.tr, self.act, self.tt, self.stt, self.cp, self.dma
    NT = self.NT
    L = self.alloc_norm(es, T, PT)
    wg = T(es, "wg", [128, 8, 3072], BF16)
    wba = T(es, "wba", [128, 4, D], BF16); wbb = T(es, "wbb", [128, 4, D], BF16)
    wo = T(es, "wo", [128, 8, D], BF16)
    wst = [T(es, "wst3%d" % i, [128, 4, 512], F32) for i in range(2)]
    oat = [T(es, "oat%d" % i, [128, 512], F32) for i in range(2)]
    obt = [T(es, "obt%d" % i, [128, 4, 128], F32) for i in range(2)]
    sza = T(es, "sza", [128, 512], F32); u1 = T(es, "u1", [128, 512], F32); ua = T(es, "ua", [128, 512], BF16)
    szb = T(es, "szb", [128, 4, 128], F32); u2 = T(es, "u2", [128, 4, 128], F32); ubT = T(es, "ubT", [128, 4, 128], BF16)
    sga = T(es, "sga", [128, D], F32); sgb = T(es, "sgb", [128, D], F32)
    uaT = T(es, "uaT", [128, 4, 128], BF16)
    m1 = T(es, "m1", [128, D], F32); m2 = T(es, "m2", [128, D], F32); mb = T(es, "mb", [128, D], BF16)
    mT = T(es, "mT", [128, 8, 128], BF16)
    ys = [T(es, "ys%d" % i, [128, D], F32) for i in range(2)]
    sm3 = [T(es, "sm3%d" % i, [128, 8], F32) for i in range(4)]
    ZA = PT(es, "ZA", [128, 512], F32); ZB = PT(es, "ZB", [128, 4, 128], F32)
    WA = PT(es, "WA", [128, 2, 512], F32); WB = PT(es, "WB", [128, 2, 512], F32)
    TP = L["TP"]
    flat = "p a b -> p (a b)"

    wi_v = w_in.rearrange("(j p) n -> p j n", p=128)
    k = 0
    srcs = [1536, 3584, 4096, 4608, 5120, 5632]
    for g in range(6):
        for half in range(2):
            w_ = wst[k % 2]; k += 1
            dma(w_.t[:], wi_v[:, 4 * half:4 * half + 4, srcs[g]:srcs[g] + 512], [], [w_.b])
            cp("dve" if k % 2 else "pool", wg.t[:, 4 * half:4 * half + 4, g * 512:(g + 1) * 512], w_.t[:], [w_.b], [wg.b])
    for wd, wt in ((wba_d, wba), (wbb_d, wbb)):
        wv = wd.rearrange("(c p) n -> p c n", p=128)
        for nh in range(2):
            w_ = wst[k % 2]; k += 1
            dma(w_.t[:], wv[:, :, nh * 512:(nh + 1) * 512], [], [w_.b])
            cp("dve" if k % 2 else "pool", wt.t[:, :, nh * 512:(nh + 1) * 512], w_.t[:], [w_.b], [wt.b])
    wv = wo_d.rearrange("(j p) n -> p j n", p=128)
    for half in range(2):
        for nh in range(2):
            w_ = wst[k % 2]; k += 1
            dma(w_.t[:], wv[:, 4 * half:4 * half + 4, nh * 512:(nh + 1) * 512], [], [w_.b])
            cp("dve" if k % 2 else "pool", wo.t[:, 4 * half:4 * half + 4, nh * 512:(nh + 1) * 512], w_.t[:], [w_.b], [wo.b])

    tiles = []
    obT_v = obT_d.rearrange("u p t -> p u t")
    for t in range(NT):
        r0 = t * 128
        tiles.append(dict(P=128, x=xp[r0:r0 + 128, :], mod=0, oa=oa_d[r0:r0 + 128, :], ob=obT_v[:, :, r0:r0 + 128],
                          y=y_p[r0:r0 + 128, :]))
    if self.with_sample:
        tiles.append(dict(P=32, x=xs, mod=1, oa=oa_sd, ob=obT_sd.rearrange("u p t -> p u t"), y=y_s))
    n = len(tiles)

    def load(i):
        tl = tiles[i]; P = tl["P"]; s = i % 2
        dma(L["xt"][s].t[0:P, :], tl["x"], [], [L["xt"][s].b])
        dma(oat[s].t[0:P, :], tl["oa"], [], [oat[s].b])
        dma(obt[s].t[:, :, 0:P], tl["ob"], [], [obt[s].b])

    def head(i):
        tl = tiles[i]; P = tl["P"]; s = i % 2
        hT = L["hT"][s]
        self.pe_hT(L, s, P)
        for j in range(8):
            mm(ZA.t[0:P, :], hT.t[:, j, 0:P], wg.t[:, j, 0:512], j == 0, j == 7, [hT.b, wg.b], [ZA.b])
        for fc in range(4):
            for j in range(8):
                mm(ZB.t[:, fc, 0:P], wg.t[:, j, 512 + fc * 128:512 + (fc + 1) * 128], hT.t[:, j, 0:P], j == 0, j == 7,
                   [hT.b, wg.b], [ZB.b])
        for nh in range(2):
            for j in range(8):
                mm(WB.t[0:P, nh, :], hT.t[:, j, 0:P], wg.t[:, j, 2048 + nh * 512:2048 + (nh + 1) * 512], j == 0, j == 7,
                   [hT.b, wg.b], [WB.b])

    def head_b(i):
        tl = tiles[i]; P = tl["P"]; s = i % 2
        hT = L["hT"][s]
        for nh in range(2):
            for j in range(8):
                mm(WA.t[0:P, nh, :], hT.t[:, j, 0:P], wg.t[:, j, 1024 + nh * 512:1024 + (nh + 1) * 512], j == 0, j == 7,
                   [hT.b, wg.b], [WA.b])

    def mid(i):
        tl = tiles[i]; P = tl["P"]; s = i % 2
        act(sza.t[0:P, :], ZA.t[0:P, :], AF.Sigmoid, [ZA.b], [sza.b])
        tt("dve", u1.t[0:P, :], ZA.t[0:P, :], sza.t[0:P, :], ALU.mult, [ZA.b, sza.b], [u1.b])
        tt("dve", ua.t[0:P, :], u1.t[0:P, :], oat[s].t[0:P, :], ALU.mult, [u1.b, oat[s].b], [ua.b])
        act(szb.t[:, :, 0:P], ZB.t[:, :, 0:P], AF.Sigmoid, [ZB.b], [szb.b])
        tt("dve", u2.t[:, :, 0:P], ZB.t[:, :, 0:P], szb.t[:, :, 0:P], ALU.mult, [ZB.b, szb.b], [u2.b])
        tt("dve", ubT.t[:, :, 0:P], u2.t[:, :, 0:P], obt[s].t[:, :, 0:P], ALU.mult, [u2.b, obt[s].b], [ubT.b])
        act(sgb.t[0:P, :], WB.t[0:P, :, :].rearrange(flat), AF.Sigmoid, [WB.b], [sgb.b])
        act(sga.t[0:P, :], WA.t[0:P, :, :].rearrange(flat), AF.Sigmoid, [WA.b], [sga.b])
        for c in range(4):
            tr(TP.t[:, c, 0:P], ua.t[0:P, c * 128:(c + 1) * 128], P, [ua.b], [TP.b])
        cp("act", uaT.t[:, :, 0:P], TP.t[:, 0:4, 0:P], [TP.b], [uaT.b])
        for nh in range(2):
            for c in range(4):
                mm(WB.t[0:P, nh, :], ubT.t[:, c, 0:P], wbb.t[:, c, nh * 512:(nh + 1) * 512], c == 0, c == 3, [ubT.b, wbb.b], [WB.b])
        for nh in range(2):
            for c in range(4):
                mm(WA.t[0:P, nh, :], uaT.t[:, c, 0:P], wba.t[:, c, nh * 512:(nh + 1) * 512], c == 0, c == 3, [uaT.b, wba.b], [WA.b])
        tt("dve", m2.t[0:P, :], WB.t[0:P, :, :].rearrange(flat), sgb.t[0:P, :], ALU.mult, [WB.b, sgb.b], [m2.b])
        tt("dve", m1.t[0:P, :], WA.t[0:P, :, :].rearrange(flat), sga.t[0:P, :], ALU.mult, [WA.b, sga.b], [m1.b])
        tt("dve", mb.t[0:P, :], m1.t[0:P, :], m2.t[0:P, :], ALU.add, [m1.b, m2.b], [mb.b])
        for j in range(8):
            tr(TP.t[:, j, 0:P], mb.t[0:P, j * 128:(j + 1) * 128], P, [mb.b], [TP.b])
        cp("act", mT.t[:, :, 0:P], TP.t[:, :, 0:P], [TP.b], [mT.b])
        for nh in range(2):
            for j in range(8):
                mm(WA.t[0:P, nh, :], mT.t[:, j, 0:P], wo.t[:, j, nh * 512:(nh + 1) * 512], j == 0, j == 7, [mT.b, wo.b], [WA.b])

    def tail_a(i):
        tl = tiles[i]; P = tl["P"]; md = tl["mod"]
        tt("dve", m1.t[0:P, :], WA.t[0:P, :, :].rearrange(flat), gtbc[md].t[0:P, :], ALU.mult, [WA.b, gtbc[md].b], [m1.b])

    def tail(i):
        tl = tiles[i]; P = tl["P"]; s = i % 2; md = tl["mod"]
        xt = L["xt"][s]
        tt("dve", m2.t[0:P, :], m1.t[0:P, :], xt.t[0:P, :], ALU.add, [m1.b, xt.b], [m2.b])
        sm = sm3[i % 4]; sqj = L["sqj"]
        act(sqj.t[0:P, :], m2.t[0:P, :], AF.Square, [m2.b], [sqj.b, sm.b], accum=sm.t[0:P, 0:1])
        self.rstd(sm.t[0:P, 0:1], sm.t[0:P, 1:2], sm.t[0:P, 2:3], sm.t[0:P, 3:4], 1.0 / D, sm.b, sm.b, sm.b)
        stt("dve", ys[s].t[0:P, :], m2.t[0:P, :], sm.t[0:P, 3:4], fgbc.t[0:P, :], ALU.mult, ALU.mult, [m2.b, sm.b, fgbc.b], [ys[s].b])
        dma(tl["y"], ys[s].t[0:P, :], [ys[s].b], [], eng="pool")

    load(0)
    if n > 1:
        load(1)
    self.norm(L, 0, tiles[0]["P"], Abc[tiles[0]["mod"]], shbc[tiles[0]["mod"]])
    head(0); head_b(0)
    for i in range(n):
        if i + 1 < n:
            self.norm(L, (i + 1) % 2, tiles[i + 1]["P"], Abc[tiles[i + 1]["mod"]], shbc[tiles[i + 1]["mod"]])
        mid(i)
        if i + 1 < n:
            head(i + 1)
        tail_a(i)
        if i + 1 < n:
            head_b(i + 1)
        tail(i)
        if i + 2 < n:
            load(i + 2)


KB.sample_attn = _sample_attn
KB.phase3 = _phase3


def _rope_tables(pos):
    half = 32
    inv = (np.float32(10000.0) ** (-np.arange(half, dtype=np.float32) / np.float32(half))).astype(np.float32)
    ang = pos.astype(np.float32)[:, None] * inv[None, :]
    cos = np.cos(ang).astype(np.float32); sin = np.sin(ang).astype(np.float32)
    return np.tile(cos, (1, 8)), np.tile(sin, (1, 8))


_CACHE = {}


def run(inputs, NT, n_cores, with_sample=True, trace=False):
    key = (NT, with_sample)
    if key not in _CACHE:
        kb = KB(NT, with_sample)
        kb.build()
        _CACHE[key] = kb
    kb = _CACHE[key]
    S = NT * 128
    bf = ml_dtypes.bfloat16
    f32 = np.float32
    g = lambda k: np.ascontiguousarray(np.asarray(inputs[k], dtype=f32))
    xP, xS, cP, cS = g("x_prompt"), g("x_sample"), g("c_prompt"), g("c_sample")
    cos_p, sin_p = _rope_tables(np.arange(S))
    pos_s = PAST + np.tile(np.arange(NS), 2)
    cos_s, sin_s = _rope_tables(pos_s)
    idx = np.arange(128)
    consts = dict(
        ident=np.eye(128, dtype=f32).astype(bf),
        triP=(idx[:, None] >= idx[None, :]).astype(f32).astype(bf),
        ident32=np.eye(128, dtype=f32),
        mask_lt=(idx[:, None] < idx[None, :]).astype(f32),
        cos_p=cos_p, sin_p=sin_p, cos_s=cos_s, sin_s=sin_s,
        w_ada=g("w_ada")[0], b_ada=g("b_ada")[0], norm_g=g("norm_g")[0], w_in=g("w_in")[0],
        lams=np.concatenate([g("lambda_q1")[0], g("lambda_k1")[0], g("lambda_q2")[0], g("lambda_k2")[0]]),
        subln_g=g("subln_g")[0], wba=g("w_branch_a")[0], wbb=g("w_branch_b")[0], w_out=g("w_out")[0],
        final_g=g("final_g"),
    )
    cak, cav, cbk, cbv = (g(k)[0].reshape(-1, PAST, 512) for k in ("cache_a_k", "cache_a_v", "cache_b_k", "cache_b_v"))
    in_maps = []
    for i in range(n_cores):
        m = dict(consts)
        m["xp"] = xP[i]
        m["xs"] = np.ascontiguousarray(xS[2 * i:2 * i + 2].reshape(32, D))
        cm = cP[i].reshape(8, 128).T
        m["cmat_p"] = np.ascontiguousarray(np.broadcast_to(cm[:, :, None], (128, 8, 128)))
        cs = cS[2 * i:2 * i + 2].reshape(2, 8, 128).transpose(2, 1, 0)
        m["cmat_s"] = np.ascontiguousarray(np.repeat(cs, NS, axis=2))
        for nm_, arr in (("cak", cak), ("cav", cav), ("cbk", cbk), ("cbv", cbv)):
            m[nm_] = np.ascontiguousarray(arr[2 * i:2 * i + 2])
        in_maps.append(m)
    res = run_bass_kernel_spmd(kb.nc, in_maps, core_ids=list(range(n_cores)), trace=trace)
    return res


def kernel(**inputs):
    NT = 64
    res = run(inputs, NT, 8)
    r = res.results
    S = NT * 128
    cat = lambda k: np.stack([r[i][k] for i in range(8)], 0)
    cats = lambda k: np.concatenate([r[i][k].reshape(2, NS, -1) for i in range(8)], 0)
    y_prompt = cat("y_p")
    y_sample = cats("y_s")
    pak = cat("pak").reshape(1, 8, S, 4, 2, 64)
    pav = cat("pav").reshape(1, 8, S, 4, 128)
    pbk = cat("pbk").reshape(1, 8, S, 8, 64)
    pbv = cat("pbv").reshape(1, 8, S, 8, 64)
    sak = cats("sak").reshape(1, 16, NS, 4, 2, 64)
    sav = cats("sav").reshape(1, 16, NS, 4, 128)
    sbk = cats("sbk").reshape(1, 16, NS, 8, 64)
    sbv = cats("sbv").reshape(1, 16, NS, 8, 64)
    return (y_prompt, y_sample, pak, pav, pbk, pbv, sak, sav, sbk, sbv)
```

```python
import contextlib
import numpy as np
import ml_dtypes
import concourse.bass as bass
import concourse.mybir as mybir
from concourse.bass_utils import run_bass_kernel_spmd

F32 = mybir.dt.float32
BF16 = mybir.dt.bfloat16
AF = mybir.ActivationFunctionType
ALU = mybir.AluOpType

ENGS = ("pe", "act", "dve", "pool", "sp")
ND_SEMS = 24


class B:
    __slots__ = ("name", "last_w", "readers")

    def __init__(self, name=""):
        self.name = name
        self.last_w = None
        self.readers = []


class Prog:
    def __init__(self, nc, sems, state):
        self.nc = nc
        self.sems = sems
        self.st = state
        self.ops = {e: [] for e in ENGS}
        self.touched = set()

    def add(self, eng, fn, reads=(), writes=(), dma=False):
        ops = self.ops[eng]
        idx = len(ops)
        deps = {}
        self.touched.update(reads)
        self.touched.update(writes)
        for b in reads:
            if b.last_w is not None:
                deps[b.last_w] = deps.get(b.last_w, 0) | 1
        for b in writes:
            if b.last_w is not None:
                deps[b.last_w] = deps.get(b.last_w, 0) | 2
            for r in b.readers:
                deps[r] = deps.get(r, 0) | 4
        if dma:
            did = self.st["dma_id"]
            self.st["dma_id"] += 1
            ev = ("dma", did)
        else:
            did = None
            ev = (eng, idx)
        deps.pop(ev, None)
        for b in reads:
            b.readers.append(ev)
        for b in writes:
            b.last_w = ev
            b.readers = []
        ops.append({"fn": fn, "deps": deps, "dma": did, "sig": False})
        return ev

    def finish(self):
        nc = self.nc
        st = self.st
        ops = self.ops
        for e in ENGS:
            for op in ops[e]:
                for (pe_, pidx), kind in op["deps"].items():
                    if pe_ == "dma":
                        continue
                    if pe_ == e and e == "pe":
                        continue
                    ops[pe_][pidx]["sig"] = True
        for e in ENGS:
            if e != "sp" and ops[e]:
                ops[e][-1]["sig"] = True
        cnt = {}
        for e in ENGS:
            c = st["sig"][e]
            for i, op in enumerate(ops[e]):
                if op["sig"] and op["dma"] is None:
                    c += 1
                cnt[(e, i)] = c
            st["sig_end"] = st.get("sig_end", {})
            st["sig_end"][e] = c
        final_cnt = dict(st["sig_end"])
        dma_first = st["dma_id"] - sum(1 for e in ENGS for op in ops[e] if op["dma"] is not None)
        dma_last = st["dma_id"]

        def dma_target(did):
            return did % ND_SEMS, 16 * (did // ND_SEMS + 1)

        sems = self.sems
        waited = st["waited"]

        def emit_engine(e, engobj):
            w = waited[e]

            def wait(key, semh, val):
                if w.get(key, 0) >= val:
                    return
                engobj.wait_ge(semh, val)
                w[key] = val

            for i, op in enumerate(ops[e]):
                need = {}
                for (pe_, pidx), kind in op["deps"].items():
                    if pe_ == "dma":
                        si, val = dma_target(pidx)
                        key = ("dma", si)
                        need[key] = max(need.get(key, 0), val)
                    else:
                        if pe_ == e and e == "pe":
                            continue
                        need[pe_] = max(need.get(pe_, 0), cnt[(pe_, pidx)])
                if op["dma"] is not None and op["dma"] >= ND_SEMS:
                    si, val = dma_target(op["dma"] - ND_SEMS)
                    key = ("dma", si)
                    need[key] = max(need.get(key, 0), val)
                for key, val in need.items():
                    if isinstance(key, tuple):
                        wait(key, sems["dma"][key[1]], val)
                    else:
                        wait(key, sems[key], val)
                ins = op["fn"](engobj)
                if op["dma"] is not None:
                    si, _ = dma_target(op["dma"])
                    ins.then_inc(sems["dma"][si], 16)
                elif op["sig"]:
                    ins.then_inc(sems[e], 1)
            for pe_ in ENGS:
                if pe_ == "sp" or pe_ == e:
                    continue
                if final_cnt[pe_] > 0:
                    wait(pe_, sems[pe_], final_cnt[pe_])
            for did in range(max(dma_first, dma_last - ND_SEMS), dma_last):
                si, val = dma_target(did)
                wait(("dma", si), sems["dma"][si], val)

        with nc.Block() as block:
            @block.tensor
            def _(eng):
                emit_engine("pe", eng)

            @block.scalar
            def _(eng):
                emit_engine("act", eng)

            @block.vector
            def _(eng):
                emit_engine("dve", eng)

            @block.gpsimd
            def _(eng):
                emit_engine("pool", eng)

            @block.sync
            def _(eng):
                emit_engine("sp", eng)

        for e in ENGS:
            st["sig"][e] = final_cnt[e]
        for b in self.touched:
            b.last_w = None
            b.readers = []


def new_state():
    return {"dma_id": 0, "sig": {e: 0 for e in ENGS}, "waited": {e: {} for e in ENGS}}


D = 1024
EPS = 1e-6
LAM_INIT = 0.2
PAST = 2048
NS = 16


class KB:
    def __init__(self, NT, with_sample=True):
        self.NT = NT
        self.S = NT * 128
        self.with_sample = with_sample
        self.nc = bass.Bass("TRN2", target_bir_lowering=False)
        self.st = new_state()
        self.uid = 0

    def din(self, name, shape, dt=F32):
        return self.nc.dram_tensor(name, list(shape), dt, kind="ExternalInput").ap()

    def dout(self, name, shape, dt=F32):
        return self.nc.dram_tensor(name, list(shape), dt, kind="ExternalOutput").ap()

    def dscr(self, name, shape, dt):
        return self.nc.dram_tensor(name, list(shape), dt).ap()

    def sb(self, es, name, shape, dt):
        self.uid += 1
        return es.enter_context(self.nc.sbuf_tensor("%s_%d" % (name, self.uid), list(shape), dt))

    def ps(self, es, name, shape, dt):
        self.uid += 1
        return es.enter_context(self.nc.psum_tensor("%s_%d" % (name, self.uid), list(shape), dt))

    def mm(self, out, lhsT, rhs, start, stop, r, w, skip=False):
        self.pg.add("pe", lambda e: e.matmul(out, lhsT=lhsT, rhs=rhs, start=start, stop=stop,
                                             skip_group_check=skip), r, w)

    def tr(self, out, in_, P, r, w):
        ident = self.ident
        self.pg.add("pe", lambda e: e.transpose(out=out, in_=in_, identity=ident[0:P, 0:P]), r, w)

    def tr32(self, out, in_, P, r, w):
        ident = self.ident32
        self.pg.add("pe", lambda e: e.transpose(out=out, in_=in_, identity=ident[0:P, 0:P]), r, w)

    def act(self, out, in_, func, r, w, scale=1.0, bias=0.0, accum=None):
        if accum is None:
            self.pg.add("act", lambda e: e.activation(out=out, in_=in_, func=func, bias=bias, scale=scale), r, w)
        else:
            self.pg.add("act", lambda e: e.activation(out=out, in_=in_, func=func, bias=bias, scale=scale,
                                                      accum_out=accum), r, w)

    def tt(self, eng, out, a, b, op, r, w):
        self.pg.add(eng, lambda e: e.tensor_tensor(out=out, in0=a, in1=b, op=op), r, w)

    def ts(self, eng, out, a, s1, s2, op0, op1, r, w):
        if s2 is None:
            self.pg.add(eng, lambda e: e.tensor_scalar(out=out, in0=a, scalar1=s1, scalar2=None, op0=op0), r, w)
        else:
            self.pg.add(eng, lambda e: e.tensor_scalar(out=out, in0=a, scalar1=s1, scalar2=s2, op0=op0, op1=op1), r, w)

    def stt(self, eng, out, a, s, b, op0, op1, r, w, accum=None):
        if accum is None:
            self.pg.add(eng, lambda e: e.scalar_tensor_tensor(out=out, in0=a, scalar=s, in1=b, op0=op0, op1=op1), r, w)
        else:
            self.pg.add(eng, lambda e: e.scalar_tensor_tensor(out=out, in0=a, scalar=s, in1=b, op0=op0, op1=op1,
                                                              accum_out=accum), r, w)

    def cp(self, eng, out, in_, r, w):
        if eng == "act":
            self.pg.add("act", lambda e: e.activation(out=out, in_=in_, func=AF.Copy), r, w)
        else:
            self.pg.add(eng, lambda e: e.tensor_copy(out=out, in_=in_), r, w)

    def rcp(self, out, in_, r, w):
        self.pg.add("dve", lambda e: e.reciprocal(out=out, in_=in_), r, w)

    def mset(self, eng, ap, val, w):
        self.pg.add(eng, lambda e: e.memset(ap, val), (), w)

    def dma(self, out, in_, r, w, eng="sp"):
        self.pg.add(eng, lambda e: e.dma_start(out=out, in_=in_), r, w, dma=True)

    def new_prog(self):
        self.pg = Prog(self.nc, self.sems, self.st)

    def rstd(self, ss, tmpa, tmpb, out, inv_n, bss, btmp, bout):
        self.ts("dve", tmpa, ss, inv_n, EPS, ALU.mult, ALU.add, [bss], [btmp])
        self.act(tmpb, tmpa, AF.Ln, [btmp], [btmp])
        self.act(out, tmpb, AF.Exp, [btmp], [bout], scale=-0.5)


class Tl:
    __slots__ = ("t", "b")

    def __init__(self, t):
        self.t = t
        self.b = B()


def _build(self):
    nc = self.nc
    S, NT = self.S, self.NT
    NKB = NT
    NQ = NT // 4
    din, dout, dscr = self.din, self.dout, self.dscr
    xp = din("xp", [S, D]); xs = din("xs", [32, D])
    cmat_p = din("cmat_p", [128, 8, 128]); cmat_s = din("cmat_s", [128, 8, 32])
    w_ada = din("w_ada", [D, 3 * D]); b_ada = din("b_ada", [3 * D]); norm_g = din("norm_g", [D])
    w_in = din("w_in", [D, 6144]); lams = din("lams", [256]); subln_g = din("subln_g", [128])
    wba_d = din("wba", [512, D]); wbb_d = din("wbb", [512, D]); wo_d = din("w_out", [D, D]); final_g = din("final_g", [D])
    cak = din("cak", [2, PAST, 512]); cav = din("cav", [2, PAST, 512])
    cbk = din("cbk", [2, PAST, 512]); cbv = din("cbv", [2, PAST, 512])
    cos_p = din("cos_p", [S, 256]); sin_p = din("sin_p", [S, 256])
    cos_s = din("cos_s", [32, 256]); sin_s = din("sin_s", [32, 256])
    ident_d = din("ident", [128, 128], BF16); triP_d = din("triP", [128, 128], BF16)
    id32_d = din("ident32", [128, 128]); mask_d = din("mask_lt", [128, 128])
    y_p = dout("y_p", [S, D]); y_s = dout("y_s", [32, D])
    pak = dout("pak", [S, 512]); pav = dout("pav", [S, 512]); pbk = dout("pbk", [S, 512]); pbv = dout("pbv", [S, 512])
    sak = dout("sak", [32, 512]); sav = dout("sav", [32, 512]); sbk = dout("sbk", [32, 512]); sbv = dout("sbv", [32, 512])
    qkT_d = dscr("qkT_d", [16, 128, S], BF16); v_d = dscr("v_d", [S, 1024], BF16)
    oa_d = dscr("oa_d", [S, 512], F32); obT_d = dscr("obT_d", [4, 128, S], F32)
    qkT_sd = dscr("qkT_sd", [16, 128, 32], BF16); v_sd = dscr("v_sd", [32, 1024], BF16)
    oa_sd = dscr("oa_sd", [32, 512], F32); obT_sd = dscr("obT_sd", [4, 128, 32], F32)

    mm, tr, act, tt, ts, stt, cp, rcp, mset, dma = (self.mm, self.tr, self.act, self.tt, self.ts, self.stt,
                                                    self.cp, self.rcp, self.mset, self.dma)

    with contextlib.ExitStack() as top:
        self.sems = {e: top.enter_context(nc.semaphore("s_" + e)) for e in ENGS if e != "sp"}
        self.sems["dma"] = [top.enter_context(nc.semaphore("s_dma%d" % i)) for i in range(ND_SEMS)]

        def T(es, name, shape, dt):
            return Tl(self.sb(es, name, shape, dt))

        def PT(es, name, shape, dt):
            return Tl(self.ps(es, name, shape, dt))

        Abc = [T(top, "Abc%d" % i, [128, D], F32) for i in range(2)]
        shbc = [T(top, "shbc%d" % i, [128, D], F32) for i in range(2)]
        gtbc = [T(top, "gtbc%d" % i, [128, D], F32) for i in range(2)]
        fgbc = T(top, "fgbc", [128, D], F32)
        gsub = T(top, "gsub", [128, 128], F32)
        nlam = T(top, "nlam", [128, 1], F32)
        identT = T(top, "ident", [128, 128], BF16); self.ident = identT.t
        id32T = T(top, "ident32", [128, 128], F32); self.ident32 = id32T.t
        triP = T(top, "triP", [128, 128], BF16); onesT = T(top, "onesT", [128, 128], mybir.dt.float32r); ones32 = T(top, "ones32", [128, 128], F32)
        zf = T(top, "zf", [128, 2, 512], F32); self.zf = zf
        maskT = T(top, "mask", [128, 128], F32)
        zer = T(top, "zer", [128, 512], BF16)
        bconst = B()

        with contextlib.ExitStack() as es01:
            wq = T(es01, "wq", [128, 8, 3072], BF16)
            with contextlib.ExitStack() as es0:
                self.new_prog()
                wst = [T(es0, "wst%d" % i, [128, 4, 512], F32) for i in range(2)]
                bada = T(es0, "bada", [128, 3 * D], F32)
                ngbc = T(es0, "ngbc", [128, D], F32)
                modt = [T(es0, "mod%d" % i, [128, 3 * D], F32) for i in range(2)]
                cm = [T(es0, "cm0", [128, 8, 128], F32), T(es0, "cm1", [128, 8, 32], F32)]
                lamv = T(es0, "lamv", [128, 256], F32)
                sm0 = T(es0, "sm0", [128, 8], F32)
                j64 = T(es0, "j64", [128, 64], F32)
                graw = T(es0, "graw", [128, 128], F32)
                MOD = [PT(es0, "MOD%d" % i, [128, 512], F32) for i in range(2)]

                dma(identT.t[:], ident_d, [], [bconst]); dma(triP.t[:], triP_d, [], [bconst]); dma(id32T.t[:], id32_d, [], [bconst])
                mset("pool", ones32.t[:], 1.0, [ones32.b]); cp("dve", onesT.t[:], ones32.t[:], [ones32.b], [bconst])
                mset("pool", zf.t[:], 0.0, [bconst]); dma(maskT.t[:], mask_d, [], [bconst])
                mset("pool", zer.t[:], 0.0, [zer.b])
                dma(bada.t[:], b_ada.partition_broadcast(128), [], [bada.b])
                dma(ngbc.t[:], norm_g.partition_broadcast(128), [], [ngbc.b])
                dma(fgbc.t[:], final_g.partition_broadcast(128), [], [fgbc.b])
                dma(graw.t[:], subln_g.partition_broadcast(128), [], [graw.b])
                dma(lamv.t[:], lams.partition_broadcast(128), [], [lamv.b])
                dma(cm[0].t[:], cmat_p, [], [cm[0].b]); dma(cm[1].t[:], cmat_s, [], [cm[1].b])
                stt("dve", j64.t[:], lamv.t[:, 0:64], 1.0, lamv.t[:, 64:128], ALU.mult, ALU.mult, [lamv.b], [j64.b, sm0.b], accum=sm0.t[:, 0:1])
                stt("dve", j64.t[:], lamv.t[:, 128:192], 1.0, lamv.t[:, 192:256], ALU.mult, ALU.mult, [lamv.b, sm0.b], [j64.b, sm0.b], accum=sm0.t[:, 1:2])
                act(sm0.t[:, 2:4], sm0.t[:, 0:2], AF.Exp, [sm0.b], [sm0.b])
                tt("dve", sm0.t[:, 4:5], sm0.t[:, 2:3], sm0.t[:, 3:4], ALU.subtract, [sm0.b], [sm0.b])
                ts("dve", nlam.t[:], sm0.t[:, 4:5], LAM_INIT, -1.0, ALU.add, ALU.mult, [sm0.b], [nlam.b])
                ts("dve", gsub.t[:], graw.t[:], 1.0 - LAM_INIT, None, ALU.mult, None, [graw.b], [gsub.b])
                wa_v = w_ada.rearrange("(j p) n -> p j n", p=128)
                k = 0
                for g in range(6):
                    for half in range(2):
                        w_ = wst[k % 2]; k += 1
                        dma(w_.t[:], wa_v[:, 4 * half:4 * half + 4, g * 512:(g + 1) * 512], [], [w_.b])
                        for jj in range(4):
                            j = 4 * half + jj
                            mm(MOD[0].t[:, :], cm[0].t[:, j, :], w_.t[:, jj, :], j == 0, j == 7, [cm[0].b, w_.b], [MOD[0].b])
                            mm(MOD[1].t[0:32, :], cm[1].t[:, j, :], w_.t[:, jj, :], j == 0, j == 7, [cm[1].b, w_.b], [MOD[1].b])
                    cs = slice(g * 512, (g + 1) * 512)
                    tt("dve", modt[0].t[:, cs], MOD[0].t[:, :], bada.t[:, cs], ALU.add, [MOD[0].b, bada.b], [modt[0].b])
                    tt("dve", modt[1].t[0:32, cs], MOD[1].t[0:32, :], bada.t[0:32, cs], ALU.add, [MOD[1].b, bada.b], [modt[1].b])
                for i, P in ((0, 128), (1, 32)):
                    stt("dve", Abc[i].t[0:P, :], modt[i].t[0:P, D:2 * D], 1.0, ngbc.t[0:P, :], ALU.add, ALU.mult, [modt[i].b, ngbc.b], [Abc[i].b])
                    cp("pool", shbc[i].t[0:P, :], modt[i].t[0:P, 0:D], [modt[i].b], [shbc[i].b])
                    cp("pool", gtbc[i].t[0:P, :], modt[i].t[0:P, 2 * D:3 * D], [modt[i].b], [gtbc[i].b])
                wi_v = w_in.rearrange("(j p) n -> p j n", p=128)
                qkv_cols = [0, 512, 1024, 2048, 2560, 3072]
                for g in range(6):
                    for half in range(2):
                        w_ = wst[k % 2]; k += 1
                        dma(w_.t[:], wi_v[:, 4 * half:4 * half + 4, qkv_cols[g]:qkv_cols[g] + 512], [], [w_.b])
                        cp("dve" if k % 2 else "pool", wq.t[:, 4 * half:4 * half + 4, g * 512:(g + 1) * 512], w_.t[:], [w_.b], [wq.b])
                self.pg.finish()

            with contextlib.ExitStack() as es1:
                self.new_prog()
                L = self.alloc_norm(es1, T, PT)
                stage = [T(es1, "stage%d" % i, [128, 2048], F32) for i in range(2)]
                qkb = [T(es1, "qkb%d" % i, [128, 2048], BF16) for i in range(2)]
                vbf = [T(es1, "vbf%d" % i, [128, 1024], BF16) for i in range(2)]
                qkTs = [T(es1, "qkTs%d" % i, [128, 16, 128], BF16) for i in range(2)]
                cst = [T(es1, "cst%d" % i, [128, 256], F32) for i in range(2)]
                snt = [T(es1, "snt%d" % i, [128, 256], F32) for i in range(2)]
                rp = [T(es1, "rp%d" % i, [128, 512], F32) for i in range(2)]
                rt = [T(es1, "rt%d" % i, [128, 256], F32) for i in range(4)]
                PG = [PT(es1, "PG%d" % i, [128, 512], F32) for i in range(4)]
                TQ = PT(es1, "TQ", [128, 16, 128], BF16)

                tiles = []
                for t in range(NT):
                    r0 = t * 128
                    tiles.append(dict(P=128, x=xp[r0:r0 + 128, :], mod=0, cos=cos_p[r0:r0 + 128, :], sin=sin_p[r0:r0 + 128, :],
                                      outs=[pak[r0:r0 + 128, :], pav[r0:r0 + 128, :], pbk[r0:r0 + 128, :], pbv[r0:r0 + 128, :]],
                                      qkT=qkT_d[:, :, r0:r0 + 128], v=v_d[r0:r0 + 128, :]))
                if self.with_sample:
                    tiles.append(dict(P=32, x=xs, mod=1, cos=cos_s, sin=sin_s, outs=[sak, sav, sbk, sbv],
                                      qkT=qkT_sd, v=v_sd))
                n = len(tiles)
                pgi = [0]

                def load(i):
                    tl = tiles[i]; P = tl["P"]; s = i % 2
                    dma(L["xt"][s].t[0:P, :], tl["x"], [], [L["xt"][s].b])
                    dma(cst[s].t[0:P, :], tl["cos"], [], [cst[s].b])
                    dma(snt[s].t[0:P, :], tl["sin"], [], [snt[s].b])

                def pe_main(i):
                    tl = tiles[i]; P = tl["P"]; s = i % 2
                    hT = L["hT"][s]
                    grp = []
                    for g in range(6):
                        pgt = PG[pgi[0] % 4]; pgi[0] += 1
                        for j in range(8):
                            mm(pgt.t[0:P, :], hT.t[:, j, 0:P], wq.t[:, j, g * 512:(g + 1) * 512], j == 0, j == 7, [hT.b, wq.b], [pgt.b])
                        grp.append(pgt)
                        self.p1_evac(g, pgt, P, s, stage[s], qkb[s], vbf[s], rp, rt, cst[s], snt[s])
                    return grp

                def tq(i):
                    tl = tiles[i]; P = tl["P"]; s = i % 2
                    for u in range(16):
                        tr(TQ.t[:, u, 0:P], qkb[s].t[0:P, u * 128:(u + 1) * 128], P, [qkb[s].b, bconst], [TQ.b])
                    cp("dve", qkTs[s].t[:, :, 0:P], TQ.t[:, :, 0:P], [TQ.b], [qkTs[s].b])
                    dma(tl["qkT"].rearrange("u p t -> p u t"), qkTs[s].t[:, :, 0:P], [qkTs[s].b], [], eng="pool")
                    for q in range(4):
                        dma(tl["outs"][q], stage[s].t[0:P, q * 512:(q + 1) * 512], [stage[s].b], [], eng="pool")
                    dma(tl["v"], vbf[s].t[0:P, :], [vbf[s].b], [], eng="pool")

                load(0)
                if n > 1:
                    load(1)
                self.norm(L, 0, tiles[0]["P"], Abc[tiles[0]["mod"]], shbc[tiles[0]["mod"]])
                for i in range(n):
                    self.pe_hT(L, i % 2, tiles[i]["P"])
                    if i + 1 < n:
                        self.norm(L, (i + 1) % 2, tiles[i + 1]["P"], Abc[tiles[i + 1]["mod"]], shbc[tiles[i + 1]["mod"]])
                    pe_main(i)
                    if i >= 1:
                        tq(i - 1)
                    if i + 2 < n:
                        load(i + 2)
                tq(n - 1)
                self.pg.finish()

        with contextlib.ExitStack() as es2:
            self.new_prog()
            A = self.alloc_attn(es2, T, PT, NKB)
            v_v = v_d.rearrange("(kb p) n -> p kb n", p=128)
            pendA = None; pendB = None
            for u in range(8):
                s = u % 2
                isA = u < 4
                KT, V = A["KT"][s], A["V"][s]
                dma(KT.t[:, 0:S], qkT_d[(4 + u) if isA else (12 + u - 4)], [], [KT.b])
                col0 = u * 128 if isA else 512 + (u - 4) * 128
                for k0 in range(0, NKB, 16):
                    k1 = min(NKB, k0 + 16)
                    dma(V.t[:, k0:k1, 0:128], v_v[:, k0:k1, col0:col0 + 128], [], [V.b])
                if u < 2:
                    mset("pool", V.t[:, :, 128:129], 1.0, [V.b])
                for Tq in range(NQ):
                    qs = A["QT"][A["qi"] % 2]; A["qi"] += 1
                    dma(qs.t[:, :], qkT_d[u if isA else (8 + u - 4)][:, Tq * 512:(Tq + 1) * 512], [], [qs.b])
                    blocks = []
                    for kb in range(4 * Tq + 4):
                        j = kb - 4 * Tq
                        blocks.append((kb, 128, 128 * j if j > 0 else 0, j >= 0))
                    if isA:
                        chunks = [(m, 128 * m, 128, 4 * Tq + m) for m in range(4)]
                        ost = A["ost"][A["oi"] % 2]; A["oi"] += 1
                        dst = oa_d.rearrange("(t m p) n -> t p m n", m=4, p=128)[Tq][:, :, u * 128:(u + 1) * 128]
                        pendA = self.a_tile(A, lambda c, kb, nk, KT=KT: KT.t[64 * c:64 * c + 64, kb * 128:(kb + 1) * 128], KT.b,
                                            lambda kb, nk, V=V: V.t[0:nk, kb, 0:129], V.b,
                                            blocks, 512, qs, chunks, ost, gsub, nlam, zer, bconst, prev=pendA,
                                            store=lambda dst=dst, ost=ost: dma(dst, ost.t[:, :, :], [ost.b], [], eng="pool"))
                    else:
                        if pendA is not None:
                            pendA(); pendA = None
                        obs = A["obs"][A["oi"] % 2]; A["oi"] += 1
                        dstb = obT_d[u - 4][:, Tq * 512:(Tq + 1) * 512]
                        pendB = self.b_tile(A, lambda h2, kb, nk, KT=KT: KT.t[64 * h2:64 * h2 + 64, kb * 128:(kb + 1) * 128], KT.b,
                                            lambda h2, kb, nk, V=V: V.t[0:nk, kb, 64 * h2:64 * h2 + 64], V.b,
                                            blocks[::-1], 512, qs, obs, triP, onesT, maskT, bconst, prev=pendB,
                                            store=lambda dstb=dstb, obs=obs: dma(dstb, obs.t[:, :], [obs.b], [], eng="pool"))
            if pendB:
                for f in pendB:
                    f()
            self.pg.finish()

        if self.with_sample:
            with contextlib.ExitStack() as es2s:
                self.new_prog()
                self.sample_attn(es2s, T, PT, cak, cav, cbk, cbv, qkT_sd, v_sd, oa_sd, obT_sd,
                                 gsub, nlam, zer, triP, onesT, maskT, bconst)
                self.pg.finish()

        with contextlib.ExitStack() as es3:
            self.new_prog()
            self.phase3(es3, T, PT, w_in, wba_d, wbb_d, wo_d, xp, xs, oa_d, obT_d, oa_sd, obT_sd, y_p, y_s,
                        Abc, shbc, gtbc, fgbc, bconst)
            self.pg.finish()
    return nc


KB.build = _build


class Tv:
    __slots__ = ("t", "b")

    def __init__(self, ap, b):
        self.t = ap
        self.b = b


def _alloc_norm(self, es, T, PT):
    L = dict(
        xt=[T(es, "xt%d" % i, [128, D], F32) for i in range(2)],
        tmp=[T(es, "tmp%d" % i, [128, D], F32) for i in range(2)],
        hb=[T(es, "hb%d" % i, [128, D], BF16) for i in range(2)],
        hT=[T(es, "hT%d" % i, [128, 8, 128], BF16) for i in range(2)],
        sqj=T(es, "sqj", [128, D], BF16),
        smn=[T(es, "smn%d" % i, [128, 8], F32) for i in range(4)],
        TP=PT(es, "TP", [128, 8, 128], BF16),
        ni=0,
    )
    return L


def _norm(self, L, s, P, Abc, shbc):
    xt = L["xt"][s]; tmp = L["tmp"][s]; hb = L["hb"][s]
    sm = L["smn"][L["ni"] % 4]; L["ni"] += 1
    sqj = L["sqj"]
    self.act(sqj.t[0:P, :], xt.t[0:P, :], AF.Square, [xt.b], [sqj.b, sm.b], accum=sm.t[0:P, 0:1])
    self.rstd(sm.t[0:P, 0:1], sm.t[0:P, 1:2], sm.t[0:P, 2:3], sm.t[0:P, 3:4], 1.0 / D, sm.b, sm.b, sm.b)
    self.stt("dve", tmp.t[0:P, :], xt.t[0:P, :], sm.t[0:P, 3:4], Abc.t[0:P, :], ALU.mult, ALU.mult,
             [xt.b, sm.b, Abc.b], [tmp.b])
    self.tt("dve", hb.t[0:P, :], tmp.t[0:P, :], shbc.t[0:P, :], ALU.add, [tmp.b, shbc.b], [hb.b])


def _pe_hT(self, L, s, P):
    hb = L["hb"][s]; hT = L["hT"][s]; TP = L["TP"]
    for j in range(8):
        self.tr(TP.t[:, j, 0:P], hb.t[0:P, j * 128:(j + 1) * 128], P, [hb.b], [TP.b])
    self.cp("act", hT.t[:, :, 0:P], TP.t[:, :, 0:P], [TP.b], [hT.b])


def _rope(self, src, P, dst_ap, bdst, cst, snt, rt):
    pat = "p (g two f) -> p g two f"
    sv = src.t[0:P, :].rearrange(pat, two=2, f=32)
    dv = dst_ap.rearrange(pat, two=2, f=32)
    x1, x2 = sv[:, :, 0, :], sv[:, :, 1, :]
    cv = cst.t[0:P, :].rearrange("p (g f) -> p g f", f=32)
    sn = snt.t[0:P, :].rearrange("p (g f) -> p g f", f=32)
    t = [r_.t[0:P, :].rearrange("p (g f) -> p g f", f=32) for r_ in rt]
    tt = self.tt
    tt("dve", t[0], x1, cv, ALU.mult, [src.b, cst.b], [rt[0].b])
    tt("pool", t[1], x2, sn, ALU.mult, [src.b, snt.b], [rt[1].b])
    tt("dve", dv[:, :, 0, :], t[0], t[1], ALU.subtract, [rt[0].b, rt[1].b], [bdst])
    tt("pool", t[2], x2, cv, ALU.mult, [src.b, cst.b], [rt[2].b])
    tt("dve", t[3], x1, sn, ALU.mult, [src.b, snt.b], [rt[3].b])
    tt("pool", dv[:, :, 1, :], t[2], t[3], ALU.add, [rt[2].b, rt[3].b], [bdst])


def _p1_evac(self, g, pgt, P, s, stage, qkb, vbf, rp, rt, cst, snt):
    cp = self.cp
    if g == 0:
        cp("act", rp[0].t[0:P, :], pgt.t[0:P, :], [pgt.b], [rp[0].b])
        self.rope(rp[0], P, qkb.t[0:P, 0:512], qkb.b, cst, snt, rt)
    elif g == 1:
        cp("act", rp[1].t[0:P, :], pgt.t[0:P, :], [pgt.b], [rp[1].b])
        self.rope(rp[1], P, stage.t[0:P, 0:512], stage.b, cst, snt, rt)
        cp("dve", qkb.t[0:P, 512:1024], stage.t[0:P, 0:512], [stage.b], [qkb.b])
    elif g == 2:
        cp("act", stage.t[0:P, 512:1024], pgt.t[0:P, :], [pgt.b], [stage.b])
        cp("dve", vbf.t[0:P, 0:512], stage.t[0:P, 512:1024], [stage.b], [vbf.b])
    elif g == 3:
        cp("act", qkb.t[0:P, 1024:1536], pgt.t[0:P, :], [pgt.b], [qkb.b])
    elif g == 4:
        cp("act", stage.t[0:P, 1024:1536], pgt.t[0:P, :], [pgt.b], [stage.b])
        cp("dve", qkb.t[0:P, 1536:2048], stage.t[0:P, 1024:1536], [stage.b], [qkb.b])
    else:
        cp("act", stage.t[0:P, 1536:2048], pgt.t[0:P, :], [pgt.b], [stage.b])
        cp("dve", vbf.t[0:P, 512:1024], stage.t[0:P, 1536:2048], [stage.b], [vbf.b])


def _alloc_attn(self, es, T, PT, NKB, G=8):
    A = {}
    if NKB:
        A["KT"] = [T(es, "KT%d" % i, [128, NKB * 128], BF16) for i in range(2)]
        A["V"] = [T(es, "V%d" % i, [128, NKB, 130], BF16) for i in range(2)]
    A["zf"] = self.zf
    A["G"] = G
    A["QT"] = [T(es, "QT%d" % i, [128, 512], BF16) for i in range(2)]
    A["S"] = [PT(es, "S%d" % i, [128, 2, 512], F32) for i in range(2)]
    A["Sb"] = [[B(), B()], [B(), B()]]
    A["X"] = PT(es, "X", [128, 3, 512], F32)
    A["bX"] = [B(), B(), B()]
    A["E"] = [T(es, "E%d" % i, [128, 2, 512], BF16) for i in range(3)]
    A["spg"] = [T(es, "spg%d" % i, [128, 2, 512], BF16) for i in range(A["G"])]
    A["LsF"] = [T(es, "LsF%d" % i, [128, 2, 512], mybir.dt.float32r) for i in range(A["G"] + 1)]
    A["nq"] = [T(es, "nq%d" % i, [128, 512], BF16) for i in range(2)]
    A["a"] = [T(es, "a%d" % i, [128, 2, 512], BF16) for i in range(A["G"] + 3)]

    A["ost"] = [T(es, "ost%d" % i, [128, 4, 128], F32) for i in range(2)]
    A["obs"] = [T(es, "obs%d" % i, [128, 512], F32) for i in range(2)]
    A["oo"] = [T(es, "oo%d" % i, [128, 4, 128], F32) for i in range(2)]
    A["t0"] = [T(es, "t0%d" % i, [128, 128], F32) for i in range(2)]
    A["sma"] = [T(es, "sma%d" % i, [128, 16], F32) for i in range(4)]
    A["smb"] = [T(es, "smb%d" % i, [128, 12], F32) for i in range(4)]
    A["jk"] = T(es, "jk", [128, 128], F32)
    for k in ("si", "ei", "wi", "qi", "oi", "ti", "fi", "nqi", "zi"):
        A[k] = 0
    return A


def _a_tile(self, A, kt, bkt, vv, bv, blocks, N, qs, chunks, ost, gsub, nlam, zer, bconst, prev=None, store=None):
    mm, act, tt, ts, stt, rcp, mset = self.mm, self.act, self.tt, self.ts, self.stt, self.rcp, self.mset
    nb = len(blocks)
    X, bX = A["X"], A["bX"]
    sbase, ebase = A["si"], A["ei"]
    A["si"] += nb; A["ei"] += nb
    ti = A["ti"]; A["ti"] += 1
    sm = A["sma"][ti % 4]; sm2 = A["smb"][ti % 4]; oo = A["oo"][ti % 2]; jk = A["jk"]
    qn0 = chunks[0][2]
    nm = len(chunks)

    def acc(m, c):
        a = m * 2 + c
        return X.t[:, a // 3, (a % 3) * 130:(a % 3) * 130 + 129], bX[a // 3]

    def qk(i):
        kbi, nk, c0, diag = blocks[i]
        Sl = A["S"][(sbase + i) % 2]; Sb = A["Sb"][(sbase + i) % 2]
        for c in range(2):
            mm(Sl.t[0:nk, c, c0:N], kt(c, kbi, nk), qs.t[64 * c:64 * c + 64, c0:N], True, True, [bkt, qs.b], [Sb[c]])

    def ex(i):
        kbi, nk, c0, diag = blocks[i]
        Sl = A["S"][(sbase + i) % 2]; El = A["E"][(ebase + i) % 3]; Sb = A["Sb"][(sbase + i) % 2]
        act(El.t[0:nk, :, c0:N], Sl.t[0:nk, :, c0:N], AF.Exp, Sb, [El.b], scale=0.125)
        if diag:
            mset("pool", El.t[64:128, :, c0:c0 + 64], 0.0, [El.b])

    def pv(i):
        kbi, nk, c0, diag = blocks[i]
        El = A["E"][(ebase + i) % 3]
        for c in range(2):
            for (m, q0, qn, last) in chunks:
                if q0 < c0:
                    continue
                o_ap, ob = acc(m, c)
                mm(o_ap[0:qn, :], El.t[0:nk, c, q0:q0 + qn], vv(kbi, nk), False, i == last, [El.b, bv], [ob], skip=True)

    def fin_chunk(m, q0, qn):
        (a0, b0), (a1, b1) = acc(m, 0), acc(m, 1)
        t0 = A["t0"][A["fi"] % 2]; A["fi"] += 1
        rcp(sm.t[0:qn, m:m + 1], a0[0:qn, 128:129], [b0], [sm.b])
        rcp(sm.t[0:qn, 4 + m:5 + m], a1[0:qn, 128:129], [b1], [sm.b])
        tt("dve", sm.t[0:qn, 8 + m:9 + m], sm.t[0:qn, 4 + m:5 + m], nlam.t[0:qn, :], ALU.mult, [sm.b], [sm.b])
        ts("dve", t0.t[0:qn, :], a0[0:qn, 0:128], sm.t[0:qn, m:m + 1], None, ALU.mult, None, [b0, sm.b], [t0.b])
        stt("dve", oo.t[0:qn, m, :], a1[0:qn, 0:128], sm.t[0:qn, 8 + m:9 + m], t0.t[0:qn, :], ALU.mult, ALU.add,
            [b1, sm.b, t0.b], [oo.b])
        stt("dve", jk.t[0:qn, :], oo.t[0:qn, m, :], 1.0, oo.t[0:qn, m, :], ALU.mult, ALU.mult, [oo.b], [jk.b, sm.b],
            accum=sm.t[0:qn, 12 + m:13 + m])

    qk(0)
    for b in range(3):
        mm(X.t[:, b, :], zer.t[:, 0:128], zer.t[:, :], True, False, [], [bX[b]], skip=True)
    if nb > 1:
        qk(1)
    for i in range(nb):
        ex(i)
        if i + 2 < nb:
            qk(i + 2)
        pv(i)
        if prev is not None and i == min(1, nb - 1):
            prev()
        for (m, q0, qn, last) in chunks:
            if last == i:
                fin_chunk(m, q0, qn)

    def finish():
        ts("dve", sm2.t[0:qn0, 0:nm], sm.t[0:qn0, 12:12 + nm], 1.0 / 128, EPS, ALU.mult, ALU.add, [sm.b], [sm2.b])
        act(sm2.t[0:qn0, 4:4 + nm], sm2.t[0:qn0, 0:nm], AF.Ln, [sm2.b], [sm2.b])
        act(sm2.t[0:qn0, 8:8 + nm], sm2.t[0:qn0, 4:4 + nm], AF.Exp, [sm2.b], [sm2.b], scale=-0.5)
        for (m, q0, qn, last) in chunks:
            stt("dve", ost.t[0:qn, m, :], oo.t[0:qn, m, :], sm2.t[0:qn, 8 + m:9 + m], gsub.t[0:qn, :], ALU.mult, ALU.mult,
                [oo.b, sm2.b], [ost.b])
        if store is not None:
            store()
    return finish


def _b_tile(self, A, kt, bkt, vv, bv, blocks, N, qs, obs, triP, ones, maskT, bconst, prev=None, store=None):
    mm, act, tt, cp = self.mm, self.act, self.tt, self.cp
    nb = len(blocks)
    G = A["G"]
    X, bX = A["X"], A["bX"]
    S0, S1 = A["S"]; Sb = A["Sb"]
    zslots = [(X.t[:, 0, :], bX[0]), (X.t[:, 1, :], bX[1]), (S0.t[:, 0, :], Sb[0][0]), (S0.t[:, 1, :], Sb[0][1]),
              (S1.t[:, 0, :], Sb[1][0]), (S1.t[:, 1, :], Sb[1][1])]
    NZ = len(zslots)
    nq = A["nq"][A["nqi"] % 2]; A["nqi"] += 1
    LsF = A["LsF"]; R = len(LsF); zf = A["zf"]
    NA = len(A["a"])
    abase = A["wi"]; A["wi"] += nb
    cbase = A["si"]; A["si"] += nb
    zbase = A["zi"]; A["zi"] += 2 * nb
    self.ts("pool", nq.t[:, 0:N], qs.t[:, 0:N], -0.125, None, ALU.mult, None, [qs.b], [nq.b])

    def zs(i, h2):
        return zslots[(zbase + 2 * i + h2) % NZ]

    ctiles = [(S0.t, Sb[0]), (S1.t, Sb[1]), (X.t[:, 0:2, :], [bX[0], bX[1]])]

    def cb(i):
        return ctiles[(cbase + i) % 3]

    def qk(i, h2):
        kbi, nk, c0, diag = blocks[i]
        zt, zb_ = zs(i, h2)
        mm(zt[0:nk, c0:N], kt(h2, kbi, nk), qs.t[64 * h2:64 * h2 + 64, c0:N], True, True, [bkt, qs.b], [zb_])

    def spl(i):
        kbi, nk, c0, diag = blocks[i]
        sp = A["spg"][i % G]
        for h2 in range(2):
            zt, zb_ = zs(i, h2)
            act(sp.t[0:nk, h2, c0:N], zt[0:nk, c0:N], AF.Softplus, [zb_], [sp.b], scale=0.125)
        if diag:
            mw = min(nk, N - c0)
            for h2 in range(2):
                tt("pool", sp.t[0:nk, h2, c0:c0 + mw], sp.t[0:nk, h2, c0:c0 + mw], maskT.t[0:nk, 0:mw], ALU.mult, [sp.b], [sp.b])
        if i + 1 < nb:
            cur = LsF[i % R]; nxt = LsF[(i + 1) % R]
            full = (nk == 128 and c0 == 0)
            if i == 0:
                if not full:
                    cp("dve", nxt.t[:, :, 0:N], zf.t[:, :, 0:N], [], [nxt.b])
                cp("dve", nxt.t[0:nk, :, c0:N], sp.t[0:nk, :, c0:N], [sp.b], [nxt.b])
            else:
                assert nk == 128
                if c0 > 0:
                    cp("dve", nxt.t[:, :, 0:c0], zf.t[:, :, 0:c0], [], [nxt.b])
                tt("dve", nxt.t[:, :, c0:N], cur.t[:, :, c0:N], sp.t[:, :, c0:N], ALU.add, [cur.b, sp.b], [nxt.b])

    def cmm(i):
        kbi, nk, c0, diag = blocks[i]
        (Ct, Cb) = cb(i); sp = A["spg"][i % G]; cur = LsF[i % R]
        for h2 in range(2):
            mm(Ct[0:nk, h2, c0:N], triP.t[0:nk, 0:nk], sp.t[0:nk, h2, c0:N], True, False, [sp.b], Cb)
            if i > 0:
                mm(Ct[:, h2, c0:N], ones.t[:, :], cur.t[:, h2, c0:N], False, False, [cur.b], Cb)
        for h2 in range(2):
            mm(Ct[0:nk, h2, c0:N], kt(h2, kbi, nk), nq.t[64 * h2:64 * h2 + 64, c0:N], False, True, [bkt, nq.b], Cb)

    def ex(i):
        kbi, nk, c0, diag = blocks[i]
        (Ct, Cb) = cb(i); a = A["a"][(abase + i) % NA]
        act(a.t[0:nk, :, c0:N], Ct[0:nk, :, c0:N], AF.Exp, Cb, [a.b], scale=-1.0)
        if diag:
            mw = min(nk, N - c0)
            for h2 in range(2):
                tt("pool", a.t[0:nk, h2, c0:c0 + mw], a.t[0:nk, h2, c0:c0 + mw], maskT.t[0:nk, 0:mw], ALU.mult, [a.b], [a.b])

    def pv(i):
        kbi, nk, c0, diag = blocks[i]
        a = A["a"][(abase + i) % NA]
        for h2 in range(2):
            mm(X.t[64 * h2:64 * h2 + 64, 2, c0:N], vv(h2, kbi, nk), a.t[0:nk, h2, c0:N], i == 0, i == nb - 1,
               [a.b, bv], [bX[2]], skip=True)

    groups = [(g0, min(nb, g0 + G)) for g0 in range(0, nb, G)]
    for gi, (g0, g1) in enumerate(groups):
        pg_ = groups[gi - 1] if gi > 0 else None
        if pg_:
            pend = [(lambda i=i: pv(i)) for i in range(pg_[0], pg_[1])]
        else:
            pend = list(prev) if prev else []
        zq = [(i, h2) for i in range(g0, g1) for h2 in range(2)]
        for k in range(min(NZ, len(zq))):
            qk(*zq[k])
        zn = NZ
        for i in range(g0, g1):
            spl(i)
            for _ in range(2):
                if zn < len(zq):
                    qk(*zq[zn]); zn += 1
            if pend:
                pend.pop(0)()
        while pend:
            pend.pop(0)()
        ahead = min(3, g1 - g0)
        for i in range(g0, g0 + ahead):
            cmm(i)
        for i in range(g0, g1):
            ex(i)
            if i + ahead < g1:
                cmm(i + ahead)
    tail = [(lambda i=i: pv(i)) for i in range(groups[-1][0], groups[-1][1])]

    def evac():
        cp("dve", obs.t[:, 0:N], X.t[:, 2, 0:N], [bX[2]], [obs.b])
        if store is not None:
            store()
    tail.append(evac)
    return tail


KB.alloc_norm = _alloc_norm
KB.norm = _norm
KB.pe_hT = _pe_hT
KB.rope = _rope
KB.p1_evac = _p1_evac
KB.alloc_attn = _alloc_attn
KB.a_tile = _a_tile
KB.b_tile = _b_tile


def _sample_attn(self, es, T, PT, cak, cav, cbk, cbv, qkT_sd, v_sd, oa_sd, obT_sd,
                 gsub, nlam, zer, triP, ones, maskT, bconst):
    mm, tr, act, tt, cp, mset, dma = self.mm, self.tr, self.act, self.tt, self.cp, self.mset, self.dma
    A = self.alloc_attn(es, T, PT, 0, G=4)
    NK = PAST + NS
    KTa = T(es, "KTa", [128, 4, NK], BF16); KTb = T(es, "KTb", [128, 4, NK], BF16)
    VAs = T(es, "VAs", [128, 17, 4, 130], BF16); VBs = T(es, "VBs", [128, 17, 512], BF16)
    ct = [T(es, "ct%d" % i, [128, 512], F32) for i in range(4)]
    qsa = T(es, "qsa", [128, 4, NS], BF16); qsb = T(es, "qsb", [128, 4, NS], BF16)
    Yv = A["X"].t[:, 2, :].rearrange("p (h k) -> p h k", k=128)
    bY = A["bX"][2]
    mset("pool", VAs.t[:, :, :, 128:129], 1.0, [VAs.b])
    qv = qkT_sd.rearrange("u p t -> p u t")
    ci = 0
    for s in range(2):
        t0, t1 = s * NS, (s + 1) * NS
        for i in range(16):
            r0 = i * 128
            for which, src in enumerate((cak, cbk, cav, cbv)):
                c_ = ct[ci % 4]; ci += 1
                dma(c_.t[:, :], src[s, r0:r0 + 128, :], [], [c_.b])
                if which < 2:
                    for h in range(4):
                        self.tr32(Yv[:, h, :], c_.t[:, h * 128:(h + 1) * 128], 128, [c_.b], [bY])
                    dstT = KTa if which == 0 else KTb
                    cp("act", dstT.t[:, :, r0:r0 + 128], Yv, [bY], [dstT.b])
                elif which == 2:
                    cp("dve", VAs.t[:, i, :, 0:128], c_.t[:, :].rearrange("p (h e) -> p h e", e=128), [c_.b], [VAs.b])
                else:
                    cp("dve", VBs.t[:, i, :], c_.t[:, :], [c_.b], [VBs.b])
        dma(KTa.t[:, :, PAST:NK], qv[:, 4:8, t0:t1], [], [KTa.b])
        dma(KTb.t[:, :, PAST:NK], qv[:, 12:16, t0:t1], [], [KTb.b])
        dma(VAs.t[0:NS, 16, :, 0:128], v_sd[t0:t1, 0:512].rearrange("t (h e) -> t h e", e=128), [], [VAs.b])
        dma(VBs.t[0:NS, 16, :], v_sd[t0:t1, 512:1024], [], [VBs.b])
        dma(qsa.t[:, :, :], qv[:, 0:4, t0:t1], [], [qsa.b])
        dma(qsb.t[:, :, :], qv[:, 8:12, t0:t1], [], [qsb.b])
        blocks = [(kb, 128, 0, False) for kb in range(16)] + [(16, NS, 0, False)]
        for h in range(4):
            ost = A["ost"][A["oi"] % 2]; A["oi"] += 1
            fin = self.a_tile(A, lambda c, kb, nk, h=h: KTa.t[64 * c:64 * c + 64, h, kb * 128:kb * 128 + nk], KTa.b,
                              lambda kb, nk, h=h: VAs.t[0:nk, kb, h, 0:129], VAs.b,
                              blocks, NS, Tv(qsa.t[:, h, :], qsa.b), [(0, 0, NS, 16)], ost, gsub, nlam, zer, bconst)
            fin()
            dma(oa_sd[t0:t1, h * 128:(h + 1) * 128], ost.t[0:NS, 0, :], [ost.b], [], eng="pool")
        blocks_b = [(16, NS, 0, True)] + [(kb, 128, 0, False) for kb in range(15, -1, -1)]
        for p in range(4):
            obs = A["obs"][A["oi"] % 2]; A["oi"] += 1
            tl_ = self.b_tile(A, lambda h2, kb, nk, p=p: KTb.t[64 * h2:64 * h2 + 64, p, kb * 128:kb * 128 + nk], KTb.b,
                              lambda h2, kb, nk, p=p: VBs.t[0:nk, kb, p * 128 + 64 * h2:p * 128 + 64 * h2 + 64], VBs.b,
                              blocks_b, NS, Tv(qsb.t[:, p, :], qsb.b), obs, triP, ones, maskT, bconst)
            for f in tl_:
                f()
            dma(obT_sd[p][:, t0:t1], obs.t[:, 0:NS], [obs.b], [], eng="pool")


def _phase3(self, es, T, PT, w_in, wba_d, wbb_d, wo_d, xp, xs, oa_d, obT_d, oa_sd, obT_sd, y_p, y_s,
            Abc, shbc, gtbc, fgbc, bconst):
    mm, tr, act, tt, stt, cp, dma = self.mm, self.tr, self.act, self.tt, self.stt, self.cp, self.dma
    NT = self.NT
    L = self.alloc_norm(es, T, PT)
    wg = T(es, "wg", [128, 8, 3072], BF16)
    wba = T(es, "wba", [128, 4, D], BF16); wbb = T(es, "wbb", [128, 4, D], BF16)
    wo = T(es, "wo", [128, 8, D], BF16)
    wst = [T(es, "wst3%d" % i, [128, 4, 512], F32) for i in range(2)]
    oat = [T(es, "oat%d" % i, [128, 512], F32) for i in range(2)]
    obt = [T(es, "obt%d" % i, [128, 4, 128], F32) for i in range(2)]
    sza = T(es, "sza", [128, 512], F32); u1 = T(es, "u1", [128, 512], F32); ua = T(es, "ua", [128, 512], BF16)
    szb = T(es, "szb", [128, 4, 128], F32); u2 = T(es, "u2", [128, 4, 128], F32); ubT = T(es, "ubT", [128, 4, 128], BF16)
    sga = T(es, "sga", [128, D], F32); sgb = T(es, "sgb", [128, D], F32)
    uaT = T(es, "uaT", [128, 4, 128], BF16)
    m1 = T(es, "m1", [128, D], F32); m2 = T(es, "m2", [128, D], F32); mb = T(es, "mb", [128, D], BF16)
    mT = T(es, "mT", [128, 8, 128], BF16)
    ys = [T(es, "ys%d" % i, [128, D], F32) for i in range(2)]
    sm3 = [T(es, "sm3%d" % i, [128, 8], F32) for i in range(4)]
    ZA = PT(es, "ZA", [128, 512], F32); ZB = PT(es, "ZB", [128, 4, 128], F32)
    WA = PT(es, "WA", [128, 2, 512], F32); WB = PT(es, "WB", [128, 2, 512], F32)
    TP = L["TP"]
    flat = "p a b -> p (a b)"

    wi_v = w_in.rearrange("(j p) n -> p j n", p=128)
    k = 0
    srcs = [1536, 3584, 4096, 4608, 5120, 5632]
    for g in range(6):
        for half in range(2):
            w_ = wst[k % 2]; k += 1
            dma(w_.t[:], wi_v[:, 4 * half:4 * half + 4, srcs[g]:srcs[g] + 512], [], [w_.b])
            cp("dve" if k % 2 else "pool", wg.t[:, 4 * half:4 * half + 4, g * 512:(g + 1) * 512], w_.t[:], [w_.b], [wg.b])
    for wd, wt in ((wba_d, wba), (wbb_d, wbb)):
        wv = wd.rearrange("(c p) n -> p c n", p=128)
        for nh in range(2):
            w_ = wst[k % 2]; k += 1
            dma(w_.t[:], wv[:, :, nh * 512:(nh + 1) * 512], [], [w_.b])
            cp("dve" if k % 2 else "pool", wt.t[:, :, nh * 512:(nh + 1) * 512], w_.t[:], [w_.b], [wt.b])
    wv = wo_d.rearrange("(j p) n -> p j n", p=128)
    for half in range(2):
        for nh in range(2):
            w_ = wst[k % 2]; k += 1
            dma(w_.t[:], wv[:, 4 * half:4 * half + 4, nh * 512:(nh + 1) * 512], [], [w_.b])
            cp("dve" if k % 2 else "pool", wo.t[:, 4 * half:4 * half + 4, nh * 512:(nh + 1) * 512], w_.t[:], [w_.b], [wo.b])

    tiles = []
    obT_v = obT_d.rearrange("u p t -> p u t")
    for t in range(NT):
        r0 = t * 128
        tiles.append(dict(P=128, x=xp[r0:r0 + 128, :], mod=0, oa=oa_d[r0:r0 + 128, :], ob=obT_v[:, :, r0:r0 + 128],
                          y=y_p[r0:r0 + 128, :]))
    if self.with_sample:
        tiles.append(dict(P=32, x=xs, mod=1, oa=oa_sd, ob=obT_sd.rearrange("u p t -> p u t"), y=y_s))
    n = len(tiles)

    def load(i):
        tl = tiles[i]; P = tl["P"]; s = i % 2
        dma(L["xt"][s].t[0:P, :], tl["x"], [], [L["xt"][s].b])
        dma(oat[s].t[0:P, :], tl["oa"], [], [oat[s].b])
        dma(obt[s].t[:, :, 0:P], tl["ob"], [], [obt[s].b])

    def head(i):
        tl = tiles[i]; P = tl["P"]; s = i % 2
        hT = L["hT"][s]
        self.pe_hT(L, s, P)
        for j in range(8):
            mm(ZA.t[0:P, :], hT.t[:, j, 0:P], wg.t[:, j, 0:512], j == 0, j == 7, [hT.b, wg.b], [ZA.b])
        for fc in range(4):
            for j in range(8):
                mm(ZB.t[:, fc, 0:P], wg.t[:, j, 512 + fc * 128:512 + (fc + 1) * 128], hT.t[:, j, 0:P], j == 0, j == 7,
                   [hT.b, wg.b], [ZB.b])
        for nh in range(2):
            for j in range(8):
                mm(WB.t[0:P, nh, :], hT.t[:, j, 0:P], wg.t[:, j, 2048 + nh * 512:2048 + (nh + 1) * 512], j == 0, j == 7,
                   [hT.b, wg.b], [WB.b])

    def head_b(i):
        tl = tiles[i]; P = tl["P"]; s = i % 2
        hT = L["hT"][s]
        for nh in range(2):
            for j in range(8):
                mm(WA.t[0:P, nh, :], hT.t[:, j, 0:P], wg.t[:, j, 1024 + nh * 512:1024 + (nh + 1) * 512], j == 0, j == 7,
                   [hT.b, wg.b], [WA.b])

    def mid(i):
        tl = tiles[i]; P = tl["P"]; s = i % 2
        act(sza.t[0:P, :], ZA.t[0:P, :], AF.Sigmoid, [ZA.b], [sza.b])
        tt("dve", u1.t[0:P, :], ZA.t[0:P, :], sza.t[0:P, :], ALU.mult, [ZA.b, sza.b], [u1.b])
        tt("dve", ua.t[0:P, :], u1.t[0:P, :], oat[s].t[0:P, :], ALU.mult, [u1.b, oat[s].b], [ua.b])
        act(szb.t[:, :, 0:P], ZB.t[:, :, 0:P], AF.Sigmoid, [ZB.b], [szb.b])
        tt("dve", u2.t[:, :, 0:P], ZB.t[:, :, 0:P], szb.t[:, :, 0:P], ALU.mult, [ZB.b, szb.b], [u2.b])
        tt("dve", ubT.t[:, :, 0:P], u2.t[:, :, 0:P], obt[s].t[:, :, 0:P], ALU.mult, [u2.b, obt[s].b], [ubT.b])
        act(sgb.t[0:P, :], WB.t[0:P, :, :].rearrange(flat), AF.Sigmoid, [WB.b], [sgb.b])
        act(sga.t[0:P, :], WA.t[0:P, :, :].rearrange(flat), AF.Sigmoid, [WA.b], [sga.b])
        for c in range(4):
            tr(TP.t[:, c, 0:P], ua.t[0:P, c * 128:(c + 1) * 128], P, [ua.b], [TP.b])
        cp("act", uaT.t[:, :, 0:P], TP.t[:, 0:4, 0:P], [TP.b], [uaT.b])
        for nh in range(2):
            for c in range(4):
                mm(WB.t[0:P, nh, :], ubT.t[:, c, 0:P], wbb.t[:, c, nh * 512:(nh + 1) * 512], c == 0, c == 3, [ubT.b, wbb.b], [WB.b])
        for nh in range(2):
            for c in range(4):
                mm(WA.t[0:P, nh, :], uaT.t[:, c, 0:P], wba.t[:, c, nh * 512:(nh + 1) * 512], c == 0, c == 3, [uaT.b, wba.b], [WA.b])
        tt("dve", m2.t[0:P, :], WB.t[0:P, :, :].rearrange(flat), sgb.t[0:P, :], ALU.mult, [WB.b, sgb.b], [m2.b])
        tt("dve", m1.t[0:P, :], WA.t[0:P, :, :].rearrange(flat), sga.t[0:P, :], ALU.mult, [WA.b, sga.b], [m1.b])
        tt("dve", mb.t[0:P, :], m1.t[0:P, :], m2.t[0:P, :], ALU.add, [m1.b, m2.b], [mb.b])
        for j in range(8):
            tr(TP.t[:, j, 0:P], mb.t[0:P, j * 128:(j + 1) * 128], P, [mb.b], [TP.b])
        cp("act", mT.t[:, :, 0:P], TP.t[:, :, 0:P], [TP.b], [mT.b])
        for nh in range(2):
            for j in range(8):
                mm(WA.t[0:P, nh, :], mT.t[:, j, 0:P], wo.t[:, j, nh * 512:(nh + 1) * 512], j == 0, j == 7, [mT.b, wo.b], [WA.b])

    def tail_a(i):
        tl = tiles[i]; P = tl["P"]; md = tl["mod"]
        tt("dve", m1.t[0:P, :], WA.t[0:P, :, :].rearrange(flat), gtbc[md].t[0:P, :], ALU.mult, [WA.b, gtbc[md].b], [m1.b])

    def tail(i):
        tl = tiles[i]; P = tl["P"]; s = i % 2; md = tl["mod"]
        xt = L["xt"][s]
        tt("dve", m2.t[0:P, :], m1.t[0:P, :], xt.t[0:P, :], ALU.add, [m1.b, xt.b], [m2.b])
        sm = sm3[i % 4]; sqj = L["sqj"]
        act(sqj.t[0:P, :], m2.t[0:P, :], AF.Square, [m2.b], [sqj.b, sm.b], accum=sm.t[0:P, 0:1])
        self.rstd(sm.t[0:P, 0:1], sm.t[0:P, 1:2], sm.t[0:P, 2:3], sm.t[0:P, 3:4], 1.0 / D, sm.b, sm.b, sm.b)
        stt("dve", ys[s].t[0:P, :], m2.t[0:P, :], sm.t[0:P, 3:4], fgbc.t[0:P, :], ALU.mult, ALU.mult, [m2.b, sm.b, fgbc.b], [ys[s].b])
        dma(tl["y"], ys[s].t[0:P, :], [ys[s].b], [], eng="pool")

    load(0)
    if n > 1:
        load(1)
    self.norm(L, 0, tiles[0]["P"], Abc[tiles[0]["mod"]], shbc[tiles[0]["mod"]])
    head(0); head_b(0)
    for i in range(n):
        if i + 1 < n:
            self.norm(L, (i + 1) % 2, tiles[i + 1]["P"], Abc[tiles[i + 1]["mod"]], shbc[tiles[i + 1]["mod"]])
        mid(i)
        if i + 1 < n:
            head(i + 1)
        tail_a(i)
        if i + 1 < n:
            head_b(i + 1)
        tail(i)
        if i + 2 < n:
            load(i + 2)


KB.sample_attn = _sample_attn
KB.phase3 = _phase3


def _rope_tables(pos):
    half = 32
    inv = (np.float32(10000.0) ** (-np.arange(half, dtype=np.float32) / np.float32(half))).astype(np.float32)
    ang = pos.astype(np.float32)[:, None] * inv[None, :]
    cos = np.cos(ang).astype(np.float32); sin = np.sin(ang).astype(np.float32)
    return np.tile(cos, (1, 8)), np.tile(sin, (1, 8))


_CACHE = {}


def run(inputs, NT, n_cores, with_sample=True, trace=False):
    key = (NT, with_sample)
    if key not in _CACHE:
        kb = KB(NT, with_sample)
        kb.build()
        _CACHE[key] = kb
    kb = _CACHE[key]
    S = NT * 128
    bf = ml_dtypes.bfloat16
    f32 = np.float32
    g = lambda k: np.ascontiguousarray(np.asarray(inputs[k], dtype=f32))
    xP, xS, cP, cS = g("x_prompt"), g("x_sample"), g("c_prompt"), g("c_sample")
    cos_p, sin_p = _rope_tables(np.arange(S))
    pos_s = PAST + np.tile(np.arange(NS), 2)
    cos_s, sin_s = _rope_tables(pos_s)
    idx = np.arange(128)
    consts = dict(
        ident=np.eye(128, dtype=f32).astype(bf),
        triP=(idx[:, None] >= idx[None, :]).astype(f32).astype(bf),
        ident32=np.eye(128, dtype=f32),
        mask_lt=(idx[:, None] < idx[None, :]).astype(f32),
        cos_p=cos_p, sin_p=sin_p, cos_s=cos_s, sin_s=sin_s,
        w_ada=g("w_ada")[0], b_ada=g("b_ada")[0], norm_g=g("norm_g")[0], w_in=g("w_in")[0],
        lams=np.concatenate([g("lambda_q1")[0], g("lambda_k1")[0], g("lambda_q2")[0], g("lambda_k2")[0]]),
        subln_g=g("subln_g")[0], wba=g("w_branch_a")[0], wbb=g("w_branch_b")[0], w_out=g("w_out")[0],
        final_g=g("final_g"),
    )
    cak, cav, cbk, cbv = (g(k)[0].reshape(-1, PAST, 512) for k in ("cache_a_k", "cache_a_v", "cache_b_k", "cache_b_v"))
    in_maps = []
    for i in range(n_cores):
        m = dict(consts)
        m["xp"] = xP[i]
        m["xs"] = np.ascontiguousarray(xS[2 * i:2 * i + 2].reshape(32, D))
        cm = cP[i].reshape(8, 128).T
        m["cmat_p"] = np.ascontiguousarray(np.broadcast_to(cm[:, :, None], (128, 8, 128)))
        cs = cS[2 * i:2 * i + 2].reshape(2, 8, 128).transpose(2, 1, 0)
        m["cmat_s"] = np.ascontiguousarray(np.repeat(cs, NS, axis=2))
        for nm_, arr in (("cak", cak), ("cav", cav), ("cbk", cbk), ("cbv", cbv)):
            m[nm_] = np.ascontiguousarray(arr[2 * i:2 * i + 2])
        in_maps.append(m)
    res = run_bass_kernel_spmd(kb.nc, in_maps, core_ids=list(range(n_cores)), trace=trace)
    return res


def kernel(**inputs):
    NT = 64
    res = run(inputs, NT, 8)
    r = res.results
    S = NT * 128
    cat = lambda k: np.stack([r[i][k] for i in range(8)], 0)
    cats = lambda k: np.concatenate([r[i][k].reshape(2, NS, -1) for i in range(8)], 0)
    y_prompt = cat("y_p")
    y_sample = cats("y_s")
    pak = cat("pak").reshape(1, 8, S, 4, 2, 64)
    pav = cat("pav").reshape(1, 8, S, 4, 128)
    pbk = cat("pbk").reshape(1, 8, S, 8, 64)
    pbv = cat("pbv").reshape(1, 8, S, 8, 64)
    sak = cats("sak").reshape(1, 16, NS, 4, 2, 64)
    sav = cats("sav").reshape(1, 16, NS, 4, 128)
    sbk = cats("sbk").reshape(1, 16, NS, 8, 64)
    sbv = cats("sbv").reshape(1, 16, NS, 8, 64)
    return (y_prompt, y_sample, pak, pav, pbk, pbv, sak, sav, sbk, sbv)
```

```python
import contextlib
import numpy as np
import ml_dtypes
import concourse.bass as bass
import concourse.mybir as mybir
from concourse.bass_utils import run_bass_kernel_spmd

F32 = mybir.dt.float32
BF16 = mybir.dt.bfloat16
AF = mybir.ActivationFunctionType
ALU = mybir.AluOpType

ENGS = ("pe", "act", "dve", "pool", "sp")
ND_SEMS = 24


class B:
    __slots__ = ("name", "last_w", "readers")

    def __init__(self, name=""):
        self.name = name
        self.last_w = None
        self.readers = []


class Prog:
    def __init__(self, nc, sems, state):
        self.nc = nc
        self.sems = sems
        self.st = state
        self.ops = {e: [] for e in ENGS}
        self.touched = set()

    def add(self, eng, fn, reads=(), writes=(), dma=False):
        ops = self.ops[eng]
        idx = len(ops)
        deps = {}
        self.touched.update(reads)
        self.touched.update(writes)
        for b in reads:
            if b.last_w is not None:
                deps[b.last_w] = deps.get(b.last_w, 0) | 1
        for b in writes:
            if b.last_w is not None:
                deps[b.last_w] = deps.get(b.last_w, 0) | 2
            for r in b.readers:
                deps[r] = deps.get(r, 0) | 4
        if dma:
            did = self.st["dma_id"]
            self.st["dma_id"] += 1
            ev = ("dma", did)
        else:
            did = None
            ev = (eng, idx)
        deps.pop(ev, None)
        for b in reads:
            b.readers.append(ev)
        for b in writes:
            b.last_w = ev
            b.readers = []
        ops.append({"fn": fn, "deps": deps, "dma": did, "sig": False})
        return ev

    def finish(self):
        nc = self.nc
        st = self.st
        ops = self.ops
        for e in ENGS:
            for op in ops[e]:
                for (pe_, pidx), kind in op["deps"].items():
                    if pe_ == "dma":
                        continue
                    if pe_ == e and e == "pe":
                        continue
                    ops[pe_][pidx]["sig"] = True
        for e in ENGS:
            if e != "sp" and ops[e]:
                ops[e][-1]["sig"] = True
        cnt = {}
        for e in ENGS:
            c = st["sig"][e]
            for i, op in enumerate(ops[e]):
                if op["sig"] and op["dma"] is None:
                    c += 1
                cnt[(e, i)] = c
            st["sig_end"] = st.get("sig_end", {})
            st["sig_end"][e] = c
        final_cnt = dict(st["sig_end"])
        dma_first = st["dma_id"] - sum(1 for e in ENGS for op in ops[e] if op["dma"] is not None)
        dma_last = st["dma_id"]

        def dma_target(did):
            return did % ND_SEMS, 16 * (did // ND_SEMS + 1)

        sems = self.sems
        waited = st["waited"]

        def emit_engine(e, engobj):
            w = waited[e]

            def wait(key, semh, val):
                if w.get(key, 0) >= val:
                    return
                engobj.wait_ge(semh, val)
                w[key] = val

            for i, op in enumerate(ops[e]):
                need = {}
                for (pe_, pidx), kind in op["deps"].items():
                    if pe_ == "dma":
                        si, val = dma_target(pidx)
                        key = ("dma", si)
                        need[key] = max(need.get(key, 0), val)
                    else:
                        if pe_ == e and e == "pe":
                            continue
                        need[pe_] = max(need.get(pe_, 0), cnt[(pe_, pidx)])
                if op["dma"] is not None and op["dma"] >= ND_SEMS:
                    si, val = dma_target(op["dma"] - ND_SEMS)
                    key = ("dma", si)
                    need[key] = max(need.get(key, 0), val)
                for key, val in need.items():
                    if isinstance(key, tuple):
                        wait(key, sems["dma"][key[1]], val)
                    else:
                        wait(key, sems[key], val)
                ins = op["fn"](engobj)
                if op["dma"] is not None:
                    si, _ = dma_target(op["dma"])
                    ins.then_inc(sems["dma"][si], 16)
                elif op["sig"]:
                    ins.then_inc(sems[e], 1)
            for pe_ in ENGS:
                if pe_ == "sp" or pe_ == e:
                    continue
                if final_cnt[pe_] > 0:
                    wait(pe_, sems[pe_], final_cnt[pe_])
            for did in range(max(dma_first, dma_last - ND_SEMS), dma_last):
                si, val = dma_target(did)
                wait(("dma", si), sems["dma"][si], val)

        with nc.Block() as block:
            @block.tensor
            def _(eng):
                emit_engine("pe", eng)

            @block.scalar
            def _(eng):
                emit_engine("act", eng)

            @block.vector
            def _(eng):
                emit_engine("dve", eng)

            @block.gpsimd
            def _(eng):
                emit_engine("pool", eng)

            @block.sync
            def _(eng):
                emit_engine("sp", eng)

        for e in ENGS:
            st["sig"][e] = final_cnt[e]
        for b in self.touched:
            b.last_w = None
            b.readers = []


def new_state():
    return {"dma_id": 0, "sig": {e: 0 for e in ENGS}, "waited": {e: {} for e in ENGS}}


D = 1024
EPS = 1e-6
LAM_INIT = 0.2
PAST = 2048
NS = 16


class KB:
    def __init__(self, NT, with_sample=True):
        self.NT = NT
        self.S = NT * 128
        self.with_sample = with_sample
        self.nc = bass.Bass("TRN2", target_bir_lowering=False)
        self.st = new_state()
        self.uid = 0

    def din(self, name, shape, dt=F32):
        return self.nc.dram_tensor(name, list(shape), dt, kind="ExternalInput").ap()

    def dout(self, name, shape, dt=F32):
        return self.nc.dram_tensor(name, list(shape), dt, kind="ExternalOutput").ap()

    def dscr(self, name, shape, dt):
        return self.nc.dram_tensor(name, list(shape), dt).ap()

    def sb(self, es, name, shape, dt):
        self.uid += 1
        return es.enter_context(self.nc.sbuf_tensor("%s_%d" % (name, self.uid), list(shape), dt))

    def ps(self, es, name, shape, dt):
        self.uid += 1
        return es.enter_context(self.nc.psum_tensor("%s_%d" % (name, self.uid), list(shape), dt))

    def mm(self, out, lhsT, rhs, start, stop, r, w, skip=False):
        self.pg.add("pe", lambda e: e.matmul(out, lhsT=lhsT, rhs=rhs, start=start, stop=stop,
                                             skip_group_check=skip), r, w)

    def tr(self, out, in_, P, r, w):
        ident = self.ident
        self.pg.add("pe", lambda e: e.transpose(out=out, in_=in_, identity=ident[0:P, 0:P]), r, w)

    def tr32(self, out, in_, P, r, w):
        ident = self.ident32
        self.pg.add("pe", lambda e: e.transpose(out=out, in_=in_, identity=ident[0:P, 0:P]), r, w)

    def act(self, out, in_, func, r, w, scale=1.0, bias=0.0, accum=None):
        if accum is None:
            self.pg.add("act", lambda e: e.activation(out=out, in_=in_, func=func, bias=bias, scale=scale), r, w)
        else:
            self.pg.add("act", lambda e: e.activation(out=out, in_=in_, func=func, bias=bias, scale=scale,
                                                      accum_out=accum), r, w)

    def tt(self, eng, out, a, b, op, r, w):
        self.pg.add(eng, lambda e: e.tensor_tensor(out=out, in0=a, in1=b, op=op), r, w)

    def ts(self, eng, out, a, s1, s2, op0, op1, r, w):
        if s2 is None:
            self.pg.add(eng, lambda e: e.tensor_scalar(out=out, in0=a, scalar1=s1, scalar2=None, op0=op0), r, w)
        else:
            self.pg.add(eng, lambda e: e.tensor_scalar(out=out, in0=a, scalar1=s1, scalar2=s2, op0=op0, op1=op1), r, w)

    def stt(self, eng, out, a, s, b, op0, op1, r, w, accum=None):
        if accum is None:
            self.pg.add(eng, lambda e: e.scalar_tensor_tensor(out=out, in0=a, scalar=s, in1=b, op0=op0, op1=op1), r, w)
        else:
            self.pg.add(eng, lambda e: e.scalar_tensor_tensor(out=out, in0=a, scalar=s, in1=b, op0=op0, op1=op1,
                                                              accum_out=accum), r, w)

    def cp(self, eng, out, in_, r, w):
        if eng == "act":
            self.pg.add("act", lambda e: e.activation(out=out, in_=in_, func=AF.Copy), r, w)
        else:
            self.pg.add(eng, lambda e: e.tensor_copy(out=out, in_=in_), r, w)

    def rcp(self, out, in_, r, w):
        self.pg.add("dve", lambda e: e.reciprocal(out=out, in_=in_), r, w)

    def mset(self, eng, ap, val, w):
        self.pg.add(eng, lambda e: e.memset(ap, val), (), w)

    def dma(self, out, in_, r, w, eng="sp"):
        self.pg.add(eng, lambda e: e.dma_start(out=out, in_=in_), r, w, dma=True)

    def new_prog(self):
        self.pg = Prog(self.nc, self.sems, self.st)

    def rstd(self, ss, tmpa, tmpb, out, inv_n, bss, btmp, bout):
        self.ts("dve", tmpa, ss, inv_n, EPS, ALU.mult, ALU.add, [bss], [btmp])
        self.act(tmpb, tmpa, AF.Ln, [btmp], [btmp])
        self.act(out, tmpb, AF.Exp, [btmp], [bout], scale=-0.5)


class Tl:
    __slots__ = ("t", "b")

    def __init__(self, t):
        self.t = t
        self.b = B()


def _build(self):
    nc = self.nc
    S, NT = self.S, self.NT
    NKB = NT
    NQ = NT // 4
    din, dout, dscr = self.din, self.dout, self.dscr
    xp = din("xp", [S, D]); xs = din("xs", [32, D])
    cmat_p = din("cmat_p", [128, 8, 128]); cmat_s = din("cmat_s", [128, 8, 32])
    w_ada = din("w_ada", [D, 3 * D]); b_ada = din("b_ada", [3 * D]); norm_g = din("norm_g", [D])
    w_in = din("w_in", [D, 6144]); lams = din("lams", [256]); subln_g = din("subln_g", [128])
    wba_d = din("wba", [512, D]); wbb_d = din("wbb", [512, D]); wo_d = din("w_out", [D, D]); final_g = din("final_g", [D])
    cak = din("cak", [2, PAST, 512]); cav = din("cav", [2, PAST, 512])
    cbk = din("cbk", [2, PAST, 512]); cbv = din("cbv", [2, PAST, 512])
    cos_p = din("cos_p", [S, 256]); sin_p = din("sin_p", [S, 256])
    cos_s = din("cos_s", [32, 256]); sin_s = din("sin_s", [32, 256])
    ident_d = din("ident", [128, 128], BF16); triP_d = din("triP", [128, 128], BF16)
    id32_d = din("ident32", [128, 128]); mask_d = din("mask_lt", [128, 128])
    y_p = dout("y_p", [S, D]); y_s = dout("y_s", [32, D])
    pak = dout("pak", [S, 512]); pav = dout("pav", [S, 512]); pbk = dout("pbk", [S, 512]); pbv = dout("pbv", [S, 512])
    sak = dout("sak", [32, 512]); sav = dout("sav", [32, 512]); sbk = dout("sbk", [32, 512]); sbv = dout("sbv", [32, 512])
    qkT_d = dscr("qkT_d", [16, 128, S], BF16); v_d = dscr("v_d", [S, 1024], BF16)
    oa_d = dscr("oa_d", [S, 512], F32); obT_d = dscr("obT_d", [4, 128, S], F32)
    qkT_sd = dscr("qkT_sd", [16, 128, 32], BF16); v_sd = dscr("v_sd", [32, 1024], BF16)
    oa_sd = dscr("oa_sd", [32, 512], F32); obT_sd = dscr("obT_sd", [4, 128, 32], F32)

    mm, tr, act, tt, ts, stt, cp, rcp, mset, dma = (self.mm, self.tr, self.act, self.tt, self.ts, self.stt,
                                                    self.cp, self.rcp, self.mset, self.dma)

    with contextlib.ExitStack() as top:
        self.sems = {e: top.enter_context(nc.semaphore("s_" + e)) for e in ENGS if e != "sp"}
        self.sems["dma"] = [top.enter_context(nc.semaphore("s_dma%d" % i)) for i in range(ND_SEMS)]

        def T(es, name, shape, dt):
            return Tl(self.sb(es, name, shape, dt))

        def PT(es, name, shape, dt):
            return Tl(self.ps(es, name, shape, dt))

        Abc = [T(top, "Abc%d" % i, [128, D], F32) for i in range(2)]
        shbc = [T(top, "shbc%d" % i, [128, D], F32) for i in range(2)]
        gtbc = [T(top, "gtbc%d" % i, [128, D], F32) for i in range(2)]
        fgbc = T(top, "fgbc", [128, D], F32)
        gsub = T(top, "gsub", [128, 128], F32)
        nlam = T(top, "nlam", [128, 1], F32)
        identT = T(top, "ident", [128, 128], BF16); self.ident = identT.t
        id32T = T(top, "ident32", [128, 128], F32); self.ident32 = id32T.t
        triP = T(top, "triP", [128, 128], BF16); onesT = T(top, "onesT", [128, 128], mybir.dt.float32r); ones32 = T(top, "ones32", [128, 128], F32)
        zf = T(top, "zf", [128, 2, 512], F32); self.zf = zf
        maskT = T(top, "mask", [128, 128], F32)
        zer = T(top, "zer", [128, 512], BF16)
        bconst = B()

        with contextlib.ExitStack() as es01:
            wq = T(es01, "wq", [128, 8, 3072], BF16)
            with contextlib.ExitStack() as es0:
                self.new_prog()
                wst = [T(es0, "wst%d" % i, [128, 4, 512], F32) for i in range(2)]
                bada = T(es0, "bada", [128, 3 * D], F32)
                ngbc = T(es0, "ngbc", [128, D], F32)
                modt = [T(es0, "mod%d" % i, [128, 3 * D], F32) for i in range(2)]
                cm = [T(es0, "cm0", [128, 8, 128], F32), T(es0, "cm1", [128, 8, 32], F32)]
                lamv = T(es0, "lamv", [128, 256], F32)
                sm0 = T(es0, "sm0", [128, 8], F32)
                j64 = T(es0, "j64", [128, 64], F32)
                graw = T(es0, "graw", [128, 128], F32)
                MOD = [PT(es0, "MOD%d" % i, [128, 512], F32) for i in range(2)]

                dma(identT.t[:], ident_d, [], [bconst]); dma(triP.t[:], triP_d, [], [bconst]); dma(id32T.t[:], id32_d, [], [bconst])
                mset("pool", ones32.t[:], 1.0, [ones32.b]); cp("dve", onesT.t[:], ones32.t[:], [ones32.b], [bconst])
                mset("pool", zf.t[:], 0.0, [bconst]); dma(maskT.t[:], mask_d, [], [bconst])
                mset("pool", zer.t[:], 0.0, [zer.b])
                dma(bada.t[:], b_ada.partition_broadcast(128), [], [bada.b])
                dma(ngbc.t[:], norm_g.partition_broadcast(128), [], [ngbc.b])
                dma(fgbc.t[:], final_g.partition_broadcast(128), [], [fgbc.b])
                dma(graw.t[:], subln_g.partition_broadcast(128), [], [graw.b])
                dma(lamv.t[:], lams.partition_broadcast(128), [], [lamv.b])
                dma(cm[0].t[:], cmat_p, [], [cm[0].b]); dma(cm[1].t[:], cmat_s, [], [cm[1].b])
                stt("dve", j64.t[:], lamv.t[:, 0:64], 1.0, lamv.t[:, 64:128], ALU.mult, ALU.mult, [lamv.b], [j64.b, sm0.b], accum=sm0.t[:, 0:1])
                stt("dve", j64.t[:], lamv.t[:, 128:192], 1.0, lamv.t[:, 192:256], ALU.mult, ALU.mult, [lamv.b, sm0.b], [j64.b, sm0.b], accum=sm0.t[:, 1:2])
                act(sm0.t[:, 2:4], sm0.t[:, 0:2], AF.Exp, [sm0.b], [sm0.b])
                tt("dve", sm0.t[:, 4:5], sm0.t[:, 2:3], sm0.t[:, 3:4], ALU.subtract, [sm0.b], [sm0.b])
                ts("dve", nlam.t[:], sm0.t[:, 4:5], LAM_INIT, -1.0, ALU.add, ALU.mult, [sm0.b], [nlam.b])
                ts("dve", gsub.t[:], graw.t[:], 1.0 - LAM_INIT, None, ALU.mult, None, [graw.b], [gsub.b])
                wa_v = w_ada.rearrange("(j p) n -> p j n", p=128)
                k = 0
                for g in range(6):
                    for half in range(2):
                        w_ = wst[k % 2]; k += 1
                        dma(w_.t[:], wa_v[:, 4 * half:4 * half + 4, g * 512:(g + 1) * 512], [], [w_.b])
                        for jj in range(4):
                            j = 4 * half + jj
                            mm(MOD[0].t[:, :], cm[0].t[:, j, :], w_.t[:, jj, :], j == 0, j == 7, [cm[0].b, w_.b], [MOD[0].b])
                            mm(MOD[1].t[0:32, :], cm[1].t[:, j, :], w_.t[:, jj, :], j == 0, j == 7, [cm[1].b, w_.b], [MOD[1].b])
                    cs = slice(g * 512, (g + 1) * 512)
                    tt("dve", modt[0].t[:, cs], MOD[0].t[:, :], bada.t[:, cs], ALU.add, [MOD[0].b, bada.b], [modt[0].b])
                    tt("dve", modt[1].t[0:32, cs], MOD[1].t[0:32, :], bada.t[0:32, cs], ALU.add, [MOD[1].b, bada.b], [modt[1].b])
                for i, P in ((0, 128), (1, 32)):
                    stt("dve", Abc[i].t[0:P, :], modt[i].t[0:P, D:2 * D], 1.0, ngbc.t[0:P, :], ALU.add, ALU.mult, [modt[i].b, ngbc.b], [Abc[i].b])
                    cp("pool", shbc[i].t[0:P, :], modt[i].t[0:P, 0:D], [modt[i].b], [shbc[i].b])
                    cp("pool", gtbc[i].t[0:P, :], modt[i].t[0:P, 2 * D:3 * D], [modt[i].b], [gtbc[i].b])
                wi_v = w_in.rearrange("(j p) n -> p j n", p=128)
                qkv_cols = [0, 512, 1024, 2048, 2560, 3072]
                for g in range(6):
                    for half in range(2):
                        w_ = wst[k % 2]; k += 1
                        dma(w_.t[:], wi_v[:, 4 * half:4 * half + 4, qkv_cols[g]:qkv_cols[g] + 512], [], [w_.b])
                        cp("dve" if k % 2 else "pool", wq.t[:, 4 * half:4 * half + 4, g * 512:(g + 1) * 512], w_.t[:], [w_.b], [wq.b])
                self.pg.finish()

            with contextlib.ExitStack() as es1:
                self.new_prog()
                L = self.alloc_norm(es1, T, PT)
                stage = [T(es1, "stage%d" % i, [128, 2048], F32) for i in range(2)]
                qkb = [T(es1, "qkb%d" % i, [128, 2048], BF16) for i in range(2)]
                vbf = [T(es1, "vbf%d" % i, [128, 1024], BF16) for i in range(2)]
                qkTs = [T(es1, "qkTs%d" % i, [128, 16, 128], BF16) for i in range(2)]
                cst = [T(es1, "cst%d" % i, [128, 256], F32) for i in range(2)]
                snt = [T(es1, "snt%d" % i, [128, 256], F32) for i in range(2)]
                rp = [T(es1, "rp%d" % i, [128, 512], F32) for i in range(2)]
                rt = [T(es1, "rt%d" % i, [128, 256], F32) for i in range(4)]
                PG = [PT(es1, "PG%d" % i, [128, 512], F32) for i in range(4)]
                TQ = PT(es1, "TQ", [128, 16, 128], BF16)

                tiles = []
                for t in range(NT):
                    r0 = t * 128
                    tiles.append(dict(P=128, x=xp[r0:r0 + 128, :], mod=0, cos=cos_p[r0:r0 + 128, :], sin=sin_p[r0:r0 + 128, :],
                                      outs=[pak[r0:r0 + 128, :], pav[r0:r0 + 128, :], pbk[r0:r0 + 128, :], pbv[r0:r0 + 128, :]],
                                      qkT=qkT_d[:, :, r0:r0 + 128], v=v_d[r0:r0 + 128, :]))
                if self.with_sample:
                    tiles.append(dict(P=32, x=xs, mod=1, cos=cos_s, sin=sin_s, outs=[sak, sav, sbk, sbv],
                                      qkT=qkT_sd, v=v_sd))
                n = len(tiles)
                pgi = [0]

                def load(i):
                    tl = tiles[i]; P = tl["P"]; s = i % 2
                    dma(L["xt"][s].t[0:P, :], tl["x"], [], [L["xt"][s].b])
                    dma(cst[s].t[0:P, :], tl["cos"], [], [cst[s].b])
                    dma(snt[s].t[0:P, :], tl["sin"], [], [snt[s].b])

                def pe_main(i):
                    tl = tiles[i]; P = tl["P"]; s = i % 2
                    hT = L["hT"][s]
                    grp = []
                    for g in range(6):
                        pgt = PG[pgi[0] % 4]; pgi[0] += 1
                        for j in range(8):
                            mm(pgt.t[0:P, :], hT.t[:, j, 0:P], wq.t[:, j, g * 512:(g + 1) * 512], j == 0, j == 7, [hT.b, wq.b], [pgt.b])
                        grp.append(pgt)
                        self.p1_evac(g, pgt, P, s, stage[s], qkb[s], vbf[s], rp, rt, cst[s], snt[s])
                    return grp

                def tq(i):
                    tl = tiles[i]; P = tl["P"]; s = i % 2
                    for u in range(16):
                        tr(TQ.t[:, u, 0:P], qkb[s].t[0:P, u * 128:(u + 1) * 128], P, [qkb[s].b, bconst], [TQ.b])
                    cp("dve", qkTs[s].t[:, :, 0:P], TQ.t[:, :, 0:P], [TQ.b], [qkTs[s].b])
                    dma(tl["qkT"].rearrange("u p t -> p u t"), qkTs[s].t[:, :, 0:P], [qkTs[s].b], [], eng="pool")
                    for q in range(4):
                        dma(tl["outs"][q], stage[s].t[0:P, q * 512:(q + 1) * 512], [stage[s].b], [], eng="pool")
                    dma(tl["v"], vbf[s].t[0:P, :], [vbf[s].b], [], eng="pool")

                load(0)
                if n > 1:
                    load(1)
                self.norm(L, 0, tiles[0]["P"], Abc[tiles[0]["mod"]], shbc[tiles[0]["mod"]])
                for i in range(n):
                    self.pe_hT(L, i % 2, tiles[i]["P"])
                    if i + 1 < n:
                        self.norm(L, (i + 1) % 2, tiles[i + 1]["P"], Abc[tiles[i + 1]["mod"]], shbc[tiles[i + 1]["mod"]])
                    pe_main(i)
                    if i >= 1:
                        tq(i - 1)
                    if i + 2 < n:
                        load(i + 2)
                tq(n - 1)
                self.pg.finish()

        with contextlib.ExitStack() as es2:
            self.new_prog()
            A = self.alloc_attn(es2, T, PT, NKB)
            v_v = v_d.rearrange("(kb p) n -> p kb n", p=128)
            pendA = None; pendB = None
            for u in range(8):
                s = u % 2
                isA = u < 4
                KT, V = A["KT"][s], A["V"][s]
                dma(KT.t[:, 0:S], qkT_d[(4 + u) if isA else (12 + u - 4)], [], [KT.b])
                col0 = u * 128 if isA else 512 + (u - 4) * 128
                for k0 in range(0, NKB, 16):
                    k1 = min(NKB, k0 + 16)
                    dma(V.t[:, k0:k1, 0:128], v_v[:, k0:k1, col0:col0 + 128], [], [V.b])
                if u < 2:
                    mset("pool", V.t[:, :, 128:129], 1.0, [V.b])
                for Tq in range(NQ):
                    qs = A["QT"][A["qi"] % 2]; A["qi"] += 1
                    dma(qs.t[:, :], qkT_d[u if isA else (8 + u - 4)][:, Tq * 512:(Tq + 1) * 512], [], [qs.b])
                    blocks = []
                    for kb in range(4 * Tq + 4):
                        j = kb - 4 * Tq
                        blocks.append((kb, 128, 128 * j if j > 0 else 0, j >= 0))
                    if isA:
                        chunks = [(m, 128 * m, 128, 4 * Tq + m) for m in range(4)]
                        ost = A["ost"][A["oi"] % 2]; A["oi"] += 1
                        dst = oa_d.rearrange("(t m p) n -> t p m n", m=4, p=128)[Tq][:, :, u * 128:(u + 1) * 128]
                        pendA = self.a_tile(A, lambda c, kb, nk, KT=KT: KT.t[64 * c:64 * c + 64, kb * 128:(kb + 1) * 128], KT.b,
                                            lambda kb, nk, V=V: V.t[0:nk, kb, 0:129], V.b,
                                            blocks, 512, qs, chunks, ost, gsub, nlam, zer, bconst, prev=pendA,
                                            store=lambda dst=dst, ost=ost: dma(dst, ost.t[:, :, :], [ost.b], [], eng="pool"))
                    else:
                        if pendA is not None:
                            pendA(); pendA = None
                        obs = A["obs"][A["oi"] % 2]; A["oi"] += 1
                        dstb = obT_d[u - 4][:, Tq * 512:(Tq + 1) * 512]
                        pendB = self.b_tile(A, lambda h2, kb, nk, KT=KT: KT.t[64 * h2:64 * h2 + 64, kb * 128:(kb + 1) * 128], KT.b,
                                            lambda h2, kb, nk, V=V: V.t[0:nk, kb, 64 * h2:64 * h2 + 64], V.b,
                                            blocks[::-1], 512, qs, obs, triP, onesT, maskT, bconst, prev=pendB,
                                            store=lambda dstb=dstb, obs=obs: dma(dstb, obs.t[:, :], [obs.b], [], eng="pool"))
            if pendB:
                for f in pendB:
                    f()
            self.pg.finish()

        if self.with_sample:
            with contextlib.ExitStack() as es2s:
                self.new_prog()
                self.sample_attn(es2s, T, PT, cak, cav, cbk, cbv, qkT_sd, v_sd, oa_sd, obT_sd,
                                 gsub, nlam, zer, triP, onesT, maskT, bconst)
                self.pg.finish()

        with contextlib.ExitStack() as es3:
            self.new_prog()
            self.phase3(es3, T, PT, w_in, wba_d, wbb_d, wo_d, xp, xs, oa_d, obT_d, oa_sd, obT_sd, y_p, y_s,
                        Abc, shbc, gtbc, fgbc, bconst)
            self.pg.finish()
    return nc


KB.build = _build


class Tv:
    __slots__ = ("t", "b")

    def __init__(self, ap, b):
        self.t = ap
        self.b = b


def _alloc_norm(self, es, T, PT):
    L = dict(
        xt=[T(es, "xt%d" % i, [128, D], F32) for i in range(2)],
        tmp=[T(es, "tmp%d" % i, [128, D], F32) for i in range(2)],
        hb=[T(es, "hb%d" % i, [128, D], BF16) for i in range(2)],
        hT=[T(es, "hT%d" % i, [128, 8, 128], BF16) for i in range(2)],
        sqj=T(es, "sqj", [128, D], BF16),
        smn=[T(es, "smn%d" % i, [128, 8], F32) for i in range(4)],
        TP=PT(es, "TP", [128, 8, 128], BF16),
        ni=0,
    )
    return L


def _norm(self, L, s, P, Abc, shbc):
    xt = L["xt"][s]; tmp = L["tmp"][s]; hb = L["hb"][s]
    sm = L["smn"][L["ni"] % 4]; L["ni"] += 1
    sqj = L["sqj"]
    self.act(sqj.t[0:P, :], xt.t[0:P, :], AF.Square, [xt.b], [sqj.b, sm.b], accum=sm.t[0:P, 0:1])
    self.rstd(sm.t[0:P, 0:1], sm.t[0:P, 1:2], sm.t[0:P, 2:3], sm.t[0:P, 3:4], 1.0 / D, sm.b, sm.b, sm.b)
    self.stt("dve", tmp.t[0:P, :], xt.t[0:P, :], sm.t[0:P, 3:4], Abc.t[0:P, :], ALU.mult, ALU.mult,
             [xt.b, sm.b, Abc.b], [tmp.b])
    self.tt("dve", hb.t[0:P, :], tmp.t[0:P, :], shbc.t[0:P, :], ALU.add, [tmp.b, shbc.b], [hb.b])


def _pe_hT(self, L, s, P):
    hb = L["hb"][s]; hT = L["hT"][s]; TP = L["TP"]
    for j in range(8):
        self.tr(TP.t[:, j, 0:P], hb.t[0:P, j * 128:(j + 1) * 128], P, [hb.b], [TP.b])
    self.cp("act", hT.t[:, :, 0:P], TP.t[:, :, 0:P], [TP.b], [hT.b])


def _rope(self, src, P, dst_ap, bdst, cst, snt, rt):
    pat = "p (g two f) -> p g two f"
    sv = src.t[0:P, :].rearrange(pat, two=2, f=32)
    dv = dst_ap.rearrange(pat, two=2, f=32)
    x1, x2 = sv[:, :, 0, :], sv[:, :, 1, :]
    cv = cst.t[0:P, :].rearrange("p (g f) -> p g f", f=32)
    sn = snt.t[0:P, :].rearrange("p (g f) -> p g f", f=32)
    t = [r_.t[0:P, :].rearrange("p (g f) -> p g f", f=32) for r_ in rt]
    tt = self.tt
    tt("dve", t[0], x1, cv, ALU.mult, [src.b, cst.b], [rt[0].b])
    tt("pool", t[1], x2, sn, ALU.mult, [src.b, snt.b], [rt[1].b])
    tt("dve", dv[:, :, 0, :], t[0], t[1], ALU.subtract, [rt[0].b, rt[1].b], [bdst])
    tt("pool", t[2], x2, cv, ALU.mult, [src.b, cst.b], [rt[2].b])
    tt("dve", t[3], x1, sn, ALU.mult, [src.b, snt.b], [rt[3].b])
    tt("pool", dv[:, :, 1, :], t[2], t[3], ALU.add, [rt[2].b, rt[3].b], [bdst])


def _p1_evac(self, g, pgt, P, s, stage, qkb, vbf, rp, rt, cst, snt):
    cp = self.cp
    if g == 0:
        cp("act", rp[0].t[0:P, :], pgt.t[0:P, :], [pgt.b], [rp[0].b])
        self.rope(rp[0], P, qkb.t[0:P, 0:512], qkb.b, cst, snt, rt)
    elif g == 1:
        cp("act", rp[1].t[0:P, :], pgt.t[0:P, :], [pgt.b], [rp[1].b])
        self.rope(rp[1], P, stage.t[0:P, 0:512], stage.b, cst, snt, rt)
        cp("dve", qkb.t[0:P, 512:1024], stage.t[0:P, 0:512], [stage.b], [qkb.b])
    elif g == 2:
        cp("act", stage.t[0:P, 512:1024], pgt.t[0:P, :], [pgt.b], [stage.b])
        cp("dve", vbf.t[0:P, 0:512], stage.t[0:P, 512:1024], [stage.b], [vbf.b])
    elif g == 3:
        cp("act", qkb.t[0:P, 1024:1536], pgt.t[0:P, :], [pgt.b], [qkb.b])
    elif g == 4:
        cp("act", stage.t[0:P, 1024:1536], pgt.t[0:P, :], [pgt.b], [stage.b])
        cp("dve", qkb.t[0:P, 1536:2048], stage.t[0:P, 1024:1536], [stage.b], [qkb.b])
    else:
        cp("act", stage.t[0:P, 1536:2048], pgt.t[0:P, :], [pgt.b], [stage.b])
        cp("dve", vbf.t[0:P, 512:1024], stage.t[0:P, 1536:2048], [stage.b], [vbf.b])


def _alloc_attn(self, es, T, PT, NKB, G=8):
    A = {}
    if NKB:
        A["KT"] = [T(es, "KT%d" % i, [128, NKB * 128], BF16) for i in range(2)]
        A["V"] = [T(es, "V%d" % i, [128, NKB, 130], BF16) for i in range(2)]
    A["zf"] = self.zf
    A["G"] = G
    A["QT"] = [T(es, "QT%d" % i, [128, 512], BF16) for i in range(2)]
    A["S"] = [PT(es, "S%d" % i, [128, 2, 512], F32) for i in range(2)]
    A["Sb"] = [[B(), B()], [B(), B()]]
    A["X"] = PT(es, "X", [128, 3, 512], F32)
    A["bX"] = [B(), B(), B()]
    A["E"] = [T(es, "E%d" % i, [128, 2, 512], BF16) for i in range(3)]
    A["spg"] = [T(es, "spg%d" % i, [128, 2, 512], BF16) for i in range(A["G"])]
    A["LsF"] = [T(es, "LsF%d" % i, [128, 2, 512], mybir.dt.float32r) for i in range(A["G"] + 1)]
    A["nq"] = [T(es, "nq%d" % i, [128, 512], BF16) for i in range(2)]
    A["a"] = [T(es, "a%d" % i, [128, 2, 512], BF16) for i in range(A["G"] + 3)]

    A["ost"] = [T(es, "ost%d" % i, [128, 4, 128], F32) for i in range(2)]
    A["obs"] = [T(es, "obs%d" % i, [128, 512], F32) for i in range(2)]
    A["oo"] = [T(es, "oo%d" % i, [128, 4, 128], F32) for i in range(2)]
    A["t0"] = [T(es, "t0%d" % i, [128, 128], F32) for i in range(2)]
    A["sma"] = [T(es, "sma%d" % i, [128, 16], F32) for i in range(4)]
    A["smb"] = [T(es, "smb%d" % i, [128, 12], F32) for i in range(4)]
    A["jk"] = T(es, "jk", [128, 128], F32)
    for k in ("si", "ei", "wi", "qi", "oi", "ti", "fi", "nqi", "zi"):
        A[k] = 0
    return A


def _a_tile(self, A, kt, bkt, vv, bv, blocks, N, qs, chunks, ost, gsub, nlam, zer, bconst, prev=None, store=None):
    mm, act, tt, ts, stt, rcp, mset = self.mm, self.act, self.tt, self.ts, self.stt, self.rcp, self.mset
    nb = len(blocks)
    X, bX = A["X"], A["bX"]
    sbase, ebase = A["si"], A["ei"]
    A["si"] += nb; A["ei"] += nb
    ti = A["ti"]; A["ti"] += 1
    sm = A["sma"][ti % 4]; sm2 = A["smb"][ti % 4]; oo = A["oo"][ti % 2]; jk = A["jk"]
    qn0 = chunks[0][2]
    nm = len(chunks)

    def acc(m, c):
        a = m * 2 + c
        return X.t[:, a // 3, (a % 3) * 130:(a % 3) * 130 + 129], bX[a // 3]

    def qk(i):
        kbi, nk, c0, diag = blocks[i]
        Sl = A["S"][(sbase + i) % 2]; Sb = A["Sb"][(sbase + i) % 2]
        for c in range(2):
            mm(Sl.t[0:nk, c, c0:N], kt(c, kbi, nk), qs.t[64 * c:64 * c + 64, c0:N], True, True, [bkt, qs.b], [Sb[c]])

    def ex(i):
        kbi, nk, c0, diag = blocks[i]
        Sl = A["S"][(sbase + i) % 2]; El = A["E"][(ebase + i) % 3]; Sb = A["Sb"][(sbase + i) % 2]
        act(El.t[0:nk, :, c0:N], Sl.t[0:nk, :, c0:N], AF.Exp, Sb, [El.b], scale=0.125)
        if diag:
            mset("pool", El.t[64:128, :, c0:c0 + 64], 0.0, [El.b])

    def pv(i):
        kbi, nk, c0, diag = blocks[i]
        El = A["E"][(ebase + i) % 3]
        for c in range(2):
            for (m, q0, qn, last) in chunks:
                if q0 < c0:
                    continue
                o_ap, ob = acc(m, c)
                mm(o_ap[0:qn, :], El.t[0:nk, c, q0:q0 + qn], vv(kbi, nk), False, i == last, [El.b, bv], [ob], skip=True)

    def fin_chunk(m, q0, qn):
        (a0, b0), (a1, b1) = acc(m, 0), acc(m, 1)
        t0 = A["t0"][A["fi"] % 2]; A["fi"] += 1
        rcp(sm.t[0:qn, m:m + 1], a0[0:qn, 128:129], [b0], [sm.b])
        rcp(sm.t[0:qn, 4 + m:5 + m], a1[0:qn, 128:129], [b1], [sm.b])
        tt("dve", sm.t[0:qn, 8 + m:9 + m], sm.t[0:qn, 4 + m:5 + m], nlam.t[0:qn, :], ALU.mult, [sm.b], [sm.b])
        ts("dve", t0.t[0:qn, :], a0[0:qn, 0:128], sm.t[0:qn, m:m + 1], None, ALU.mult, None, [b0, sm.b], [t0.b])
        stt("dve", oo.t[0:qn, m, :], a1[0:qn, 0:128], sm.t[0:qn, 8 + m:9 + m], t0.t[0:qn, :], ALU.mult, ALU.add,
            [b1, sm.b, t0.b], [oo.b])
        stt("dve", jk.t[0:qn, :], oo.t[0:qn, m, :], 1.0, oo.t[0:qn, m, :], ALU.mult, ALU.mult, [oo.b], [jk.b, sm.b],
            accum=sm.t[0:qn, 12 + m:13 + m])

    qk(0)
    for b in range(3):
        mm(X.t[:, b, :], zer.t[:, 0:128], zer.t[:, :], True, False, [], [bX[b]], skip=True)
    if nb > 1:
        qk(1)
    for i in range(nb):
        ex(i)
        if i + 2 < nb:
            qk(i + 2)
        pv(i)
        if prev is not None and i == min(1, nb - 1):
            prev()
        for (m, q0, qn, last) in chunks:
            if last == i:
                fin_chunk(m, q0, qn)

    def finish():
        ts("dve", sm2.t[0:qn0, 0:nm], sm.t[0:qn0, 12:12 + nm], 1.0 / 128, EPS, ALU.mult, ALU.add, [sm.b], [sm2.b])
        act(sm2.t[0:qn0, 4:4 + nm], sm2.t[0:qn0, 0:nm], AF.Ln, [sm2.b], [sm2.b])
        act(sm2.t[0:qn0, 8:8 + nm], sm2.t[0:qn0, 4:4 + nm], AF.Exp, [sm2.b], [sm2.b], scale=-0.5)
        for (m, q0, qn, last) in chunks:
            stt("dve", ost.t[0:qn, m, :], oo.t[0:qn, m, :], sm2.t[0:qn, 8 + m:9 + m], gsub.t[0:qn, :], ALU.mult, ALU.mult,
                [oo.b, sm2.b], [ost.b])
        if store is not None:
            store()
    return finish


def _b_tile(self, A, kt, bkt, vv, bv, blocks, N, qs, obs, triP, ones, maskT, bconst, prev=None, store=None):
    mm, act, tt, cp = self.mm, self.act, self.tt, self.cp
    nb = len(blocks)
    G = A["G"]
    X, bX = A["X"], A["bX"]
    S0, S1 = A["S"]; Sb = A["Sb"]
    zpairs = [(X.t[:, 0:2, :], [bX[0], bX[1]]), (S0.t, Sb[0]), (S1.t, Sb[1])]
    NZ = 2 * len(zpairs)
    nq = A["nq"][A["nqi"] % 2]; A["nqi"] += 1
    LsF = A["LsF"]; R = len(LsF); zf = A["zf"]
    NA = len(A["a"])
    abase = A["wi"]; A["wi"] += nb
    cbase = A["si"]; A["si"] += nb
    zbase = A["zi"]; A["zi"] += 2 * nb
    self.ts("pool", nq.t[:, 0:N], qs.t[:, 0:N], -0.125, None, ALU.mult, None, [qs.b], [nq.b])

    def zs(i, h2):
        zt, zb_ = zpairs[(zbase // 2 + i) % 3]
        return zt[:, h2, :], zb_[h2]

    def zpair(i):
        return zpairs[(zbase // 2 + i) % 3]

    ctiles = [(S0.t, Sb[0]), (S1.t, Sb[1]), (X.t[:, 0:2, :], [bX[0], bX[1]])]

    def cb(i):
        return ctiles[(cbase + i) % 3]

    def qk(i, h2):
        kbi, nk, c0, diag = blocks[i]
        zt, zb_ = zs(i, h2)
        mm(zt[0:nk, c0:N], kt(h2, kbi, nk), qs.t[64 * h2:64 * h2 + 64, c0:N], True, True, [bkt, qs.b], [zb_])

    def spl(i):
        kbi, nk, c0, diag = blocks[i]
        sp = A["spg"][i % G]
        zt, zb_ = zpair(i)
        act(sp.t[0:nk, :, c0:N], zt[0:nk, :, c0:N], AF.Softplus, zb_, [sp.b], scale=0.125)
        if diag:
            mw = min(nk, N - c0)
            for h2 in range(2):
                tt("pool", sp.t[0:nk, h2, c0:c0 + mw], sp.t[0:nk, h2, c0:c0 + mw], maskT.t[0:nk, 0:mw], ALU.mult, [sp.b], [sp.b])
        if i + 1 < nb:
            cur = LsF[i % R]; nxt = LsF[(i + 1) % R]
            full = (nk == 128 and c0 == 0)
            if i == 0:
                if not full:
                    cp("dve", nxt.t[:, :, 0:N], zf.t[:, :, 0:N], [], [nxt.b])
                cp("dve", nxt.t[0:nk, :, c0:N], sp.t[0:nk, :, c0:N], [sp.b], [nxt.b])
            else:
                assert nk == 128
                if c0 > 0:
                    cp("dve", nxt.t[:, :, 0:c0], zf.t[:, :, 0:c0], [], [nxt.b])
                tt("dve", nxt.t[:, :, c0:N], cur.t[:, :, c0:N], sp.t[:, :, c0:N], ALU.add, [cur.b, sp.b], [nxt.b])

    def cmm(i):
        kbi, nk, c0, diag = blocks[i]
        (Ct, Cb) = cb(i); sp = A["spg"][i % G]; cur = LsF[i % R]
        for h2 in range(2):
            mm(Ct[0:nk, h2, c0:N], triP.t[0:nk, 0:nk], sp.t[0:nk, h2, c0:N], True, False, [sp.b], Cb)
            if i > 0:
                mm(Ct[:, h2, c0:N], ones.t[:, :], cur.t[:, h2, c0:N], False, False, [cur.b], Cb)
        for h2 in range(2):
            mm(Ct[0:nk, h2, c0:N], kt(h2, kbi, nk), nq.t[64 * h2:64 * h2 + 64, c0:N], False, True, [bkt, nq.b], Cb)

    def ex(i):
        kbi, nk, c0, diag = blocks[i]
        (Ct, Cb) = cb(i); a = A["a"][(abase + i) % NA]
        act(a.t[0:nk, :, c0:N], Ct[0:nk, :, c0:N], AF.Exp, Cb, [a.b], scale=-1.0)
        if diag:
            mw = min(nk, N - c0)
            for h2 in range(2):
                tt("pool", a.t[0:nk, h2, c0:c0 + mw], a.t[0:nk, h2, c0:c0 + mw], maskT.t[0:nk, 0:mw], ALU.mult, [a.b], [a.b])

    def pv(i):
        kbi, nk, c0, diag = blocks[i]
        a = A["a"][(abase + i) % NA]
        for h2 in range(2):
            mm(X.t[64 * h2:64 * h2 + 64, 2, c0:N], vv(h2, kbi, nk), a.t[0:nk, h2, c0:N], i == 0, i == nb - 1,
               [a.b, bv], [bX[2]], skip=True)

    groups = [(g0, min(nb, g0 + G)) for g0 in range(0, nb, G)]
    for gi, (g0, g1) in enumerate(groups):
        pg_ = groups[gi - 1] if gi > 0 else None
        if pg_:
            pend = [(lambda i=i: pv(i)) for i in range(pg_[0], pg_[1])]
        else:
            pend = list(prev) if prev else []
        zq = [(i, h2) for i in range(g0, g1) for h2 in range(2)]
        for k in range(min(NZ, len(zq))):
            qk(*zq[k])
        zn = NZ
        for i in range(g0, g1):
            spl(i)
            for _ in range(2):
                if zn < len(zq):
                    qk(*zq[zn]); zn += 1
            if pend:
                pend.pop(0)()
        while pend:
            pend.pop(0)()
        ahead = min(3, g1 - g0)
        for i in range(g0, g0 + ahead):
            cmm(i)
        for i in range(g0, g1):
            ex(i)
            if i + ahead < g1:
                cmm(i + ahead)
    tail = [(lambda i=i: pv(i)) for i in range(groups[-1][0], groups[-1][1])]

    def evac():
        cp("dve", obs.t[:, 0:N], X.t[:, 2, 0:N], [bX[2]], [obs.b])
        if store is not None:
            store()
    tail.append(evac)
    return tail


KB.alloc_norm = _alloc_norm
KB.norm = _norm
KB.pe_hT = _pe_hT
KB.rope = _rope
KB.p1_evac = _p1_evac
KB.alloc_attn = _alloc_attn
KB.a_tile = _a_tile
KB.b_tile = _b_tile


def _sample_attn(self, es, T, PT, cak, cav, cbk, cbv, qkT_sd, v_sd, oa_sd, obT_sd,
                 gsub, nlam, zer, triP, ones, maskT, bconst):
    mm, tr, act, tt, cp, mset, dma = self.mm, self.tr, self.act, self.tt, self.cp, self.mset, self.dma
    A = self.alloc_attn(es, T, PT, 0, G=4)
    NK = PAST + NS
    KTa = T(es, "KTa", [128, 4, NK], BF16); KTb = T(es, "KTb", [128, 4, NK], BF16)
    VAs = T(es, "VAs", [128, 17, 4, 130], BF16); VBs = T(es, "VBs", [128, 17, 512], BF16)
    ct = [T(es, "ct%d" % i, [128, 512], F32) for i in range(4)]
    qsa = T(es, "qsa", [128, 4, NS], BF16); qsb = T(es, "qsb", [128, 4, NS], BF16)
    Yv = A["X"].t[:, 2, :].rearrange("p (h k) -> p h k", k=128)
    bY = A["bX"][2]
    mset("pool", VAs.t[:, :, :, 128:129], 1.0, [VAs.b])
    qv = qkT_sd.rearrange("u p t -> p u t")
    ci = 0
    for s in range(2):
        t0, t1 = s * NS, (s + 1) * NS
        for i in range(16):
            r0 = i * 128
            for which, src in enumerate((cak, cbk, cav, cbv)):
                c_ = ct[ci % 4]; ci += 1
                dma(c_.t[:, :], src[s, r0:r0 + 128, :], [], [c_.b])
                if which < 2:
                    for h in range(4):
                        self.tr32(Yv[:, h, :], c_.t[:, h * 128:(h + 1) * 128], 128, [c_.b], [bY])
                    dstT = KTa if which == 0 else KTb
                    cp("act", dstT.t[:, :, r0:r0 + 128], Yv, [bY], [dstT.b])
                elif which == 2:
                    cp("dve", VAs.t[:, i, :, 0:128], c_.t[:, :].rearrange("p (h e) -> p h e", e=128), [c_.b], [VAs.b])
                else:
                    cp("dve", VBs.t[:, i, :], c_.t[:, :], [c_.b], [VBs.b])
        dma(KTa.t[:, :, PAST:NK], qv[:, 4:8, t0:t1], [], [KTa.b])
        dma(KTb.t[:, :, PAST:NK], qv[:, 12:16, t0:t1], [], [KTb.b])
        dma(VAs.t[0:NS, 16, :, 0:128], v_sd[t0:t1, 0:512].rearrange("t (h e) -> t h e", e=128), [], [VAs.b])
        dma(VBs.t[0:NS, 16, :], v_sd[t0:t1, 512:1024], [], [VBs.b])
        dma(qsa.t[:, :, :], qv[:, 0:4, t0:t1], [], [qsa.b])
        dma(qsb.t[:, :, :], qv[:, 8:12, t0:t1], [], [qsb.b])
        blocks = [(kb, 128, 0, False) for kb in range(16)] + [(16, NS, 0, False)]
        for h in range(4):
            ost = A["ost"][A["oi"] % 2]; A["oi"] += 1
            fin = self.a_tile(A, lambda c, kb, nk, h=h: KTa.t[64 * c:64 * c + 64, h, kb * 128:kb * 128 + nk], KTa.b,
                              lambda kb, nk, h=h: VAs.t[0:nk, kb, h, 0:129], VAs.b,
                              blocks, NS, Tv(qsa.t[:, h, :], qsa.b), [(0, 0, NS, 16)], ost, gsub, nlam, zer, bconst)
            fin()
            dma(oa_sd[t0:t1, h * 128:(h + 1) * 128], ost.t[0:NS, 0, :], [ost.b], [], eng="pool")
        blocks_b = [(16, NS, 0, True)] + [(kb, 128, 0, False) for kb in range(15, -1, -1)]
        for p in range(4):
            obs = A["obs"][A["oi"] % 2]; A["oi"] += 1
            tl_ = self.b_tile(A, lambda h2, kb, nk, p=p: KTb.t[64 * h2:64 * h2 + 64, p, kb * 128:kb * 128 + nk], KTb.b,
                              lambda h2, kb, nk, p=p: VBs.t[0:nk, kb, p * 128 + 64 * h2:p * 128 + 64 * h2 + 64], VBs.b,
                              blocks_b, NS, Tv(qsb.t[:, p, :], qsb.b), obs, triP, ones, maskT, bconst)
            for f in tl_:
                f()
            dma(obT_sd[p][:, t0:t1], obs.t[:, 0:NS], [obs.b], [], eng="pool")


def _phase3(self, es, T, PT, w_in, wba_d, wbb_d, wo_d, xp, xs, oa_d, obT_d, oa_sd, obT_sd, y_p, y_s,
            Abc, shbc, gtbc, fgbc, bconst):
    mm, tr, act, tt, stt, cp, dma = self.mm, self.tr, self.act, self.tt, self.stt, self.cp, self.dma
    NT = self.NT
    L = self.alloc_norm(es, T, PT)
    wg = T(es, "wg", [128, 8, 3072], BF16)
    wba = T(es, "wba", [128, 4, D], BF16); wbb = T(es, "wbb", [128, 4, D], BF16)
    wo = T(es, "wo", [128, 8, D], BF16)
    wst = [T(es, "wst3%d" % i, [128, 4, 512], F32) for i in range(2)]
    oat = [T(es, "oat%d" % i, [128, 512], F32) for i in range(2)]
    obt = [T(es, "obt%d" % i, [128, 4, 128], F32) for i in range(2)]
    sza = T(es, "sza", [128, 512], F32); u1 = T(es, "u1", [128, 512], F32); ua = T(es, "ua", [128, 512], BF16)
    szb = T(es, "szb", [128, 4, 128], F32); u2 = T(es, "u2", [128, 4, 128], F32); ubT = T(es, "ubT", [128, 4, 128], BF16)
    sga = T(es, "sga", [128, D], F32); sgb = T(es, "sgb", [128, D], F32)
    uaT = T(es, "uaT", [128, 4, 128], BF16)
    m1 = T(es, "m1", [128, D], F32); m2 = T(es, "m2", [128, D], F32); mb = T(es, "mb", [128, D], BF16)
    mT = T(es, "mT", [128, 8, 128], BF16)
    ys = [T(es, "ys%d" % i, [128, D], F32) for i in range(2)]
    sm3 = [T(es, "sm3%d" % i, [128, 8], F32) for i in range(4)]
    ZA = PT(es, "ZA", [128, 512], F32); ZB = PT(es, "ZB", [128, 4, 128], F32)
    WA = PT(es, "WA", [128, 2, 512], F32); WB = PT(es, "WB", [128, 2, 512], F32)
    TP = L["TP"]
    flat = "p a b -> p (a b)"

    wi_v = w_in.rearrange("(j p) n -> p j n", p=128)
    k = 0
    srcs = [1536, 3584, 4096, 4608, 5120, 5632]
    for g in range(6):
        for half in range(2):
            w_ = wst[k % 2]; k += 1
            dma(w_.t[:], wi_v[:, 4 * half:4 * half + 4, srcs[g]:srcs[g] + 512], [], [w_.b])
            cp("dve" if k % 2 else "pool", wg.t[:, 4 * half:4 * half + 4, g * 512:(g + 1) * 512], w_.t[:], [w_.b], [wg.b])
    for wd, wt in ((wba_d, wba), (wbb_d, wbb)):
        wv = wd.rearrange("(c p) n -> p c n", p=128)
        for nh in range(2):
            w_ = wst[k % 2]; k += 1
            dma(w_.t[:], wv[:, :, nh * 512:(nh + 1) * 512], [], [w_.b])
            cp("dve" if k % 2 else "pool", wt.t[:, :, nh * 512:(nh + 1) * 512], w_.t[:], [w_.b], [wt.b])
    wv = wo_d.rearrange("(j p) n -> p j n", p=128)
    for half in range(2):
        for nh in range(2):
            w_ = wst[k % 2]; k += 1
            dma(w_.t[:], wv[:, 4 * half:4 * half + 4, nh * 512:(nh + 1) * 512], [], [w_.b])
            cp("dve" if k % 2 else "pool", wo.t[:, 4 * half:4 * half + 4, nh * 512:(nh + 1) * 512], w_.t[:], [w_.b], [wo.b])

    tiles = []
    obT_v = obT_d.rearrange("u p t -> p u t")
    for t in range(NT):
        r0 = t * 128
        tiles.append(dict(P=128, x=xp[r0:r0 + 128, :], mod=0, oa=oa_d[r0:r0 + 128, :], ob=obT_v[:, :, r0:r0 + 128],
                          y=y_p[r0:r0 + 128, :]))
    if self.with_sample:
        tiles.append(dict(P=32, x=xs, mod=1, oa=oa_sd, ob=obT_sd.rearrange("u p t -> p u t"), y=y_s))
    n = len(tiles)

    def load(i):
        tl = tiles[i]; P = tl["P"]; s = i % 2
        dma(L["xt"][s].t[0:P, :], tl["x"], [], [L["xt"][s].b])
        dma(oat[s].t[0:P, :], tl["oa"], [], [oat[s].b])
        dma(obt[s].t[:, :, 0:P], tl["ob"], [], [obt[s].b])

    def head(i):
        tl = tiles[i]; P = tl["P"]; s = i % 2
        hT = L["hT"][s]
        self.pe_hT(L, s, P)
        for j in range(8):
            mm(ZA.t[0:P, :], hT.t[:, j, 0:P], wg.t[:, j, 0:512], j == 0, j == 7, [hT.b, wg.b], [ZA.b])
        for fc in range(4):
            for j in range(8):
                mm(ZB.t[:, fc, 0:P], wg.t[:, j, 512 + fc * 128:512 + (fc + 1) * 128], hT.t[:, j, 0:P], j == 0, j == 7,
                   [hT.b, wg.b], [ZB.b])
        for nh in range(2):
            for j in range(8):
                mm(WB.t[0:P, nh, :], hT.t[:, j, 0:P], wg.t[:, j, 2048 + nh * 512:2048 + (nh + 1) * 512], j == 0, j == 7,
                   [hT.b, wg.b], [WB.b])

    def head_b(i):
        tl = tiles[i]; P = tl["P"]; s = i % 2
        hT = L["hT"][s]
        for nh in range(2):
            for j in range(8):
                mm(WA.t[0:P, nh, :], hT.t[:, j, 0:P], wg.t[:, j, 1024 + nh * 512:1024 + (nh + 1) * 512], j == 0, j == 7,
                   [hT.b, wg.b], [WA.b])

    def mid(i):
        tl = tiles[i]; P = tl["P"]; s = i % 2
        act(sza.t[0:P, :], ZA.t[0:P, :], AF.Sigmoid, [ZA.b], [sza.b])
        tt("dve", u1.t[0:P, :], ZA.t[0:P, :], sza.t[0:P, :], ALU.mult, [ZA.b, sza.b], [u1.b])
        tt("dve", ua.t[0:P, :], u1.t[0:P, :], oat[s].t[0:P, :], ALU.mult, [u1.b, oat[s].b], [ua.b])
        act(szb.t[:, :, 0:P], ZB.t[:, :, 0:P], AF.Sigmoid, [ZB.b], [szb.b])
        tt("dve", u2.t[:, :, 0:P], ZB.t[:, :, 0:P], szb.t[:, :, 0:P], ALU.mult, [ZB.b, szb.b], [u2.b])
        tt("dve", ubT.t[:, :, 0:P], u2.t[:, :, 0:P], obt[s].t[:, :, 0:P], ALU.mult, [u2.b, obt[s].b], [ubT.b])
        act(sgb.t[0:P, :], WB.t[0:P, :, :].rearrange(flat), AF.Sigmoid, [WB.b], [sgb.b])
        act(sga.t[0:P, :], WA.t[0:P, :, :].rearrange(flat), AF.Sigmoid, [WA.b], [sga.b])
        for c in range(4):
            tr(TP.t[:, c, 0:P], ua.t[0:P, c * 128:(c + 1) * 128], P, [ua.b], [TP.b])
        cp("act", uaT.t[:, :, 0:P], TP.t[:, 0:4, 0:P], [TP.b], [uaT.b])
        for nh in range(2):
            for c in range(4):
                mm(WB.t[0:P, nh, :], ubT.t[:, c, 0:P], wbb.t[:, c, nh * 512:(nh + 1) * 512], c == 0, c == 3, [ubT.b, wbb.b], [WB.b])
        for nh in range(2):
            for c in range(4):
                mm(WA.t[0:P, nh, :], uaT.t[:, c, 0:P], wba.t[:, c, nh * 512:(nh + 1) * 512], c == 0, c == 3, [uaT.b, wba.b], [WA.b])
        tt("dve", m2.t[0:P, :], WB.t[0:P, :, :].rearrange(flat), sgb.t[0:P, :], ALU.mult, [WB.b, sgb.b], [m2.b])
        tt("dve", m1.t[0:P, :], WA.t[0:P, :, :].rearrange(flat), sga.t[0:P, :], ALU.mult, [WA.b, sga.b], [m1.b])
        tt("dve", mb.t[0:P, :], m1.t[0:P, :], m2.t[0:P, :], ALU.add, [m1.b, m2.b], [mb.b])
        for j in range(8):
            tr(TP.t[:, j, 0:P], mb.t[0:P, j * 128:(j + 1) * 128], P, [mb.b], [TP.b])
        cp("act", mT.t[:, :, 0:P], TP.t[:, :, 0:P], [TP.b], [mT.b])
        for nh in range(2):
            for j in range(8):
                mm(WA.t[0:P, nh, :], mT.t[:, j, 0:P], wo.t[:, j, nh * 512:(nh + 1) * 512], j == 0, j == 7, [mT.b, wo.b], [WA.b])

    def tail_a(i):
        tl = tiles[i]; P = tl["P"]; md = tl["mod"]
        tt("dve", m1.t[0:P, :], WA.t[0:P, :, :].rearrange(flat), gtbc[md].t[0:P, :], ALU.mult, [WA.b, gtbc[md].b], [m1.b])

    def tail(i):
        tl = tiles[i]; P = tl["P"]; s = i % 2; md = tl["mod"]
        xt = L["xt"][s]
        tt("dve", m2.t[0:P, :], m1.t[0:P, :], xt.t[0:P, :], ALU.add, [m1.b, xt.b], [m2.b])
        sm = sm3[i % 4]; sqj = L["sqj"]
        act(sqj.t[0:P, :], m2.t[0:P, :], AF.Square, [m2.b], [sqj.b, sm.b], accum=sm.t[0:P, 0:1])
        self.rstd(sm.t[0:P, 0:1], sm.t[0:P, 1:2], sm.t[0:P, 2:3], sm.t[0:P, 3:4], 1.0 / D, sm.b, sm.b, sm.b)
        stt("dve", ys[s].t[0:P, :], m2.t[0:P, :], sm.t[0:P, 3:4], fgbc.t[0:P, :], ALU.mult, ALU.mult, [m2.b, sm.b, fgbc.b], [ys[s].b])
        dma(tl["y"], ys[s].t[0:P, :], [ys[s].b], [], eng="pool")

    load(0)
    if n > 1:
        load(1)
    self.norm(L, 0, tiles[0]["P"], Abc[tiles[0]["mod"]], shbc[tiles[0]["mod"]])
    head(0); head_b(0)
    for i in range(n):
        if i + 1 < n:
            self.norm(L, (i + 1) % 2, tiles[i + 1]["P"], Abc[tiles[i + 1]["mod"]], shbc[tiles[i + 1]["mod"]])
        mid(i)
        if i + 1 < n:
            head(i + 1)
        tail_a(i)
        if i + 1 < n:
            head_b(i + 1)
        tail(i)
        if i + 2 < n:
            load(i + 2)


KB.sample_attn = _sample_attn
KB.phase3 = _phase3


def _rope_tables(pos):
    half = 32
    inv = (np.float32(10000.0) ** (-np.arange(half, dtype=np.float32) / np.float32(half))).astype(np.float32)
    ang = pos.astype(np.float32)[:, None] * inv[None, :]
    cos = np.cos(ang).astype(np.float32); sin = np.sin(ang).astype(np.float32)
    return np.tile(cos, (1, 8)), np.tile(sin, (1, 8))


_CACHE = {}


def run(inputs, NT, n_cores, with_sample=True, trace=False):
    key = (NT, with_sample)
    if key not in _CACHE:
        kb = KB(NT, with_sample)
        kb.build()
        _CACHE[key] = kb
    kb = _CACHE[key]
    S = NT * 128
    bf = ml_dtypes.bfloat16
    f32 = np.float32
    g = lambda k: np.ascontiguousarray(np.asarray(inputs[k], dtype=f32))
    xP, xS, cP, cS = g("x_prompt"), g("x_sample"), g("c_prompt"), g("c_sample")
    cos_p, sin_p = _rope_tables(np.arange(S))
    pos_s = PAST + np.tile(np.arange(NS), 2)
    cos_s, sin_s = _rope_tables(pos_s)
    idx = np.arange(128)
    consts = dict(
        ident=np.eye(128, dtype=f32).astype(bf),
        triP=(idx[:, None] >= idx[None, :]).astype(f32).astype(bf),
        ident32=np.eye(128, dtype=f32),
        mask_lt=(idx[:, None] < idx[None, :]).astype(f32),
        cos_p=cos_p, sin_p=sin_p, cos_s=cos_s, sin_s=sin_s,
        w_ada=g("w_ada")[0], b_ada=g("b_ada")[0], norm_g=g("norm_g")[0], w_in=g("w_in")[0],
        lams=np.concatenate([g("lambda_q1")[0], g("lambda_k1")[0], g("lambda_q2")[0], g("lambda_k2")[0]]),
        subln_g=g("subln_g")[0], wba=g("w_branch_a")[0], wbb=g("w_branch_b")[0], w_out=g("w_out")[0],
        final_g=g("final_g"),
    )
    cak, cav, cbk, cbv = (g(k)[0].reshape(-1, PAST, 512) for k in ("cache_a_k", "cache_a_v", "cache_b_k", "cache_b_v"))
    in_maps = []
    for i in range(n_cores):
        m = dict(consts)
        m["xp"] = xP[i]
        m["xs"] = np.ascontiguousarray(xS[2 * i:2 * i + 2].reshape(32, D))
        cm = cP[i].reshape(8, 128).T
        m["cmat_p"] = np.ascontiguousarray(np.broadcast_to(cm[:, :, None], (128, 8, 128)))
        cs = cS[2 * i:2 * i + 2].reshape(2, 8, 128).transpose(2, 1, 0)
        m["cmat_s"] = np.ascontiguousarray(np.repeat(cs, NS, axis=2))
        for nm_, arr in (("cak", cak), ("cav", cav), ("cbk", cbk), ("cbv", cbv)):
            m[nm_] = np.ascontiguousarray(arr[2 * i:2 * i + 2])
        in_maps.append(m)
    res = run_bass_kernel_spmd(kb.nc, in_maps, core_ids=list(range(n_cores)), trace=trace)
    return res


def kernel(**inputs):
    NT = 64
    res = run(inputs, NT, 8)
    r = res.results
    S = NT * 128
    cat = lambda k: np.stack([r[i][k] for i in range(8)], 0)
    cats = lambda k: np.concatenate([r[i][k].reshape(2, NS, -1) for i in range(8)], 0)
    y_prompt = cat("y_p")
    y_sample = cats("y_s")
    pak = cat("pak").reshape(1, 8, S, 4, 2, 64)
    pav = cat("pav").reshape(1, 8, S, 4, 128)
    pbk = cat("pbk").reshape(1, 8, S, 8, 64)
    pbv = cat("pbv").reshape(1, 8, S, 8, 64)
    sak = cats("sak").reshape(1, 16, NS, 4, 2, 64)
    sav = cats("sav").reshape(1, 16, NS, 4, 128)
    sbk = cats("sbk").reshape(1, 16, NS, 8, 64)
    sbv = cats("sbv").reshape(1, 16, NS, 8, 64)
    return (y_prompt, y_sample, pak, pav, pbk, pbv, sak, sav, sbk, sbv)
```
